# Optimizing a Trainium2 kernel written in Bass

```python
import jax, jax.numpy as jnp
from jax import lax
import numpy as np

D_MODEL = 2048
BATCH = 16
SEQ = 2048
DEPTH = 1
DEC_BATCH = 32
DEC_SEQ = 4
PAST_LEN = 16384
PAGE_SIZE = 128

MIX_WIDTH = D_MODEL
HEAD_DIM = 128
A_WIDTH = MIX_WIDTH // 2
A_HEADS = A_WIDTH // HEAD_DIM
DILATED_CONFIGS = ((128, 1), (512, 4), (2048, 16))
A_BUF = max(w for w, _ in DILATED_CONFIGS)
B_WIDTH = MIX_WIDTH - A_WIDTH
B_HEADS = 4
B_DV = B_WIDTH // B_HEADS
B_DK = B_DV // 2
GATE_RANK = 16
GATE_TAU = 16.0
GLA_CHUNK = 64
D_FF = 4 * D_MODEL
RMS_EPS = 1e-6
N_IN = 3 * A_WIDTH + 2 * B_HEADS * B_DK + 2 * B_WIDTH + GATE_RANK

kernel_name = "hymba_dilated_gla_decoder_step"


def rms_f32(x, g):
    xf = x.astype(jnp.float32)
    return xf * lax.rsqrt(jnp.mean(xf * xf, axis=-1, keepdims=True) + RMS_EPS) * g.astype(jnp.float32)


def rmsnorm(x, g):
    return rms_f32(x, g).astype(x.dtype)


def mixer_inputs(h, w_in, q_g, k_g, w_gate2, b_gate):
    bn, t, _ = h.shape
    proj = h @ w_in
    splits = [int(s) for s in np.cumsum([A_WIDTH, A_WIDTH, A_WIDTH, B_HEADS * B_DK,
                                         B_HEADS * B_DK, B_WIDTH, B_WIDTH])]
    qa, ka, va, qb, kb, vb, rb, ga = jnp.split(proj, splits, axis=-1)
    qa = rms_f32(qa.reshape(bn, t, A_HEADS, HEAD_DIM), q_g)
    ka = rms_f32(ka.reshape(bn, t, A_HEADS, HEAD_DIM), k_g)
    va = va.reshape(bn, t, A_HEADS, HEAD_DIM).astype(jnp.float32)
    qb = qb.reshape(bn, t, B_HEADS, B_DK).astype(jnp.float32) * (B_DK ** -0.5)
    kb = kb.reshape(bn, t, B_HEADS, B_DK).astype(jnp.float32)
    vb = vb.reshape(bn, t, B_HEADS, B_DV).astype(jnp.float32)
    z = (ga @ w_gate2 + b_gate).astype(jnp.float32)
    log_a = (jax.nn.log_sigmoid(z) / GATE_TAU).reshape(bn, t, B_HEADS, B_DK)
    return qa, ka, va, qb, kb, vb, rb, log_a


def dilated_window_full(q, k, v, window, dilation):
    bn, s_len, h, dh = q.shape
    blk = window // dilation
    span = blk * dilation
    sp = -(-s_len // span) * span
    pad = sp - s_len
    nb = sp // span

    def to_blocks(t):
        t = jnp.pad(t, ((0, 0), (0, pad), (0, 0), (0, 0)))
        return t.reshape(bn, nb, blk, dilation, h, dh).transpose(0, 3, 1, 2, 4, 5)

    qb, kb, vb = to_blocks(q), to_blocks(k), to_blocks(v)

    def with_prev(t):
        prev = jnp.pad(t, ((0, 0), (0, 0), (1, 0), (0, 0), (0, 0), (0, 0)))[:, :, :-1]
        return jnp.concatenate([prev, t], axis=3)

    kk, vv = with_prev(kb), with_prev(vb)
    s = jnp.einsum('brnqhd,brnkhd->brnhqk', qb, kk) * (HEAD_DIM ** -0.5)
    qi = jnp.arange(blk)[:, None]
    kj = jnp.arange(2 * blk)[None, :]
    dist = qi + blk - kj
    band = (dist >= 0) & (dist <= blk)
    valid = band[None] & ((jnp.arange(nb)[:, None, None] > 0) | (kj >= blk)[None])
    s = jnp.where(valid[None, None, :, None], s, -jnp.inf)
    lse = jax.nn.logsumexp(s, axis=-1)
    p = jnp.exp(s - lse[..., None])
    o = jnp.einsum('brnhqk,brnkhd->brnqhd', p, vv)
    o = o.transpose(0, 2, 3, 1, 4, 5).reshape(bn, sp, h, dh)[:, :s_len]
    lse = lse.transpose(0, 2, 4, 1, 3).reshape(bn, sp, h)[:, :s_len]
    return o, lse


def dilated_window_step(q, k_ext, v_ext, window, dilation):
    t = q.shape[1]
    l = k_ext.shape[1] - t
    n_keys = window // dilation + 1
    idx = l + jnp.arange(t)[:, None] - dilation * jnp.arange(n_keys)[None, :]
    valid = idx >= 0
    idxc = jnp.maximum(idx, 0)
    kg = k_ext[:, idxc]
    vg = v_ext[:, idxc]
    s = jnp.einsum('bthd,btjhd->bthj', q, kg) * (HEAD_DIM ** -0.5)
    s = jnp.where(valid[None, :, None, :], s, -jnp.inf)
    lse = jax.nn.logsumexp(s, axis=-1)
    p = jnp.exp(s - lse[..., None])
    o = jnp.einsum('bthj,btjhd->bthd', p, vg)
    return o, lse


def combine_by_denominator(outs, lses):
    o = jnp.stack(outs)
    w = jax.nn.softmax(jnp.stack(lses), axis=0)
    return jnp.sum(o * w[..., None], axis=0)


def gla_chunked(q, k, v, log_a, s0):
    bn, t, h, _ = q.shape
    c = GLA_CHUNK
    tp = -(-t // c) * c
    pad = tp - t
    n = tp // c

    def chunks(x):
        x = jnp.pad(x, ((0, 0), (0, pad), (0, 0), (0, 0)))
        return x.reshape(bn, n, c, h, x.shape[-1])

    q, k, v, log_a = chunks(q), chunks(k), chunks(v), chunks(log_a)
    b = jnp.cumsum(log_a, axis=2)
    b_last = b[:, :, -1]
    q_dec = q * jnp.exp(b)
    k_inv = k * jnp.exp(-b)
    causal = jnp.tril(jnp.ones((c, c), dtype=bool))
    att = jnp.einsum('bnihk,bnjhk->bnhij', q_dec, k_inv)
    att = jnp.where(causal, att, 0.0)
    o = jnp.einsum('bnhij,bnjhv->bnihv', att, v)
    k_end = k * jnp.exp(b_last[:, :, None] - b)
    delta = jnp.einsum('bnjhk,bnjhv->bnhkv', k_end, v)
    decay = jnp.exp(b_last)

    def step(s, xs):
        dec, dlt = xs
        return dec[..., None] * s + dlt, s

    s_fin, s_prev = lax.scan(step, s0, (jnp.moveaxis(decay, 1, 0), jnp.moveaxis(delta, 1, 0)))
    o = o + jnp.einsum('bnihk,nbhkv->bnihv', q_dec, s_prev)
    return o.reshape(bn, tp, h, v.shape[-1])[:, :t], s_fin


def layer_output(x, oa, ob, rb, gla_norm_g, w_out, mlp_norm_g, w_up, w_down):
    bn, t, _ = x.shape
    ob = rms_f32(ob, gla_norm_g).reshape(bn, t, B_WIDTH) * jax.nn.silu(rb.astype(jnp.float32))
    mix = jnp.concatenate([oa.reshape(bn, t, A_WIDTH), ob], axis=-1).astype(x.dtype)
    x = x + mix @ w_out
    hm = rmsnorm(x, mlp_norm_g)
    u = jnp.square(jax.nn.relu(hm @ w_up))
    return x + u @ w_down


def setup_inputs(seed: int = 0) -> dict:
    key = jax.random.key(seed)
    ks = jax.random.split(key, 16)
    lb = min(A_BUF, PAST_LEN)
    nrm = jax.random.normal
    f32 = jnp.float32
    return {
        "x_prompt": nrm(ks[0], (BATCH, SEQ, D_MODEL), f32),
        "x_sample": nrm(ks[1], (DEC_BATCH, DEC_SEQ, D_MODEL), f32),
        "cache_win_k": nrm(ks[2], (DEPTH, DEC_BATCH, lb, A_HEADS, HEAD_DIM), f32),
        "cache_win_v": nrm(ks[3], (DEPTH, DEC_BATCH, lb, A_HEADS, HEAD_DIM), f32),
        "state_gla": nrm(ks[4], (DEPTH, DEC_BATCH, B_HEADS, B_DK, B_DV), f32),
        "attn_norm_g": 1.0 + 0.02 * nrm(ks[5], (DEPTH, D_MODEL), f32),
        "w_in": nrm(ks[6], (DEPTH, D_MODEL, N_IN), f32) * D_MODEL ** -0.5,
        "q_norm_g": 1.0 + 0.02 * nrm(ks[7], (DEPTH, HEAD_DIM), f32),
        "k_norm_g": 1.0 + 0.02 * nrm(ks[8], (DEPTH, HEAD_DIM), f32),
        "w_gate2": nrm(ks[9], (DEPTH, GATE_RANK, B_HEADS * B_DK), f32) * GATE_RANK ** -0.5,
        "b_gate": 0.1 * nrm(ks[10], (DEPTH, B_HEADS * B_DK), f32),
        "gla_norm_g": 1.0 + 0.02 * nrm(ks[11], (DEPTH, B_HEADS, B_DV), f32),
        "w_out": nrm(ks[12], (DEPTH, MIX_WIDTH, D_MODEL), f32) * MIX_WIDTH ** -0.5,
        "mlp_norm_g": 1.0 + 0.02 * nrm(ks[13], (DEPTH, D_MODEL), f32),
        "w_up": nrm(ks[14], (DEPTH, D_MODEL, D_FF), f32) * D_MODEL ** -0.5,
        "w_down": nrm(ks[15], (DEPTH, D_FF, D_MODEL), f32) * D_FF ** -0.5,
    }


def reference(x_prompt, x_sample, cache_win_k, cache_win_v, state_gla, attn_norm_g, w_in,
              q_norm_g, k_norm_g, w_gate2, b_gate, gla_norm_g, w_out, mlp_norm_g, w_up, w_down):
    xp, xs = x_prompt, x_sample
    pk, pv, ps, sk, sv, ss = [], [], [], [], [], []
    for l in range(DEPTH):
        h = rmsnorm(xp, attn_norm_g[l])
        qa, ka, va, qb, kb, vb, rb, log_a = mixer_inputs(h, w_in[l], q_norm_g[l], k_norm_g[l],
                                                          w_gate2[l], b_gate[l])
        outs, lses = [], []
        for window, dilation in DILATED_CONFIGS:
            o_i, l_i = dilated_window_full(qa, ka, va, window, dilation)
            outs.append(o_i)
            lses.append(l_i)
        oa = combine_by_denominator(outs, lses)
        s0 = jnp.zeros((xp.shape[0], B_HEADS, B_DK, B_DV), jnp.float32)
        ob, s_fin = gla_chunked(qb, kb, vb, log_a, s0)
        t = xp.shape[1]
        lp = min(A_BUF, t)
        pk.append(ka[:, t - lp:].astype(cache_win_k.dtype))
        pv.append(va[:, t - lp:].astype(cache_win_v.dtype))
        ps.append(s_fin.astype(state_gla.dtype))
        xp = layer_output(xp, oa, ob, rb, gla_norm_g[l], w_out[l], mlp_norm_g[l], w_up[l], w_down[l])

        h = rmsnorm(xs, attn_norm_g[l])
        qa, ka, va, qb, kb, vb, rb, log_a = mixer_inputs(h, w_in[l], q_norm_g[l], k_norm_g[l],
                                                          w_gate2[l], b_gate[l])
        k_ext = jnp.concatenate([cache_win_k[l].astype(jnp.float32), ka], axis=1)
        v_ext = jnp.concatenate([cache_win_v[l].astype(jnp.float32), va], axis=1)
        outs, lses = [], []
        for window, dilation in DILATED_CONFIGS:
            o_i, l_i = dilated_window_step(qa, k_ext, v_ext, window, dilation)
            outs.append(o_i)
            lses.append(l_i)
        oa = combine_by_denominator(outs, lses)
        ob, s_new = gla_chunked(qb, kb, vb, log_a, state_gla[l].astype(jnp.float32))
        sk.append(ka.astype(cache_win_k.dtype))
        sv.append(va.astype(cache_win_v.dtype))
        ss.append(s_new.astype(state_gla.dtype))
        xs = layer_output(xs, oa, ob, rb, gla_norm_g[l], w_out[l], mlp_norm_g[l], w_up[l], w_down[l])

    return (xp, xs, jnp.stack(pk), jnp.stack(pv), jnp.stack(ps), jnp.stack(sk), jnp.stack(sv), jnp.stack(ss))
```

```python
import os
import numpy as np
from contextlib import ExitStack
import ml_dtypes
import concourse.bass as bass
import concourse.mybir as mybir
from concourse.bass_utils import run_bass_kernel_spmd

F32 = mybir.dt.float32
BF16 = mybir.dt.bfloat16
AF = mybir.ActivationFunctionType
ALU = mybir.AluOpType
AX = mybir.AxisListType

NCORES = 8
SEQ = 2048
D = 2048
NIN = 6160
DFF = 8192
EPS = 1e-6
NTOK = 2176
STRIP0 = 384
STRIPW = 2432


class Buf:
    __slots__ = ("name", "lw", "rd", "dsem", "alias", "keep")

    def __init__(self, name):
        self.name = name
        self.lw = {}
        self.rd = {}
        self.dsem = None
        self.alias = ()
        self.keep = False


class Prog:
    ENG = ("pe", "act", "dve", "pool", "sp")

    def __init__(self):
        self.q = {e: [] for e in self.ENG}
        self.cnt = {}
        self.known = {e: {} for e in self.ENG}
        self.nsem = 0

    def _issue(self, eng, fn, reads, writes, semkey, inc):
        d = {}
        for b in reads:
            for k, v in b.lw.items():
                if d.get(k, 0) < v:
                    d[k] = v
        ali = False
        for b in writes:
            for k, v in b.lw.items():
                if d.get(k, 0) < v:
                    d[k] = v
            for k, v in b.rd.items():
                if d.get(k, 0) < v:
                    d[k] = v
            if b.alias:
                al = b.alias() if callable(b.alias) else b.alias
                for ab in al:
                    if ab is b:
                        continue
                    ali = True
                    for k, v in ab.lw.items():
                        if d.get(k, 0) < v:
                            d[k] = v
                    for k, v in ab.rd.items():
                        if d.get(k, 0) < v:
                            d[k] = v
                if not b.keep:
                    b.alias = ()
        waits = []
        kn = self.known[eng]
        for k, v in d.items():
            if k == "pe" and eng == "pe" and not ali:
                continue
            if isinstance(k, tuple):
                v = self.cnt[k]
            if kn.get(k, 0) >= v:
                continue
            kn[k] = v
            waits.append((k, v))
        self.cnt[semkey] = self.cnt.get(semkey, 0) + inc
        val = self.cnt[semkey]
        self.q[eng].append((waits, fn, semkey, inc))
        for b in reads:
            if b.rd.get(semkey, 0) < val:
                b.rd[semkey] = val
        for b in writes:
            if b.lw.get(semkey, 0) < val:
                b.lw[semkey] = val

    def op(self, eng, fn, reads=(), writes=()):
        self._issue(eng, fn, reads, writes, eng, 1)

    def dma(self, fn, sbuf, reads=(), writes=()):
        if sbuf.dsem is None:
            sbuf.dsem = ("d", self.nsem)
            self.nsem += 1
        self._issue("sp", fn, reads, writes, sbuf.dsem, 16)

    def barrier(self, force=False):
        if not force:
            return
        for e in self.ENG:
            kn = self.known[e]
            waits = []
            for k, v in self.cnt.items():
                if kn.get(k, 0) < v:
                    kn[k] = v
                    waits.append((k, v))
            if waits:
                self.q[e].append((waits, None, None, 0))

    def emit(self, nc, es):
        sems = {}
        for k in sorted(self.cnt.keys(), key=str):
            nm = k if isinstance(k, str) else "d%d" % k[1]
            sems[k] = es.enter_context(nc.semaphore("s_" + nm))
        block = es.enter_context(nc.Block())
        final = dict(self.cnt)

        def run(ename, e):
            for waits, fn, semkey, inc in self.q[ename]:
                for k, v in waits:
                    e.wait_ge(sems[k], v)
                if fn is not None:
                    fn(e).then_inc(sems[semkey], inc)
            if ename == "sp":
                for k, v in final.items():
                    e.wait_ge(sems[k], v)

        @block.tensor
        def _(e):
            run("pe", e)

        @block.scalar
        def _(e):
            run("act", e)

        @block.vector
        def _(e):
            run("dve", e)

        @block.gpsimd
        def _(e):
            run("pool", e)

        @block.sync
        def _(e):
            run("sp", e)


class Arena:
    def __init__(self, tensor, nwords):
        self.t = tensor
        self.n = nwords
        self.top = 0
        self.hi = nwords
        self.hist = []
        self.track = False

    def reset(self, base):
        self.top = base
        self.hi = self.n

    def _take(self, name, nw, top):
        nw = (nw + 7) // 8 * 8
        if top:
            self.hi -= nw
            off = self.hi
        else:
            off = self.top
            self.top += nw
        assert self.top <= self.hi <= self.n, (name, self.top, self.hi, self.n)
        b = Buf(name)
        if self.track:
            b.alias = [ob for (lo, hi, ob) in self.hist if lo < off + nw and off < hi]
            self.hist.append((off, off + nw, b))
        return off, b

    def f32(self, name, ncols, shape=None, top=False):
        off, b = self._take(name, ncols, top)
        ap = self.t[:, off:off + ncols]
        if shape:
            ap = ap.rearrange(shape[0], **shape[1])
        return ap, b

    def b16(self, name, ncols, shape=None, top=False):
        nw = (ncols + 1) // 2
        off, b = self._take(name, nw, top)
        ap = self.t[:, off:off + nw].bitcast(BF16)[:, 0:ncols]
        if shape:
            ap = ap.rearrange(shape[0], **shape[1])
        return ap, b


def tiles_of(s):
    tl = [(t, t * 128, 128) for t in range(16)]
    if s == 1:
        tl.append((16, 2048, 16))
    return tl


def build(stop=None):
    nc = bass.Bass("TRN2", target_bir_lowering=False)

    def din(name, shape, dt=F32):
        return nc.dram_tensor(name, list(shape), dt, kind="ExternalInput").ap()

    def dout(name, shape, dt=F32):
        return nc.dram_tensor(name, list(shape), dt, kind="ExternalOutput").ap()

    def dscr(name, shape, dt=F32):
        return nc.dram_tensor(name, list(shape), dt, kind="ExternalOutput" if (stop or os.environ.get("KSCR")) else "Internal").ap()

    xp = din("xp", [2, SEQ, D])
    xs = din("xs", [16, D])
    ck = din("ck", [4, SEQ, 1024])
    cv = din("cv", [4, SEQ, 1024])
    sg = din("sg", [4, 4, 128, 256])
    w_in = din("w_in", [D, NIN])
    w_out = din("w_out", [D, D])
    w_up = din("w_up", [D, DFF])
    w_down = din("w_down", [DFF, D])
    ag = din("ag", [1, D])
    mg = din("mg", [1, D])
    qg = din("qg", [1, 128])
    kg = din("kg", [1, 128])
    gg = din("gg", [1, 1024])
    wg2 = din("wg2", [16, 512])
    bg = din("bg", [1, 512])
    constf = din("constf", [128, 392])
    constb = din("constb", [128, 256 + STRIPW + 288], BF16)

    yp = dout("yp", [2, SEQ, D])
    ys = dout("ys", [16, D])
    pk = dout("pk", [2, SEQ, 1024])
    pv = dout("pv", [2, SEQ, 1024])
    pst = dout("pst", [2, 4, 128, 256])
    sk = dout("sk", [16, 1024])
    sv = dout("sv", [16, 1024])
    sst = dout("sst", [4, 4, 128, 256])

    projscr = [dscr("projscr%d" % s, [2064, NIN]) for s in range(2)]
    mixscr = [dscr("mixscr%d" % s, [16, 128, NTOK], BF16) for s in range(2)]
    x1scr = [dscr("x1scr%d" % s, [NTOK, D]) for s in range(2)]
    utscr = [dscr("utscr%d" % s, [17, 128, 64, 128], BF16) for s in range(2)]
    B_proj = [[Buf("proj%d_%d" % (s, t)) for t in range(17)] for s in range(2)]
    B_mix = [Buf("mix%d" % s) for s in range(2)]
    B_x1 = [[Buf("x1_%d_%d" % (s, t)) for t in range(17)] for s in range(2)]
    B_ut = [[Buf("ut_%d_%d" % (s, t)) for t in range(17)] for s in range(2)]
    B_in = Buf("inputs")
    B_out = Buf("outputs")

    P = Prog()
    es = ExitStack()
    with es:
        NW = 45056
        arena_t = es.enter_context(nc.sbuf_tensor("arena", [128, NW], F32))
        AR = Arena(arena_t, NW)
        psA = []
        for i in range(6):
            t_ = es.enter_context(nc.psum_tensor("psA%d" % i, [128, 512], F32))
            psA.append((t_, Buf("psA%d" % i)))
        psT = []
        for i in range(2):
            t_ = es.enter_context(nc.psum_tensor("psT%d" % i, [128, 8, 128], BF16))
            psT.append((t_, Buf("psT%d" % i)))
        rot = {"a": 0, "t": 0}

        def rotA():
            r = psA[rot["a"] % 6]
            rot["a"] += 1
            return r

        def rotT():
            r = psT[rot["t"] % 2]
            rot["t"] += 1
            return r

        cf, B_cf = AR.f32("constf", 392)
        cb, B_cb = AR.b16("constb", 256 + STRIPW + 288)
        identf = cf[:, 0:128]
        Umat = cf[:, 128:256]
        Lmat = cf[:, 256:384]
        onescol = cf[:, 384:385]
        identb = cb[:, 0:128]
        onesb = cb[:, 128:256]
        strip = cb[:, 256:256 + STRIPW]
        smask = cb[:, 256 + STRIPW:256 + STRIPW + 288]
        gq_bc, B_gq = AR.f32("gq", 128)
        gk_bc, B_gk = AR.f32("gk", 128)
        gg_bc, B_gg = AR.f32("gg", 1024)
        wga, B_wga = AR.f32("wga", 512)
        Sst, B_S = AR.f32("S", 1024, ("p (h v) -> p h v", dict(h=4)))
        Sb, B_Sb = AR.b16("Sb", 1024, ("p (h v) -> p h v", dict(h=4)))
        gaT, B_gaT = AR.f32("gaT", 128)
        gbc, B_gbc = AR.f32("gbc", 2048)
        stat, B_stat_unused = AR.f32("stat", 128)
        stat_bufs = [Buf("stat%d" % i) for i in range(8)]
        BIG, B_big = AR.b16("BIG", 16 * NTOK, ("p (k n) -> p k n", dict(k=16)))
        PERSIST_TOP = AR.top
        AR.track = True
        BIGAL = []
        B_big.alias = lambda: BIGAL
        B_big.keep = True

        def bigview(name, extra=()):
            b = Buf(name)
            b.alias = [B_big] + list(extra)
            b.keep = True
            BIGAL.append(b)
            return b

        P.dma(lambda e: e.dma_start(out=cf, in_=constf), B_cf, reads=[B_in], writes=[B_cf])
        P.dma(lambda e: e.dma_start(out=cb, in_=constb), B_cb, reads=[B_in], writes=[B_cb])
        P.dma(lambda e: e.dma_start(out=gq_bc, in_=qg[0, :].partition_broadcast(128)), B_gq, reads=[B_in], writes=[B_gq])
        P.dma(lambda e: e.dma_start(out=gk_bc, in_=kg[0, :].partition_broadcast(128)), B_gk, reads=[B_in], writes=[B_gk])
        P.dma(lambda e: e.dma_start(out=gg_bc, in_=gg[0, :].partition_broadcast(128)), B_gg, reads=[B_in], writes=[B_gg])
        P.dma(lambda e: e.dma_start(out=wga[0:16, :], in_=wg2), B_wga, reads=[B_in], writes=[B_wga])
        P.dma(lambda e: e.dma_start(out=wga[16:17, :], in_=bg), B_wga, reads=[B_in], writes=[B_wga])
        P.op("act", lambda e: e.mul(gq_bc, gq_bc, 128.0 ** -0.5), reads=[B_gq], writes=[B_gq])
        P.op("pool", lambda e: e.memset(gaT[:, :], 1.0), writes=[B_gaT])

        stat_i = [0]

        def newstat(n):
            i = stat_i[0] % 8
            stat_i[0] += 1
            return stat[:, i * 16:i * 16 + n], stat_bufs[i]

        def phase_norm(s, src_fn, src_bufs, gvec):
            AR.reset(PERSIST_TOP)
            xt = [AR.f32("xt%d" % i, 2048) for i in range(2)]
            junk, B_junk = AR.f32("junk", 2048)
            hb = [AR.b16("hb%d" % i, 2048) for i in range(2)]
            P.dma(lambda e: e.dma_start(out=gbc, in_=gvec[0, :].partition_broadcast(128)), B_gbc, reads=[B_in], writes=[B_gbc])
            tl = tiles_of(s)

            def S1(i):
                t, row0, nt = tl[i]
                x_, Bx = xt[i % 2]
                h_, Bh = hb[i % 2]
                P.dma(lambda e: e.dma_start(out=x_[0:nt, :], in_=src_fn(t, nt)), Bx, reads=[src_bufs[t]], writes=[Bx])
                ssq, Bq = newstat(1)
                rs, Br = newstat(1)
                P.op("act", lambda e: e.activation(out=junk[0:nt, :], in_=x_[0:nt, :], func=AF.Square, accum_out=ssq[0:nt, :]),
                     reads=[Bx], writes=[B_junk, Bq])
                P.op("act", lambda e: e.activation(out=rs[0:nt, :], in_=ssq[0:nt, :], func=AF.Sqrt, scale=1.0 / D, bias=EPS),
                     reads=[Bq], writes=[Br])
                P.op("dve", lambda e: e.reciprocal(out=rs[0:nt, :], in_=rs[0:nt, :]), reads=[Br], writes=[Br])
                P.op("dve", lambda e: e.scalar_tensor_tensor(out=h_[0:nt, :], in0=x_[0:nt, :], scalar=rs[0:nt, 0:1], in1=gbc[0:nt, :],
                                                             op0=ALU.mult, op1=ALU.mult), reads=[Bx, Br, B_gbc], writes=[Bh])

            def S2(i):
                t, row0, nt = tl[i]
                h_, Bh = hb[i % 2]
                for half in range(2):
                    pt, Bp = rotT()
                    for k in range(8):
                        kk = half * 8 + k
                        P.op("pe", lambda e, pt=pt, k=k, kk=kk: e.transpose(
                            out=pt[:, k, 0:nt], in_=h_[0:nt, kk * 128:(kk + 1) * 128], identity=identb[0:nt, 0:nt]),
                            reads=[Bh, B_cb], writes=[Bp])
                    if half == 0:
                        P.op("act", lambda e, pt=pt, half=half: e.activation(
                            out=BIG[:, half * 8:half * 8 + 8, row0:row0 + nt], in_=pt[:, :, 0:nt], func=AF.Copy),
                            reads=[Bp], writes=[B_big])
                    else:
                        P.op("dve", lambda e, pt=pt, half=half: e.tensor_copy(
                            out=BIG[:, half * 8:half * 8 + 8, row0:row0 + nt], in_=pt[:, :, 0:nt]),
                            reads=[Bp], writes=[B_big])

            S1(0)
            for i in range(len(tl)):
                if i + 1 < len(tl):
                    S1(i + 1)
                S2(i)
            P.barrier()

        def phase_gemm_tok(s, W, ncols_total, evac):
            AR.reset(PERSIST_TOP)
            wst = [AR.f32("wst%d" % i, 16 * 256, ("p (k n) -> p k n", dict(k=16)), top=True) for i in range(2)]
            wbf = [AR.b16("wbf%d" % i, 16 * 256, ("p (k n) -> p k n", dict(k=16)), top=True) for i in range(2)]
            ctx = {"AR": AR}
            Wpk = W.rearrange("(k p) n -> p k n", p=128)
            groups = []
            c0 = 0
            while c0 < ncols_total:
                nc_ = min(256, ncols_total - c0)
                groups.append((c0, nc_))
                c0 += nc_

            def wload(gi):
                c0, ncol = groups[gi]
                ws, Bws = wst[gi % 2]
                wb, Bwb = wbf[gi % 2]
                P.dma(lambda e: e.dma_start(out=ws[:, :, 0:ncol], in_=Wpk[:, :, c0:c0 + ncol]), Bws, reads=[B_in], writes=[Bws])
                P.op("dve", lambda e: e.tensor_copy(out=wb[:, 0:8, 0:ncol], in_=ws[:, 0:8, 0:ncol]), reads=[Bws], writes=[Bwb])
                P.op("act", lambda e: e.activation(out=wb[:, 8:16, 0:ncol], in_=ws[:, 8:16, 0:ncol], func=AF.Copy), reads=[Bws], writes=[Bwb])

            import os
            if os.environ.get("KGROUPS"):
                a_, b_ = [int(v) for v in os.environ["KGROUPS"].split(",")]
                groups = groups[a_:b_]
            st = evac("init", ctx)
            wload(0)
            steps = [(gi, c0, ncol, t, row0, nt) for gi, (c0, ncol) in enumerate(groups) for (t, row0, nt) in tiles_of(s)]
            PF = 2
            for j in range(min(PF, len(steps))):
                evac("pre", ctx, st, *steps[j][3:6], *steps[j][1:3])
            for j, (gi, c0, ncol, t, row0, nt) in enumerate(steps):
                if t == 0 and gi + 1 < len(groups):
                    wload(gi + 1)
                if j + PF < len(steps):
                    evac("pre", ctx, st, *steps[j + PF][3:6], *steps[j + PF][1:3])
                wb, Bwb = wbf[gi % 2]
                ps, Bps = rotA()
                for k in range(16):
                    P.op("pe", lambda e, ps=ps, k=k, row0=row0, nt=nt, wb=wb, ncol=ncol: e.matmul(
                        ps[0:nt, 0:ncol], lhsT=BIG[:, k, row0:row0 + nt], rhs=wb[:, k, 0:ncol],
                        start=(k == 0), stop=(k == 15)), reads=[B_big, Bwb], writes=[Bps])
                evac("tile", ctx, st, t, row0, nt, c0, ncol, ps, Bps)
            P.barrier()

        def evac_A1(s):
            cnt = [0]

            def f(kind, ctx, st=None, t=0, row0=0, nt=0, c0=0, ncol=0, ps=None, Bps=None):
                if kind == "init":
                    return [ctx["AR"].f32("ost%d" % i, 256) for i in range(4)]
                if kind == "pre":
                    return
                o_, Bo = st[cnt[0] % 4]
                eng = "act" if cnt[0] % 2 == 0 else "dve"
                cnt[0] += 1
                if eng == "act":
                    P.op("act", lambda e: e.activation(out=o_[0:nt, 0:ncol], in_=ps[0:nt, 0:ncol], func=AF.Copy),
                         reads=[Bps], writes=[Bo])
                else:
                    P.op("dve", lambda e: e.tensor_copy(out=o_[0:nt, 0:ncol], in_=ps[0:nt, 0:ncol]), reads=[Bps], writes=[Bo])
                P.dma(lambda e: e.dma_start(out=projscr[s][row0:row0 + nt, c0:c0 + ncol], in_=o_[0:nt, 0:ncol]), Bo,
                      reads=[Bo], writes=[B_proj[s][t]])
            return f

        def xrows(s, t, nt):
            if t < 16:
                return xp[s, t * 128:t * 128 + nt, :]
            return xs[0:nt, :]

        def evac_C(s):
            cnt = [0]
            pcnt = [0]

            def f(kind, ctx, st=None, t=0, row0=0, nt=0, c0=0, ncol=0, ps=None, Bps=None):
                if kind == "init":
                    a = [ctx["AR"].f32("ost%d" % i, 256) for i in range(4)]
                    b = [ctx["AR"].f32("xsl%d" % i, 256) for i in range(4)]
                    return (a, b)
                if kind == "pre":
                    x_, Bx = st[1][pcnt[0] % 4]
                    pcnt[0] += 1
                    P.dma(lambda e: e.dma_start(out=x_[0:nt, 0:ncol], in_=xrows(s, t, nt)[:, c0:c0 + ncol]), Bx,
                          reads=[B_in], writes=[Bx])
                    return
                o_, Bo = st[0][cnt[0] % 4]
                x_, Bx = st[1][cnt[0] % 4]
                cnt[0] += 1
                P.op("dve", lambda e: e.tensor_tensor(out=o_[0:nt, 0:ncol], in0=ps[0:nt, 0:ncol], in1=x_[0:nt, 0:ncol], op=ALU.add),
                     reads=[Bps, Bx], writes=[Bo])
                P.dma(lambda e: e.dma_start(out=x1scr[s][row0:row0 + nt, c0:c0 + ncol], in_=o_[0:nt, 0:ncol]), Bo,
                      reads=[Bo], writes=[B_x1[s][t]])
            return f

        def qk_prep(pj, Bpj, nt, junk, B_junk, qh, Bqh, kf, Bkf, kh, Bkh):
            P.op("act", lambda e: e.activation(out=junk[0:nt, 0:2048], in_=pj[0:nt, 0:2048], func=AF.Square),
                 reads=[Bpj], writes=[B_junk])
            ssq, Bq = newstat(16)
            P.op("dve", lambda e: e.tensor_reduce(out=ssq[0:nt, :], in_=junk[0:nt, 0:2048].rearrange("p (h d) -> p h d", h=16),
                                                  axis=AX.X, op=ALU.add), reads=[B_junk], writes=[Bq])
            P.op("act", lambda e: e.activation(out=ssq[0:nt, :], in_=ssq[0:nt, :], func=AF.Sqrt, scale=1.0 / 128, bias=EPS),
                 reads=[Bq], writes=[Bq])
            P.op("dve", lambda e: e.reciprocal(out=ssq[0:nt, :], in_=ssq[0:nt, :]), reads=[Bq], writes=[Bq])
            for h in range(8):
                P.op("dve", lambda e, h=h: e.scalar_tensor_tensor(
                    out=qh[0:nt, h * 128:(h + 1) * 128], in0=pj[0:nt, h * 128:(h + 1) * 128], scalar=ssq[0:nt, h:h + 1],
                    in1=gq_bc[0:nt, :], op0=ALU.mult, op1=ALU.mult), reads=[Bpj, Bq, B_gq], writes=[Bqh])
            for h in range(8):
                P.op("dve", lambda e, h=h: e.scalar_tensor_tensor(
                    out=kf[0:nt, h * 128:(h + 1) * 128], in0=pj[0:nt, 1024 + h * 128:1024 + (h + 1) * 128],
                    scalar=ssq[0:nt, 8 + h:9 + h], in1=gk_bc[0:nt, :], op0=ALU.mult, op1=ALU.mult),
                    reads=[Bpj, Bq, B_gk], writes=[Bkf])
            P.op("dve", lambda e: e.tensor_copy(out=kh[0:nt, :], in_=kf[0:nt, :]), reads=[Bkf], writes=[Bkh])

        def phase_B1(s):
            AR.reset(PERSIST_TOP)
            KT = BIG[:, 0:8, 0:2048]
            pj = [AR.f32("pjqk%d" % i, 3072) for i in range(2)]
            junk, B_junk = AR.f32("junk", 2048)
            qh, Bqh = AR.b16("qh", 1024)
            kh, Bkh = AR.b16("kh", 1024)
            kf, Bkf = AR.f32("kf", 1024)
            QT, B_QT = AR.b16("QT", 8 * 512, ("p (h n) -> p h n", dict(h=8)))
            Eb = [AR.b16("E%d" % i, 512) for i in range(2)]
            Pb = [AR.b16("P%d" % i, 512) for i in range(2)]
            lnz, B_lnz = AR.f32("lnz", 512)
            mixA = [AR.b16("mixA%d" % i, 512) for i in range(2)]
            B_KT = [bigview("KT%d" % t) for t in range(16)]
            B_V = [bigview("V%d" % t) for t in range(16)]

            def Vtile(t):
                return BIG[:, 8 + t // 2, (t % 2) * 1024:(t % 2) * 1024 + 1024]

            hcount = [0]
            QT2, B_QT2 = AR.b16("QT2", 8 * 512, ("p (h n) -> p h n", dict(h=8)))
            QTs = [(QT, B_QT), (QT2, B_QT2)]

            def pjload(t):
                pj_, Bpj = pj[t % 2]
                P.dma(lambda e: e.dma_start(out=pj_[:, :], in_=projscr[s][t * 128:t * 128 + 128, 0:3072]), Bpj,
                      reads=[B_proj[s][t]], writes=[Bpj])

            def prep(t, part=3):
                ts = t % 4
                row0 = t * 128
                QT_, BQT_ = QTs[(t // 4) % 2]
                pj_, Bpj = pj[t % 2]
                if part & 1:
                    prep1(t, row0, pj_, Bpj)
                if part & 2:
                    prep2(t, ts, row0, QT_, BQT_)

            def prep1(t, row0, pj_, Bpj):
                if t + 1 < 16:
                    pjload(t + 1)
                qk_prep(pj_, Bpj, 128, junk, B_junk, qh, Bqh, kf, Bkf, kh, Bkh)
                P.dma(lambda e: e.dma_start(out=pk[s, row0:row0 + 128, :], in_=kf[:, :]), Bkf, reads=[Bkf], writes=[B_out])
                P.dma(lambda e: e.dma_start(out=pv[s, row0:row0 + 128, :], in_=pj_[:, 2048:3072]), Bpj, reads=[Bpj], writes=[B_out])
                P.op("pool", lambda e: e.tensor_copy(out=Vtile(t), in_=pj_[:, 2048:3072]), reads=[Bpj], writes=[B_V[t]])

            def prep2(t, ts, row0, QT_, BQT_):
                pt, Bp = rotT()
                for h in range(8):
                    P.op("pe", lambda e, pt=pt, h=h: e.transpose(out=pt[:, h, :], in_=qh[:, h * 128:(h + 1) * 128], identity=identb),
                         reads=[Bqh, B_cb], writes=[Bp])
                P.op("act", lambda e, pt=pt: e.activation(out=QT_[:, :, ts * 128:(ts + 1) * 128], in_=pt[:, :, :], func=AF.Copy),
                     reads=[Bp], writes=[BQT_])
                pt2, Bp2 = rotT()
                for h in range(8):
                    P.op("pe", lambda e, pt2=pt2, h=h: e.transpose(out=pt2[:, h, :], in_=kh[:, h * 128:(h + 1) * 128], identity=identb),
                         reads=[Bkh, B_cb], writes=[Bp2])
                P.op("dve", lambda e, pt2=pt2: e.tensor_copy(out=KT[:, :, row0:row0 + 128], in_=pt2[:, :, :]),
                     reads=[Bp2], writes=[B_KT[t]])

            Eb4 = Eb + [AR.b16("E%d" % i, 512) for i in range(2, 4)]
            Pb4 = Pb + [AR.b16("P%d" % i, 512) for i in range(2, 4)]
            lnz2 = [(lnz, B_lnz), AR.f32("lnzB", 512)]
            sTbanks = [[psA[4], psA[5]],
                       [(psT[0][0].rearrange("p a b -> p (a b)").bitcast(F32), psT[0][1]),
                        (psT[1][0].rearrange("p a b -> p (a b)").bitcast(F32), psT[1][1])]]

            def attention_pair(b, hA, hB):
                QT_, BQT_ = QTs[b % 2]
                nkt = 4 * b + 4
                heads = (hA, hB)
                oTs = [psA[0], psA[2]]
                zbs = [psA[1], psA[3]]

                def geom(kt):
                    a = kt - 4 * b
                    q0 = max(0, a) * 128
                    return q0, 512 - q0

                def emit_sT(w, kt):
                    h = heads[w]
                    q0, nq = geom(kt)
                    sT, BsT = sTbanks[w][kt % 2]
                    P.op("pe", lambda e: e.matmul(sT[:, 0:nq], lhsT=KT[:, h, kt * 128:(kt + 1) * 128], rhs=QT_[:, h, q0:512],
                                                  start=True, stop=True), reads=[B_KT[kt], BQT_], writes=[BsT])

                def emit_soft(w, kt):
                    q0, nq = geom(kt)
                    sT, BsT = sTbanks[w][kt % 2]
                    E_, BE = Eb4[2 * w + kt % 2]
                    P_, BP = Pb4[2 * w + kt % 2]
                    soff = b * 512 - kt * 128 + STRIP0 + q0
                    P.op("act", lambda e: e.activation(out=E_[:, 0:nq], in_=sT[:, 0:nq], func=AF.Exp), reads=[BsT], writes=[BE])
                    P.op("dve", lambda e: e.tensor_tensor(out=P_[:, 0:nq], in0=E_[:, 0:nq], in1=strip[:, soff:soff + nq], op=ALU.mult),
                         reads=[BE, B_cb], writes=[BP])

                def emit_pv(w, kt):
                    h = heads[w]
                    q0, nq = geom(kt)
                    P_, BP = Pb4[2 * w + kt % 2]
                    oT, BoT = oTs[w]
                    zb, Bzb = zbs[w]
                    P.op("pe", lambda e: e.matmul(oT[:, q0:512], lhsT=Vtile(kt)[:, h * 128:(h + 1) * 128], rhs=P_[:, 0:nq],
                                                  start=(kt == 0), stop=(kt == nkt - 1)), reads=[B_V[kt], BP], writes=[BoT])
                    P.op("pe", lambda e: e.matmul(zb[:, q0:512], lhsT=onesb, rhs=P_[:, 0:nq], start=(kt == 0), stop=(kt == nkt - 1)),
                         reads=[B_cb, BP], writes=[Bzb])

                emit_sT(0, 0)
                emit_sT(1, 0)
                for kt in range(nkt):
                    if kt + 1 < nkt:
                        emit_sT(0, kt + 1)
                        emit_sT(1, kt + 1)
                    emit_soft(0, kt)
                    emit_soft(1, kt)
                    emit_pv(0, kt)
                    emit_pv(1, kt)
                for w in range(2):
                    h = heads[w]
                    oT, BoT = oTs[w]
                    zb, Bzb = zbs[w]
                    lz, Blz = lnz2[w]
                    m_, Bm = mixA[w]

                    def tail(h=h, oT=oT, BoT=BoT, zb=zb, Bzb=Bzb, lz=lz, Blz=Blz, m_=m_, Bm=Bm):
                        P.op("act", lambda e: e.activation(out=lz[:, :], in_=zb[:, :], func=AF.Ln), reads=[Bzb], writes=[Blz])
                        P.op("act", lambda e: e.activation(out=lz[:, :], in_=lz[:, :], func=AF.Exp, scale=-1.0), reads=[Blz], writes=[Blz])
                        P.op("dve", lambda e: e.tensor_tensor(out=m_[:, :], in0=oT[:, :], in1=lz[:, :], op=ALU.mult),
                             reads=[BoT, Blz], writes=[Bm])
                        P.dma(lambda e: e.dma_start(out=mixscr[s][h, :, b * 512:(b + 1) * 512], in_=m_[:, :]), Bm,
                              reads=[Bm], writes=[B_mix[s]])
                    tail()

            pjload(0)
            for t in range(4):
                prep(t)
            for b in range(4):
                for hp in range(4):
                    if b < 3:
                        prep(4 * (b + 1) + hp, 1)
                    attention_pair(b, 2 * hp, 2 * hp + 1)
                    if b < 3:
                        prep(4 * (b + 1) + hp, 2)
            P.barrier()
            if s != 1:
                return
            def _sample_part():
                AR.reset(PERSIST_TOP)
                KTs = BIG[:, 0:8, 0:128 * 9].rearrange("p h (t n) -> p h t n", t=9)
                pjs, Bpjs = AR.f32("pjs", 3072)
                junk, B_junk = AR.f32("junk", 2048)
                qh, Bqh = AR.b16("qh", 1024)
                kh, Bkh = AR.b16("kh", 1024)
                kf, Bkf = AR.f32("kf", 1024)
                qTs, BqTs = AR.b16("qTs", 32)
                ckt = [AR.f32("ckt%d" % i, 1024) for i in range(2)]
                ckb = [AR.b16("ckb%d" % i, 1024) for i in range(2)]
                Vs = [AR.b16("Vs%d" % i, 1024) for i in range(9)]
                cvt = [AR.f32("cvt%d" % i, 1024) for i in range(2)]
                Es, BEs = AR.b16("Es", 32)
                Ps = [AR.b16("Ps%d" % i, 32) for i in range(9)]
                rzs, Brzs = AR.f32("rzs", 32)
                mAs, BmAs = AR.b16("mAs", 32)
                B_KTs = [bigview("KTs%d" % i, B_KT) for i in range(9)]
                for bi in range(4):
                    r0 = 2048 + 4 * bi
                    P.dma(lambda e, r0=r0: e.dma_start(out=pjs[0:4, :], in_=projscr[1][r0:r0 + 4, 0:3072]), Bpjs,
                          reads=[B_proj[1][16]], writes=[Bpjs])
                    qk_prep(pjs, Bpjs, 4, junk, B_junk, qh, Bqh, kf, Bkf, kh, Bkh)
                    P.dma(lambda e, bi=bi: e.dma_start(out=sk[4 * bi:4 * bi + 4, :], in_=kf[0:4, :]), Bkf, reads=[Bkf], writes=[B_out])
                    P.dma(lambda e, bi=bi: e.dma_start(out=sv[4 * bi:4 * bi + 4, :], in_=pjs[0:4, 2048:3072]), Bpjs, reads=[Bpjs], writes=[B_out])
                    pt, Bp = rotT()
                    for h in range(8):
                        P.op("pe", lambda e, pt=pt, h=h: e.transpose(out=pt[:, h, 0:4], in_=qh[0:4, h * 128:(h + 1) * 128], identity=identb[0:4, 0:4]),
                             reads=[Bqh, B_cb], writes=[Bp])
                    P.op("act", lambda e, pt=pt: e.activation(out=qTs.rearrange("p (h t) -> p h t", h=8), in_=pt[:, :, 0:4], func=AF.Copy),
                         reads=[Bp], writes=[BqTs])
                    def X(i, bi=bi):
                        nk = 128 if i < 8 else 4
                        V_, BV = Vs[i]
                        if i < 8:
                            c_, Bc = ckt[i % 2]
                            cb_, Bcb_ = ckb[i % 2]
                            v_, Bv = cvt[i % 2]
                            if i < 4:
                                srck = ck[bi, 1536 + 128 * i:1536 + 128 * (i + 1), :]
                                srcv = cv[bi, 1536 + 128 * i:1536 + 128 * (i + 1), :]
                            else:
                                r = i - 4
                                srck = ck[bi].rearrange("(m r) c -> r m c", r=16)[r]
                                srcv = cv[bi].rearrange("(m r) c -> r m c", r=16)[r]
                            P.dma(lambda e: e.dma_start(out=c_[:, :], in_=srck), Bc, reads=[B_in], writes=[Bc])
                            P.dma(lambda e: e.dma_start(out=v_[:, :], in_=srcv), Bv, reads=[B_in], writes=[Bv])
                            P.op("dve", lambda e: e.tensor_copy(out=cb_[:, :], in_=c_[:, :]), reads=[Bc], writes=[Bcb_])
                            P.op("act", lambda e: e.activation(out=V_[:, :], in_=v_[:, :], func=AF.Copy), reads=[Bv], writes=[BV])
                            ksrc, Bks = cb_, Bcb_
                        else:
                            P.op("pool", lambda e: e.tensor_copy(out=V_[0:4, :], in_=pjs[0:4, 2048:3072]), reads=[Bpjs], writes=[BV])
                            ksrc, Bks = kh, Bkh
                        pt, Bp = rotT()
                        for h in range(8):
                            P.op("pe", lambda e, h=h: e.transpose(
                                out=pt[:, h, 0:nk], in_=ksrc[0:nk, h * 128:(h + 1) * 128], identity=identb[0:nk, 0:nk]),
                                reads=[Bks, B_cb], writes=[Bp])
                        P.op("dve", lambda e: e.tensor_copy(out=KTs[:, :, i, 0:nk], in_=pt[:, :, 0:nk]),
                             reads=[Bp], writes=[B_KTs[i]])

                    def Y(i):
                        nk = 128 if i < 8 else 4
                        sT, BsT = psA[4 + i % 2]
                        for h in range(8):
                            P.op("pe", lambda e, h=h: e.matmul(
                                sT[0:nk, h * 4:h * 4 + 4], lhsT=KTs[:, h, i, 0:nk], rhs=qTs[:, h * 4:h * 4 + 4], start=True, stop=True),
                                reads=[B_KTs[i], BqTs], writes=[BsT])
                        P_, BP = Ps[i]
                        P.op("act", lambda e: e.activation(out=Es[0:nk, :], in_=sT[0:nk, 0:32], func=AF.Exp), reads=[BsT], writes=[BEs])
                        P.op("dve", lambda e: e.tensor_tensor(out=P_[0:nk, :], in0=Es[0:nk, :], in1=smask[0:nk, i * 32:(i + 1) * 32],
                                                              op=ALU.mult), reads=[BEs, B_cb], writes=[BP])

                    X(0)
                    for i in range(9):
                        if i + 1 < 9:
                            X(i + 1)
                        Y(i)
                    oT, BoT = psA[bi % 2]
                    zb, Bzb = psA[2 + bi % 2]
                    for h in range(8):
                        for i in range(9):
                            nk = 128 if i < 8 else 4
                            P.op("pe", lambda e, oT=oT, h=h, i=i, nk=nk: e.matmul(
                                oT[:, h * 4:h * 4 + 4], lhsT=Vs[i][0][0:nk, h * 128:(h + 1) * 128], rhs=Ps[i][0][0:nk, h * 4:h * 4 + 4],
                                start=(i == 0), stop=(i == 8)), reads=[Vs[i][1], Ps[i][1]], writes=[BoT])
                    for i in range(9):
                        nk = 128 if i < 8 else 4
                        P.op("pe", lambda e, zb=zb, i=i, nk=nk: e.matmul(zb[:, 0:32], lhsT=onesb[0:nk, :], rhs=Ps[i][0][0:nk, :],
                                                                        start=(i == 0), stop=(i == 8)), reads=[B_cb, Ps[i][1]], writes=[Bzb])
                    P.op("dve", lambda e, zb=zb: e.reciprocal(out=rzs[:, :], in_=zb[:, 0:32]), reads=[Bzb], writes=[Brzs])
                    P.op("dve", lambda e, oT=oT: e.tensor_tensor(out=mAs[:, :], in0=oT[:, 0:32], in1=rzs[:, :], op=ALU.mult),
                         reads=[BoT, Brzs], writes=[BmAs])
                    P.dma(lambda e, r0=r0: e.dma_start(out=mixscr[1][0:8, :, r0:r0 + 4].rearrange("h p t -> p h t"),
                                                      in_=mAs.rearrange("p (h t) -> p h t", h=8)), BmAs, reads=[BmAs], writes=[B_mix[1]])
                P.barrier()

            _sample_part()

        def phase_B2(s):
            AR.reset(PERSIST_TOP)
            pjg = [AR.f32("pjg%d" % i, 3088) for i in range(2)]
            lbuf, Bl = AR.f32("lbuf", 512)
            Ebuf = [AR.f32("Eb%d" % i, 512) for i in range(2)]
            qd, Bqd = AR.b16("qd", 512)
            ki, Bki = AR.b16("ki", 512)
            ke, Bke = AR.b16("ke", 512)
            vb16, Bvb = AR.b16("vb16", 1024)
            sr, Bsr = AR.f32("sr", 1024)
            ob, Bob = AR.f32("ob", 1024)
            junk, B_junk = AR.f32("junkg", 1024)
            mixb, Bmixb = AR.b16("mixb", 1024)
            qdT, BqdT = AR.b16("qdT", 512, ("p (h n) -> p h n", dict(h=4)))
            kiT, BkiT = AR.b16("kiT", 512, ("p (h n) -> p h n", dict(h=4)))
            attm = [AR.b16("attm%d" % i, 128) for i in range(4)]
            dec, Bdec = AR.f32("dec", 8)
            mixBT, BmixBT = AR.b16("mixBT", 8 * 512, ("p (c n) -> p c n", dict(c=8)))

            import os
            KSTEP = int(os.environ.get("KGLA_STEP", "99"))
            KTILES = int(os.environ.get("KGLA_TILES", "16"))

            def gla_tile(pj_, Bpj, nt, colslot):
                ps, Bps = psA[0]
                P.op("pe", lambda e: e.transpose(out=ps[0:16, 0:nt], in_=pj_[0:nt, 3072:3088], identity=identf[0:nt, 0:nt]),
                     reads=[Bpj, B_cf], writes=[Bps])
                P.op("dve", lambda e: e.tensor_copy(out=gaT[0:16, 0:nt], in_=ps[0:16, 0:nt]), reads=[Bps], writes=[B_gaT])
                pz, Bpz = psA[1]
                P.op("pe", lambda e: e.matmul(pz[0:nt, :], lhsT=gaT[0:17, 0:nt], rhs=wga[0:17, :], start=True, stop=True),
                     reads=[B_gaT, B_wga], writes=[Bpz])
                P.op("act", lambda e: e.activation(out=lbuf[0:nt, :], in_=pz[0:nt, :], func=AF.Exp, scale=-1.0), reads=[Bpz], writes=[Bl])
                P.op("act", lambda e: e.activation(out=lbuf[0:nt, :], in_=lbuf[0:nt, :], func=AF.Ln, bias=1.0), reads=[Bl], writes=[Bl])
                if KSTEP <= 1:
                    return
                pc, Bpc = psA[2]
                P.op("pe", lambda e: e.matmul(pc[0:nt, :], lhsT=Umat[0:nt, 0:nt], rhs=lbuf[0:nt, :], start=True, stop=True),
                     reads=[B_cf, Bl], writes=[Bpc])
                pr, Bpr = psA[3]
                P.op("pe", lambda e: e.matmul(pr[0:nt, :], lhsT=Lmat[0:nt, 0:nt], rhs=lbuf[0:nt, :], start=True, stop=True),
                     reads=[B_cf, Bl], writes=[Bpr])
                ptt, Bptt = psA[0]
                for h in range(4):
                    P.op("pe", lambda e, h=h: e.matmul(ptt[:, h:h + 1], lhsT=lbuf[0:nt, h * 128:(h + 1) * 128], rhs=onescol[0:nt, :],
                                                       start=True, stop=True), reads=[Bl, B_cf], writes=[Bptt])
                if KSTEP <= 2:
                    return
                E0, BE0 = Ebuf[0]
                E1, BE1 = Ebuf[1]
                P.op("act", lambda e: e.activation(out=E0[0:nt, :], in_=pc[0:nt, :], func=AF.Exp, scale=-1.0 / 16), reads=[Bpc], writes=[BE0])
                P.op("dve", lambda e: e.scalar_tensor_tensor(out=qd[0:nt, :], in0=pj_[0:nt, 0:512], scalar=128.0 ** -0.5, in1=E0[0:nt, :],
                                                             op0=ALU.mult, op1=ALU.mult), reads=[Bpj, BE0], writes=[Bqd])
                P.op("act", lambda e: e.activation(out=E1[0:nt, :], in_=pc[0:nt, :], func=AF.Exp, scale=1.0 / 16), reads=[Bpc], writes=[BE1])
                P.op("dve", lambda e: e.tensor_tensor(out=ki[0:nt, :], in0=pj_[0:nt, 512:1024], in1=E1[0:nt, :], op=ALU.mult),
                     reads=[Bpj, BE1], writes=[Bki])
                P.op("act", lambda e: e.activation(out=E0[0:nt, :], in_=pr[0:nt, :], func=AF.Exp, scale=-1.0 / 16), reads=[Bpr], writes=[BE0])
                P.op("dve", lambda e: e.tensor_tensor(out=ke[0:nt, :], in0=pj_[0:nt, 512:1024], in1=E0[0:nt, :], op=ALU.mult),
                     reads=[Bpj, BE0], writes=[Bke])
                P.op("act", lambda e: e.activation(out=dec[:, 0:4], in_=ptt[:, 0:4],
                                                   func=AF.Exp, scale=-1.0 / 16), reads=[Bptt], writes=[Bdec])
                if KSTEP <= 3:
                    return
                P.op("dve", lambda e: e.tensor_copy(out=vb16[0:nt, :], in_=pj_[0:nt, 1024:2048]), reads=[Bpj], writes=[Bvb])
                P.op("act", lambda e: e.activation(out=sr[0:nt, :], in_=pj_[0:nt, 2048:3072], func=AF.Silu), reads=[Bpj], writes=[Bsr])
                if KSTEP <= 4:
                    return
                pt1, Bp1 = rotT()
                for h in range(4):
                    P.op("pe", lambda e, h=h: e.transpose(out=pt1[:, h, 0:nt], in_=qd[0:nt, h * 128:(h + 1) * 128], identity=identb[0:nt, 0:nt]),
                         reads=[Bqd, B_cb], writes=[Bp1])
                for h in range(4):
                    P.op("pe", lambda e, h=h: e.transpose(out=pt1[:, 4 + h, 0:nt], in_=ki[0:nt, h * 128:(h + 1) * 128], identity=identb[0:nt, 0:nt]),
                         reads=[Bki, B_cb], writes=[Bp1])
                P.op("act", lambda e: e.activation(out=qdT[:, :, 0:nt], in_=pt1[:, 0:4, 0:nt], func=AF.Copy), reads=[Bp1], writes=[BqdT])
                P.op("dve", lambda e: e.tensor_copy(out=kiT[:, :, 0:nt], in_=pt1[:, 4:8, 0:nt]), reads=[Bp1], writes=[BkiT])
                if KSTEP <= 5:
                    return
                po = [psA[4], psA[5]]
                for h in range(4):
                    pa, Bpa = psA[h % 2]
                    am, Bam = attm[h]
                    P.op("pe", lambda e, h=h, pa=pa: e.matmul(pa[0:nt, 0:nt], lhsT=kiT[:, h, 0:nt], rhs=qdT[:, h, 0:nt], start=True, stop=True),
                         reads=[BkiT, BqdT], writes=[Bpa])
                    P.op("dve", lambda e, pa=pa, am=am: e.tensor_tensor(out=am[0:nt, 0:nt], in0=pa[0:nt, 0:nt], in1=Umat[0:nt, 0:nt], op=ALU.mult),
                         reads=[Bpa, B_cf], writes=[Bam])
                    o_, Bo = po[h // 2]
                    c0 = (h % 2) * 256
                    P.op("pe", lambda e, h=h, o_=o_, c0=c0, am=am: e.matmul(o_[0:nt, c0:c0 + 256], lhsT=am[0:nt, 0:nt],
                                                                           rhs=vb16[0:nt, h * 256:(h + 1) * 256], start=True, stop=False),
                         reads=[Bam, Bvb], writes=[Bo])
                    P.op("pe", lambda e, h=h, o_=o_, c0=c0: e.matmul(o_[0:nt, c0:c0 + 256], lhsT=qdT[:, h, 0:nt], rhs=Sb[:, h, :],
                                                                    start=False, stop=True), reads=[BqdT, B_Sb], writes=[Bo])
                    pd, Bpd = psA[2 + h % 2]
                    P.op("pe", lambda e, h=h, pd=pd: e.matmul(pd[:, 0:256], lhsT=ke[0:nt, h * 128:(h + 1) * 128],
                                                              rhs=vb16[0:nt, h * 256:(h + 1) * 256], start=True, stop=True),
                         reads=[Bke, Bvb], writes=[Bpd])
                    P.op("dve", lambda e, h=h, pd=pd: e.scalar_tensor_tensor(out=Sst[:, h, :], in0=Sst[:, h, :], scalar=dec[:, h:h + 1],
                                                                            in1=pd[:, 0:256], op0=ALU.mult, op1=ALU.add),
                         reads=[B_S, Bdec, Bpd], writes=[B_S])
                    P.op("act", lambda e, h=h: e.activation(out=Sb[:, h, :], in_=Sst[:, h, :], func=AF.Copy), reads=[B_S], writes=[B_Sb])
                if KSTEP <= 6:
                    return
                for j in range(2):
                    o_, Bo = po[j]
                    P.op("act", lambda e, j=j, o_=o_: e.activation(out=ob[0:nt, j * 512:(j + 1) * 512], in_=o_[0:nt, :], func=AF.Copy),
                         reads=[Bo], writes=[Bob])
                P.op("dve", lambda e: e.tensor_tensor(out=junk[0:nt, :], in0=ob[0:nt, :], in1=ob[0:nt, :], op=ALU.mult), reads=[Bob], writes=[B_junk])
                ssq, Bq = newstat(4)
                P.op("dve", lambda e: e.tensor_reduce(out=ssq[0:nt, :], in_=junk[0:nt, :].rearrange("p (h d) -> p h d", h=4), axis=AX.X, op=ALU.add),
                     reads=[B_junk], writes=[Bq])
                P.op("act", lambda e: e.activation(out=ssq[0:nt, :], in_=ssq[0:nt, :], func=AF.Sqrt, scale=1.0 / 256, bias=EPS), reads=[Bq], writes=[Bq])
                P.op("dve", lambda e: e.reciprocal(out=ssq[0:nt, :], in_=ssq[0:nt, :]), reads=[Bq], writes=[Bq])
                for h in range(4):
                    P.op("dve", lambda e, h=h: e.scalar_tensor_tensor(out=junk[0:nt, h * 256:(h + 1) * 256], in0=ob[0:nt, h * 256:(h + 1) * 256],
                                                                      scalar=ssq[0:nt, h:h + 1], in1=gg_bc[0:nt, h * 256:(h + 1) * 256],
                                                                      op0=ALU.mult, op1=ALU.mult), reads=[Bob, Bq, B_gg, B_junk], writes=[B_junk])
                P.op("dve", lambda e: e.tensor_tensor(out=mixb[0:nt, :], in0=junk[0:nt, :], in1=sr[0:nt, :], op=ALU.mult),
                     reads=[B_junk, Bsr], writes=[Bmixb])
                if KSTEP <= 7:
                    return
                pt, Bp = rotT()
                for j in range(8):
                    P.op("pe", lambda e, j=j, pt=pt: e.transpose(out=pt[:, j, 0:nt], in_=mixb[0:nt, j * 128:(j + 1) * 128], identity=identb[0:nt, 0:nt]),
                         reads=[Bmixb, B_cb], writes=[Bp])
                P.op("act", lambda e, pt=pt: e.activation(out=mixBT[:, :, colslot:colslot + nt], in_=pt[:, :, 0:nt], func=AF.Copy),
                     reads=[Bp], writes=[BmixBT])

            P.op("pool", lambda e: e.memset(Sst[:, :, :], 0.0), writes=[B_S])
            P.op("pool", lambda e: e.memset(Sb[:, :, :], 0.0), writes=[B_Sb])
            def pjgload(t):
                pj_, Bpj = pjg[t % 2]
                P.dma(lambda e: e.dma_start(out=pj_[:, :], in_=projscr[s][t * 128:t * 128 + 128, 3072:NIN]), Bpj,
                      reads=[B_proj[s][t]], writes=[Bpj])

            pjgload(0)
            for t in range(KTILES):
                pj_, Bpj = pjg[t % 2]
                row0 = t * 128
                if t + 1 < KTILES:
                    pjgload(t + 1)
                gla_tile(pj_, Bpj, 128, (t % 4) * 128)
                if t % 4 == 3:
                    b = t // 4
                    P.dma(lambda e, b=b: e.dma_start(out=mixscr[s][8:16, :, b * 512:(b + 1) * 512].rearrange("c p n -> p c n"),
                                                    in_=mixBT[:, :, :]), BmixBT, reads=[BmixBT], writes=[B_mix[s]])
            P.dma(lambda e: e.dma_start(out=pst[s].rearrange("h p v -> p h v"), in_=Sst[:, :, :]), B_S, reads=[B_S], writes=[B_out])
            if s == 1:
                for bi in range(4):
                    pj_, Bpj = pjg[bi % 2]
                    r0 = 2048 + 4 * bi
                    P.dma(lambda e, bi=bi: e.dma_start(out=Sst[:, :, :], in_=sg[bi].rearrange("h p v -> p h v")), B_S, reads=[B_in], writes=[B_S])
                    P.op("pool", lambda e: e.tensor_copy(out=Sb[:, :, :], in_=Sst[:, :, :]), reads=[B_S], writes=[B_Sb])
                    P.dma(lambda e, pj_=pj_, r0=r0: e.dma_start(out=pj_[0:4, :], in_=projscr[1][r0:r0 + 4, 3072:NIN]), Bpj,
                          reads=[B_proj[1][16]], writes=[Bpj])
                    gla_tile(pj_, Bpj, 4, 4 * bi)
                    P.dma(lambda e, bi=bi: e.dma_start(out=sst[bi].rearrange("h p v -> p h v"), in_=Sst[:, :, :]), B_S, reads=[B_S], writes=[B_out])
                P.dma(lambda e: e.dma_start(out=mixscr[1][8:16, :, 2048:2064].rearrange("c p n -> p c n"), in_=mixBT[:, :, 0:16]), BmixBT,
                      reads=[BmixBT], writes=[B_mix[1]])
            P.barrier()

        def phase_D(s):
            AR.reset(PERSIST_TOP)
            wst = [AR.f32("wst%d" % i, 16 * 256, ("p (k n) -> p k n", dict(k=16)), top=True) for i in range(2)]
            wbf = [AR.b16("wbf%d" % i, 16 * 256, ("p (k n) -> p k n", dict(k=16)), top=True) for i in range(2)]
            ust = [AR.b16("ust%d" % i, 2 * NTOK, ("p (c n) -> p c n", dict(c=2)), top=True) for i in range(2)]
            rt = [AR.f32("rt%d" % i, 512, top=True) for i in range(2)]
            Wpk = w_up.rearrange("(k p) n -> p k n", p=128)
            ntok = 2048 + (16 if s == 1 else 0)
            blocks = [(i * 512, 512) for i in range(4)] if s == 0 else [(0, 416), (416, 416), (832, 416), (1248, 416), (1664, 400)]
            ngr = DFF // 256

            def wload(gi):
                ws, Bws = wst[gi % 2]
                wb, Bwb = wbf[gi % 2]
                P.dma(lambda e: e.dma_start(out=ws[:, :, :], in_=Wpk[:, :, gi * 256:(gi + 1) * 256]), Bws, reads=[B_in], writes=[Bws])
                P.op("dve", lambda e: e.tensor_copy(out=wb[:, 0:8, :], in_=ws[:, 0:8, :]), reads=[Bws], writes=[Bwb])
                P.op("act", lambda e: e.activation(out=wb[:, 8:16, :], in_=ws[:, 8:16, :], func=AF.Copy), reads=[Bws], writes=[Bwb])

            wload(0)
            cnt = 0
            for gi in range(ngr):
                if gi + 1 < ngr:
                    wload(gi + 1)
                wb, Bwb = wbf[gi % 2]
                us, Bus = ust[gi % 2]
                for cc in range(2):
                    for (tb0, nb) in blocks:
                        ps, Bps = rotA()
                        for k in range(16):
                            P.op("pe", lambda e, ps=ps, k=k, cc=cc, tb0=tb0, nb=nb, wb=wb: e.matmul(
                                ps[:, 0:nb], lhsT=wb[:, k, cc * 128:(cc + 1) * 128], rhs=BIG[:, k, tb0:tb0 + nb],
                                start=(k == 0), stop=(k == 15)), reads=[B_big, Bwb], writes=[Bps])
                        r_, Br = rt[cnt % 2]
                        cnt += 1
                        P.op("act", lambda e, ps=ps, r_=r_, nb=nb: e.activation(out=r_[:, 0:nb], in_=ps[:, 0:nb], func=AF.Relu),
                             reads=[Bps], writes=[Br])
                        P.op("dve", lambda e, r_=r_, us=us, cc=cc, tb0=tb0, nb=nb: e.tensor_tensor(
                            out=us[:, cc, tb0:tb0 + nb], in0=r_[:, 0:nb], in1=r_[:, 0:nb], op=ALU.mult), reads=[Br], writes=[Bus])
                for cc in range(2):
                    P.dma(lambda e, us=us, gi=gi, cc=cc: e.dma_start(
                        out=utscr[s][0:16, :, 2 * gi + cc, :].rearrange("t p n -> p t n"),
                        in_=us[:, cc, 0:2048].rearrange("p (t n) -> p t n", t=16)), Bus, reads=[Bus], writes=B_ut[s][0:16])
                if s == 1:
                    P.dma(lambda e, us=us, gi=gi: e.dma_start(out=utscr[s][16, :, 2 * gi:2 * gi + 2, 0:16], in_=us[:, :, 2048:2064]), Bus,
                          reads=[Bus], writes=[B_ut[s][16]])
            P.barrier()

        def phase_E(s):
            AR.reset(PERSIST_TOP)
            def wres(c):
                return BIG[:, c // 4, (c % 4) * 512:(c % 4) * 512 + 512]
            wst = [AR.f32("wst%d" % i, 8 * 512, ("p (k n) -> p k n", dict(k=8)), top=True) for i in range(2)]
            ut = [AR.b16("ut%d" % i, 64 * 128, ("p (c n) -> p c n", dict(c=64))) for i in range(2)]
            x1s = [AR.f32("x1s%d" % i, 512) for i in range(2)]
            yst = [AR.f32("yst%d" % i, 512) for i in range(2)]
            Wpc = w_down.rearrange("(c p) n -> p c n", p=128)
            tl = tiles_of(s)
            steps = [(g, t, row0, nt) for g in range(4) for (t, row0, nt) in tl]
            wcnt = [0]

            def wgroup(g):
                c0 = g * 512
                for i in range(8):
                    ws, Bws = wst[wcnt[0] % 2]
                    wcnt[0] += 1
                    P.dma(lambda e, ws=ws, i=i, c0=c0: e.dma_start(out=ws[:, :, :], in_=Wpc[:, 8 * i:8 * i + 8, c0:c0 + 512]), Bws,
                          reads=[B_in], writes=[Bws])
                    if i % 2 == 0:
                        P.op("dve", lambda e, ws=ws, i=i: e.tensor_copy(out=BIG[:, 2 * i:2 * i + 2, 0:2048].rearrange("p k (a n) -> p k a n", n=512),
                                                                        in_=ws[:, :, :].rearrange("p (k a) n -> p k a n", k=2)),
                             reads=[Bws], writes=[B_big])
                    else:
                        P.op("act", lambda e, ws=ws, i=i: e.activation(out=BIG[:, 2 * i:2 * i + 2, 0:2048].rearrange("p k (a n) -> p k a n", n=512),
                                                                       in_=ws[:, :, :].rearrange("p (k a) n -> p k a n", k=2), func=AF.Copy),
                             reads=[Bws], writes=[B_big])

            def pre(j):
                g, t, row0, nt = steps[j]
                c0 = g * 512
                u_, Bu = ut[j % 2]
                x_, Bx = x1s[j % 2]
                P.dma(lambda e: e.dma_start(out=u_[:, :, 0:nt], in_=utscr[s][t, :, :, 0:nt]), Bu, reads=[B_ut[s][t]], writes=[Bu])
                P.dma(lambda e: e.dma_start(out=x_[0:nt, :], in_=x1scr[s][row0:row0 + nt, c0:c0 + 512]), Bx, reads=[B_x1[s][t]], writes=[Bx])

            pre(0)
            for j, (g, t, row0, nt) in enumerate(steps):
                c0 = g * 512
                if t == 0:
                    wgroup(g)
                if j + 1 < len(steps):
                    pre(j + 1)
                u_, Bu = ut[j % 2]
                x_, Bx = x1s[j % 2]
                y_, By = yst[j % 2]
                ps, Bps = rotA()
                for c in range(64):
                    P.op("pe", lambda e, ps=ps, c=c, u_=u_, nt=nt: e.matmul(ps[0:nt, :], lhsT=u_[:, c, 0:nt], rhs=wres(c),
                                                                            start=(c == 0), stop=(c == 63)),
                         reads=[Bu, B_big], writes=[Bps])
                P.op("dve", lambda e, ps=ps, x_=x_, y_=y_, nt=nt: e.tensor_tensor(out=y_[0:nt, :], in0=ps[0:nt, :], in1=x_[0:nt, :], op=ALU.add),
                     reads=[Bps, Bx], writes=[By])
                dst = yp[s, row0:row0 + nt, c0:c0 + 512] if t < 16 else ys[0:nt, c0:c0 + 512]
                P.dma(lambda e, y_=y_, dst=dst, nt=nt: e.dma_start(out=dst, in_=y_[0:nt, :]), By, reads=[By], writes=[B_out])
            P.barrier()

        P.barrier(force=True)
        for s in range(2):
            phase_norm(s, lambda t, nt, s=s: xrows(s, t, nt), [B_in] * 17, ag)
            if stop == "A0":
                break
            if not os.environ.get("KSKIP"):
                phase_gemm_tok(s, w_in, NIN, evac_A1(s))
            if stop == "A1":
                break
            if not os.environ.get("KSKIP"):
                phase_B1(s)
            if stop == "B1":
                break
            phase_B2(s)
            if stop == "B2":
                break
            AR.reset(PERSIST_TOP)
            ntok = 2048 + (16 if s == 1 else 0)
            for half in range(2):
                P.dma(lambda e, half=half, s=s, ntok=ntok: e.dma_start(
                    out=BIG[:, half * 8:half * 8 + 8, 0:ntok],
                    in_=mixscr[s][half * 8:half * 8 + 8, :, 0:ntok].rearrange("c p n -> p c n")), B_big,
                    reads=[B_mix[s]], writes=[B_big])
            phase_gemm_tok(s, w_out, D, evac_C(s))
            if stop == "C":
                break
            phase_norm(s, lambda t, nt, s=s: x1scr[s][t * 128:t * 128 + nt, :], B_x1[s], mg)
            phase_D(s)
            if stop == "D":
                break
            phase_E(s)
            if stop == "E":
                break
        P.emit(nc, es)
    return nc


def _consts():
    cf = np.zeros((128, 392), np.float32)
    j = np.arange(128)[:, None]
    i = np.arange(128)[None, :]
    cf[:, 0:128] = (j == i)
    cf[:, 128:256] = (j <= i)
    cf[:, 256:384] = (j > i)
    cf[:, 384] = 1.0
    cb = np.zeros((128, 256 + STRIPW + 288), np.float32)
    cb[:, 0:128] = (j == i)
    cb[:, 128:256] = 1.0
    c = np.arange(STRIPW)[None, :] - STRIP0
    dl = c - j
    m = ((dl <= 128).astype(np.float32) + ((dl % 4 == 0) & (dl <= 512)) + ((dl % 16 == 0) & (dl <= 2048))) * (dl >= 0)
    cb[:, 256:256 + STRIPW] = m
    sm = np.zeros((128, 9, 8, 4), np.float32)
    p = np.arange(128)
    for t in range(4):
        for ti in range(4):
            dl = 2048 + t - (1536 + 128 * ti + p)
            sm[:, ti, :, t] = ((dl <= 128).astype(np.float32) + ((dl % 4 == 0) & (dl <= 512)))[:, None]
        for r in range(4):
            sm[:, 4 + r, :, t] = 1.0 if t == r else 0.0
        for tp in range(4):
            sm[tp, 8, :, t] = 0.0 if tp > t else (3.0 if tp == t else 1.0)
    cb[:, 256 + STRIPW:] = sm.reshape(128, 288)
    return cf, cb.astype(ml_dtypes.bfloat16)


_NC_CACHE = {}


def kernel(x_prompt, x_sample, cache_win_k, cache_win_v, state_gla, attn_norm_g, w_in, q_norm_g, k_norm_g,
           w_gate2, b_gate, gla_norm_g, w_out, mlp_norm_g, w_up, w_down, _cores=None, _stop=None):
    f = lambda a: np.ascontiguousarray(np.asarray(a, dtype=np.float32))
    x_prompt, x_sample = f(x_prompt), f(x_sample)
    cache_win_k, cache_win_v, state_gla = f(cache_win_k), f(cache_win_v), f(state_gla)
    cf, cb = _consts()
    shared = {
        "w_in": f(w_in)[0], "w_out": f(w_out)[0], "w_up": f(w_up)[0], "w_down": f(w_down)[0],
        "ag": f(attn_norm_g).reshape(1, D), "mg": f(mlp_norm_g).reshape(1, D),
        "qg": f(q_norm_g).reshape(1, 128), "kg": f(k_norm_g).reshape(1, 128),
        "gg": f(gla_norm_g).reshape(1, 1024), "wg2": f(w_gate2)[0], "bg": f(b_gate).reshape(1, 512),
        "constf": cf, "constb": cb,
    }
    cores = list(range(NCORES)) if _cores is None else list(_cores)
    in_maps = []
    for c in cores:
        m = dict(shared)
        m["xp"] = x_prompt[2 * c:2 * c + 2]
        m["xs"] = x_sample[4 * c:4 * c + 4].reshape(16, D)
        m["ck"] = cache_win_k[0, 4 * c:4 * c + 4].reshape(4, SEQ, 1024)
        m["cv"] = cache_win_v[0, 4 * c:4 * c + 4].reshape(4, SEQ, 1024)
        m["sg"] = state_gla[0, 4 * c:4 * c + 4]
        in_maps.append(m)
    if _stop not in _NC_CACHE:
        _NC_CACHE[_stop] = build(_stop)
    res = run_bass_kernel_spmd(_NC_CACHE[_stop], in_maps, core_ids=list(range(len(cores))))
    R = res.results
    if _stop:
        return R
    n = len(cores)
    y_p = np.concatenate([R[i]["yp"] for i in range(n)], 0)
    y_s = np.concatenate([R[i]["ys"].reshape(4, 4, D) for i in range(n)], 0)
    p_k = np.concatenate([R[i]["pk"].reshape(2, SEQ, 8, 128) for i in range(n)], 0)[None]
    p_v = np.concatenate([R[i]["pv"].reshape(2, SEQ, 8, 128) for i in range(n)], 0)[None]
    p_s = np.concatenate([R[i]["pst"] for i in range(n)], 0)[None]
    s_k = np.concatenate([R[i]["sk"].reshape(4, 4, 8, 128) for i in range(n)], 0)[None]
    s_v = np.concatenate([R[i]["sv"].reshape(4, 4, 8, 128) for i in range(n)], 0)[None]
    s_s = np.concatenate([R[i]["sst"] for i in range(n)], 0)[None]
    return (y_p, y_s, p_k, p_v, p_s, s_k, s_v, s_s)
```

```python
import os
import numpy as np
from contextlib import ExitStack
import ml_dtypes
import concourse.bass as bass
import concourse.mybir as mybir
from concourse.bass_utils import run_bass_kernel_spmd

F32 = mybir.dt.float32
BF16 = mybir.dt.bfloat16
AF = mybir.ActivationFunctionType
ALU = mybir.AluOpType
AX = mybir.AxisListType

NCORES = 8
SEQ = 2048
D = 2048
NIN = 6160
DFF = 8192
EPS = 1e-6
NTOK = 2176
STRIP0 = 384
STRIPW = 2432


class Buf:
    __slots__ = ("name", "lw", "rd", "dsem", "alias", "keep")

    def __init__(self, name):
        self.name = name
        self.lw = {}
        self.rd = {}
        self.dsem = None
        self.alias = ()
        self.keep = False


class Prog:
    ENG = ("pe", "act", "dve", "pool", "sp")

    def __init__(self):
        self.q = {e: [] for e in self.ENG}
        self.cnt = {}
        self.known = {e: {} for e in self.ENG}
        self.nsem = 0

    def _issue(self, eng, fn, reads, writes, semkey, inc):
        d = {}
        for b in reads:
            for k, v in b.lw.items():
                if d.get(k, 0) < v:
                    d[k] = v
        ali = False
        for b in writes:
            for k, v in b.lw.items():
                if d.get(k, 0) < v:
                    d[k] = v
            for k, v in b.rd.items():
                if d.get(k, 0) < v:
                    d[k] = v
            if b.alias:
                al = b.alias() if callable(b.alias) else b.alias
                for ab in al:
                    if ab is b:
                        continue
                    ali = True
                    for k, v in ab.lw.items():
                        if d.get(k, 0) < v:
                            d[k] = v
                    for k, v in ab.rd.items():
                        if d.get(k, 0) < v:
                            d[k] = v
                if not b.keep:
                    b.alias = ()
        waits = []
        kn = self.known[eng]
        for k, v in d.items():
            if k == "pe" and eng == "pe" and not ali:
                continue
            if isinstance(k, tuple):
                v = self.cnt[k]
            if kn.get(k, 0) >= v:
                continue
            kn[k] = v
            waits.append((k, v))
        self.cnt[semkey] = self.cnt.get(semkey, 0) + inc
        val = self.cnt[semkey]
        self.q[eng].append((waits, fn, semkey, inc))
        for b in reads:
            if b.rd.get(semkey, 0) < val:
                b.rd[semkey] = val
        for b in writes:
            if b.lw.get(semkey, 0) < val:
                b.lw[semkey] = val

    def op(self, eng, fn, reads=(), writes=()):
        self._issue(eng, fn, reads, writes, eng, 1)

    def dma(self, fn, sbuf, reads=(), writes=()):
        if sbuf.dsem is None:
            sbuf.dsem = ("d", self.nsem)
            self.nsem += 1
        self._issue("sp", fn, reads, writes, sbuf.dsem, 16)

    def barrier(self, force=False):
        if not force:
            return
        for e in self.ENG:
            kn = self.known[e]
            waits = []
            for k, v in self.cnt.items():
                if kn.get(k, 0) < v:
                    kn[k] = v
                    waits.append((k, v))
            if waits:
                self.q[e].append((waits, None, None, 0))

    def emit(self, nc, es):
        sems = {}
        for k in sorted(self.cnt.keys(), key=str):
            nm = k if isinstance(k, str) else "d%d" % k[1]
            sems[k] = es.enter_context(nc.semaphore("s_" + nm))
        block = es.enter_context(nc.Block())
        final = dict(self.cnt)

        def run(ename, e):
            for waits, fn, semkey, inc in self.q[ename]:
                for k, v in waits:
                    e.wait_ge(sems[k], v)
                if fn is not None:
                    fn(e).then_inc(sems[semkey], inc)
            if ename == "sp":
                for k, v in final.items():
                    e.wait_ge(sems[k], v)

        @block.tensor
        def _(e):
            run("pe", e)

        @block.scalar
        def _(e):
            run("act", e)

        @block.vector
        def _(e):
            run("dve", e)

        @block.gpsimd
        def _(e):
            run("pool", e)

        @block.sync
        def _(e):
            run("sp", e)


class Arena:
    def __init__(self, tensor, nwords):
        self.t = tensor
        self.n = nwords
        self.top = 0
        self.hi = nwords
        self.hist = []
        self.track = False

    def reset(self, base):
        self.top = base
        self.hi = self.n

    def _take(self, name, nw, top):
        nw = (nw + 7) // 8 * 8
        if top:
            self.hi -= nw
            off = self.hi
        else:
            off = self.top
            self.top += nw
        assert self.top <= self.hi <= self.n, (name, self.top, self.hi, self.n)
        b = Buf(name)
        if self.track:
            b.alias = [ob for (lo, hi, ob) in self.hist if lo < off + nw and off < hi]
            self.hist.append((off, off + nw, b))
        return off, b

    def f32(self, name, ncols, shape=None, top=False):
        off, b = self._take(name, ncols, top)
        ap = self.t[:, off:off + ncols]
        if shape:
            ap = ap.rearrange(shape[0], **shape[1])
        return ap, b

    def b16(self, name, ncols, shape=None, top=False):
        nw = (ncols + 1) // 2
        off, b = self._take(name, nw, top)
        ap = self.t[:, off:off + nw].bitcast(BF16)[:, 0:ncols]
        if shape:
            ap = ap.rearrange(shape[0], **shape[1])
        return ap, b


def tiles_of(s):
    tl = [(t, t * 128, 128) for t in range(16)]
    if s == 1:
        tl.append((16, 2048, 16))
    return tl


def build(stop=None):
    nc = bass.Bass("TRN2", target_bir_lowering=False)

    def din(name, shape, dt=F32):
        return nc.dram_tensor(name, list(shape), dt, kind="ExternalInput").ap()

    def dout(name, shape, dt=F32):
        return nc.dram_tensor(name, list(shape), dt, kind="ExternalOutput").ap()

    def dscr(name, shape, dt=F32):
        return nc.dram_tensor(name, list(shape), dt, kind="ExternalOutput" if (stop or os.environ.get("KSCR")) else "Internal").ap()

    xp = din("xp", [2, SEQ, D])
    xs = din("xs", [16, D])
    ck = din("ck", [4, SEQ, 1024])
    cv = din("cv", [4, SEQ, 1024])
    sg = din("sg", [4, 4, 128, 256])
    w_in = din("w_in", [D, NIN])
    w_out = din("w_out", [D, D])
    w_up = din("w_up", [D, DFF])
    w_down = din("w_down", [DFF, D])
    ag = din("ag", [1, D])
    mg = din("mg", [1, D])
    qg = din("qg", [1, 128])
    kg = din("kg", [1, 128])
    gg = din("gg", [1, 1024])
    wg2 = din("wg2", [16, 512])
    bg = din("bg", [1, 512])
    constf = din("constf", [128, 392])
    constb = din("constb", [128, 256 + STRIPW + 288], BF16)

    yp = dout("yp", [2, SEQ, D])
    ys = dout("ys", [16, D])
    pk = dout("pk", [2, SEQ, 1024])
    pv = dout("pv", [2, SEQ, 1024])
    pst = dout("pst", [2, 4, 128, 256])
    sk = dout("sk", [16, 1024])
    sv = dout("sv", [16, 1024])
    sst = dout("sst", [4, 4, 128, 256])

    projscr = [dscr("projscr%d" % s, [2064, NIN]) for s in range(2)]
    mixscr = [dscr("mixscr%d" % s, [16, 128, NTOK], BF16) for s in range(2)]
    x1scr = [dscr("x1scr%d" % s, [NTOK, D]) for s in range(2)]
    utscr = [dscr("utscr%d" % s, [17, 128, 64, 128], BF16) for s in range(2)]
    B_proj = [[Buf("proj%d_%d" % (s, t)) for t in range(17)] for s in range(2)]
    B_mix = [Buf("mix%d" % s) for s in range(2)]
    B_x1 = [[Buf("x1_%d_%d" % (s, t)) for t in range(17)] for s in range(2)]
    B_ut = [[Buf("ut_%d_%d" % (s, t)) for t in range(17)] for s in range(2)]
    B_in = Buf("inputs")
    B_out = Buf("outputs")

    P = Prog()
    es = ExitStack()
    with es:
        NW = 45056
        arena_t = es.enter_context(nc.sbuf_tensor("arena", [128, NW], F32))
        AR = Arena(arena_t, NW)
        psA = []
        for i in range(6):
            t_ = es.enter_context(nc.psum_tensor("psA%d" % i, [128, 512], F32))
            psA.append((t_, Buf("psA%d" % i)))
        psT = []
        for i in range(2):
            t_ = es.enter_context(nc.psum_tensor("psT%d" % i, [128, 8, 128], BF16))
            psT.append((t_, Buf("psT%d" % i)))
        rot = {"a": 0, "t": 0}

        def rotA():
            r = psA[rot["a"] % 6]
            rot["a"] += 1
            return r

        def rotT():
            r = psT[rot["t"] % 2]
            rot["t"] += 1
            return r

        cf, B_cf = AR.f32("constf", 392)
        cb, B_cb = AR.b16("constb", 256 + STRIPW + 288)
        identf = cf[:, 0:128]
        Umat = cf[:, 128:256]
        Lmat = cf[:, 256:384]
        onescol = cf[:, 384:385]
        identb = cb[:, 0:128]
        onesb = cb[:, 128:256]
        strip = cb[:, 256:256 + STRIPW]
        smask = cb[:, 256 + STRIPW:256 + STRIPW + 288]
        gq_bc, B_gq = AR.f32("gq", 128)
        gk_bc, B_gk = AR.f32("gk", 128)
        gg_bc, B_gg = AR.f32("gg", 1024)
        wga, B_wga = AR.f32("wga", 512)
        Sst, B_S = AR.f32("S", 1024, ("p (h v) -> p h v", dict(h=4)))
        Sb, B_Sb = AR.b16("Sb", 1024, ("p (h v) -> p h v", dict(h=4)))
        gaT, B_gaT = AR.f32("gaT", 128)
        gbc, B_gbc = AR.f32("gbc", 2048)
        stat, B_stat_unused = AR.f32("stat", 128)
        stat_bufs = [Buf("stat%d" % i) for i in range(8)]
        BIG, B_big = AR.b16("BIG", 16 * NTOK, ("p (k n) -> p k n", dict(k=16)))
        PERSIST_TOP = AR.top
        AR.track = True
        BIGAL = []
        B_big.alias = lambda: BIGAL
        B_big.keep = True

        def bigview(name, extra=()):
            b = Buf(name)
            b.alias = [B_big] + list(extra)
            b.keep = True
            BIGAL.append(b)
            return b

        P.dma(lambda e: e.dma_start(out=cf, in_=constf), B_cf, reads=[B_in], writes=[B_cf])
        P.dma(lambda e: e.dma_start(out=cb, in_=constb), B_cb, reads=[B_in], writes=[B_cb])
        P.dma(lambda e: e.dma_start(out=gq_bc, in_=qg[0, :].partition_broadcast(128)), B_gq, reads=[B_in], writes=[B_gq])
        P.dma(lambda e: e.dma_start(out=gk_bc, in_=kg[0, :].partition_broadcast(128)), B_gk, reads=[B_in], writes=[B_gk])
        P.dma(lambda e: e.dma_start(out=gg_bc, in_=gg[0, :].partition_broadcast(128)), B_gg, reads=[B_in], writes=[B_gg])
        P.dma(lambda e: e.dma_start(out=wga[0:16, :], in_=wg2), B_wga, reads=[B_in], writes=[B_wga])
        P.dma(lambda e: e.dma_start(out=wga[16:17, :], in_=bg), B_wga, reads=[B_in], writes=[B_wga])
        P.op("act", lambda e: e.mul(gq_bc, gq_bc, 128.0 ** -0.5), reads=[B_gq], writes=[B_gq])
        P.op("pool", lambda e: e.memset(gaT[:, :], 1.0), writes=[B_gaT])

        stat_i = [0]

        def newstat(n):
            i = stat_i[0] % 8
            stat_i[0] += 1
            return stat[:, i * 16:i * 16 + n], stat_bufs[i]

        def phase_norm(s, src_fn, src_bufs, gvec):
            AR.reset(PERSIST_TOP)
            xt = [AR.f32("xt%d" % i, 2048) for i in range(2)]
            junk, B_junk = AR.f32("junk", 2048)
            hb = [AR.b16("hb%d" % i, 2048) for i in range(2)]
            P.dma(lambda e: e.dma_start(out=gbc, in_=gvec[0, :].partition_broadcast(128)), B_gbc, reads=[B_in], writes=[B_gbc])
            tl = tiles_of(s)

            def S1(i):
                t, row0, nt = tl[i]
                x_, Bx = xt[i % 2]
                h_, Bh = hb[i % 2]
                P.dma(lambda e: e.dma_start(out=x_[0:nt, :], in_=src_fn(t, nt)), Bx, reads=[src_bufs[t]], writes=[Bx])
                ssq, Bq = newstat(1)
                rs, Br = newstat(1)
                P.op("act", lambda e: e.activation(out=junk[0:nt, :], in_=x_[0:nt, :], func=AF.Square, accum_out=ssq[0:nt, :]),
                     reads=[Bx], writes=[B_junk, Bq])
                P.op("act", lambda e: e.activation(out=rs[0:nt, :], in_=ssq[0:nt, :], func=AF.Sqrt, scale=1.0 / D, bias=EPS),
                     reads=[Bq], writes=[Br])
                P.op("dve", lambda e: e.reciprocal(out=rs[0:nt, :], in_=rs[0:nt, :]), reads=[Br], writes=[Br])
                P.op("dve", lambda e: e.scalar_tensor_tensor(out=h_[0:nt, :], in0=x_[0:nt, :], scalar=rs[0:nt, 0:1], in1=gbc[0:nt, :],
                                                             op0=ALU.mult, op1=ALU.mult), reads=[Bx, Br, B_gbc], writes=[Bh])

            def S2(i):
                t, row0, nt = tl[i]
                h_, Bh = hb[i % 2]
                for half in range(2):
                    pt, Bp = rotT()
                    for k in range(8):
                        kk = half * 8 + k
                        P.op("pe", lambda e, pt=pt, k=k, kk=kk: e.transpose(
                            out=pt[:, k, 0:nt], in_=h_[0:nt, kk * 128:(kk + 1) * 128], identity=identb[0:nt, 0:nt]),
                            reads=[Bh, B_cb], writes=[Bp])
                    if half == 0:
                        P.op("act", lambda e, pt=pt, half=half: e.activation(
                            out=BIG[:, half * 8:half * 8 + 8, row0:row0 + nt], in_=pt[:, :, 0:nt], func=AF.Copy),
                            reads=[Bp], writes=[B_big])
                    else:
                        P.op("dve", lambda e, pt=pt, half=half: e.tensor_copy(
                            out=BIG[:, half * 8:half * 8 + 8, row0:row0 + nt], in_=pt[:, :, 0:nt]),
                            reads=[Bp], writes=[B_big])

            S1(0)
            for i in range(len(tl)):
                if i + 1 < len(tl):
                    S1(i + 1)
                S2(i)
            P.barrier()

        def phase_gemm_tok(s, W, ncols_total, evac):
            AR.reset(PERSIST_TOP)
            wst = [AR.f32("wst%d" % i, 16 * 256, ("p (k n) -> p k n", dict(k=16)), top=True) for i in range(2)]
            wbf = [AR.b16("wbf%d" % i, 16 * 256, ("p (k n) -> p k n", dict(k=16)), top=True) for i in range(2)]
            ctx = {"AR": AR}
            Wpk = W.rearrange("(k p) n -> p k n", p=128)
            groups = []
            c0 = 0
            while c0 < ncols_total:
                nc_ = min(256, ncols_total - c0)
                groups.append((c0, nc_))
                c0 += nc_

            def wload(gi):
                c0, ncol = groups[gi]
                ws, Bws = wst[gi % 2]
                wb, Bwb = wbf[gi % 2]
                P.dma(lambda e: e.dma_start(out=ws[:, :, 0:ncol], in_=Wpk[:, :, c0:c0 + ncol]), Bws, reads=[B_in], writes=[Bws])
                P.op("dve", lambda e: e.tensor_copy(out=wb[:, 0:8, 0:ncol], in_=ws[:, 0:8, 0:ncol]), reads=[Bws], writes=[Bwb])
                P.op("act", lambda e: e.activation(out=wb[:, 8:16, 0:ncol], in_=ws[:, 8:16, 0:ncol], func=AF.Copy), reads=[Bws], writes=[Bwb])

            import os
            if os.environ.get("KGROUPS"):
                a_, b_ = [int(v) for v in os.environ["KGROUPS"].split(",")]
                groups = groups[a_:b_]
            st = evac("init", ctx)
            wload(0)
            steps = [(gi, c0, ncol, t, row0, nt) for gi, (c0, ncol) in enumerate(groups) for (t, row0, nt) in tiles_of(s)]
            PF = 2
            for j in range(min(PF, len(steps))):
                evac("pre", ctx, st, *steps[j][3:6], *steps[j][1:3])
            for j, (gi, c0, ncol, t, row0, nt) in enumerate(steps):
                if t == 0 and gi + 1 < len(groups):
                    wload(gi + 1)
                if j + PF < len(steps):
                    evac("pre", ctx, st, *steps[j + PF][3:6], *steps[j + PF][1:3])
                wb, Bwb = wbf[gi % 2]
                ps, Bps = rotA()
                for k in range(16):
                    P.op("pe", lambda e, ps=ps, k=k, row0=row0, nt=nt, wb=wb, ncol=ncol: e.matmul(
                        ps[0:nt, 0:ncol], lhsT=BIG[:, k, row0:row0 + nt], rhs=wb[:, k, 0:ncol],
                        start=(k == 0), stop=(k == 15)), reads=[B_big, Bwb], writes=[Bps])
                evac("tile", ctx, st, t, row0, nt, c0, ncol, ps, Bps)
            P.barrier()

        def evac_A1(s):
            cnt = [0]

            def f(kind, ctx, st=None, t=0, row0=0, nt=0, c0=0, ncol=0, ps=None, Bps=None):
                if kind == "init":
                    return [ctx["AR"].f32("ost%d" % i, 256) for i in range(4)]
                if kind == "pre":
                    return
                o_, Bo = st[cnt[0] % 4]
                eng = "act" if cnt[0] % 2 == 0 else "dve"
                cnt[0] += 1
                if eng == "act":
                    P.op("act", lambda e: e.activation(out=o_[0:nt, 0:ncol], in_=ps[0:nt, 0:ncol], func=AF.Copy),
                         reads=[Bps], writes=[Bo])
                else:
                    P.op("dve", lambda e: e.tensor_copy(out=o_[0:nt, 0:ncol], in_=ps[0:nt, 0:ncol]), reads=[Bps], writes=[Bo])
                P.dma(lambda e: e.dma_start(out=projscr[s][row0:row0 + nt, c0:c0 + ncol], in_=o_[0:nt, 0:ncol]), Bo,
                      reads=[Bo], writes=[B_proj[s][t]])
            return f

        def xrows(s, t, nt):
            if t < 16:
                return xp[s, t * 128:t * 128 + nt, :]
            return xs[0:nt, :]

        def evac_C(s):
            cnt = [0]
            pcnt = [0]

            def f(kind, ctx, st=None, t=0, row0=0, nt=0, c0=0, ncol=0, ps=None, Bps=None):
                if kind == "init":
                    a = [ctx["AR"].f32("ost%d" % i, 256) for i in range(4)]
                    b = [ctx["AR"].f32("xsl%d" % i, 256) for i in range(4)]
                    return (a, b)
                if kind == "pre":
                    x_, Bx = st[1][pcnt[0] % 4]
                    pcnt[0] += 1
                    P.dma(lambda e: e.dma_start(out=x_[0:nt, 0:ncol], in_=xrows(s, t, nt)[:, c0:c0 + ncol]), Bx,
                          reads=[B_in], writes=[Bx])
                    return
                o_, Bo = st[0][cnt[0] % 4]
                x_, Bx = st[1][cnt[0] % 4]
                cnt[0] += 1
                P.op("dve", lambda e: e.tensor_tensor(out=o_[0:nt, 0:ncol], in0=ps[0:nt, 0:ncol], in1=x_[0:nt, 0:ncol], op=ALU.add),
                     reads=[Bps, Bx], writes=[Bo])
                P.dma(lambda e: e.dma_start(out=x1scr[s][row0:row0 + nt, c0:c0 + ncol], in_=o_[0:nt, 0:ncol]), Bo,
                      reads=[Bo], writes=[B_x1[s][t]])
            return f

        def qk_prep(pj, Bpj, nt, junk, B_junk, qh, Bqh, kf, Bkf, kh, Bkh):
            P.op("act", lambda e: e.activation(out=junk[0:nt, 0:2048], in_=pj[0:nt, 0:2048], func=AF.Square),
                 reads=[Bpj], writes=[B_junk])
            ssq, Bq = newstat(16)
            P.op("dve", lambda e: e.tensor_reduce(out=ssq[0:nt, :], in_=junk[0:nt, 0:2048].rearrange("p (h d) -> p h d", h=16),
                                                  axis=AX.X, op=ALU.add), reads=[B_junk], writes=[Bq])
            P.op("act", lambda e: e.activation(out=ssq[0:nt, :], in_=ssq[0:nt, :], func=AF.Ln, scale=1.0 / 128, bias=EPS),
                 reads=[Bq], writes=[Bq])
            P.op("act", lambda e: e.activation(out=ssq[0:nt, :], in_=ssq[0:nt, :], func=AF.Exp, scale=-0.5), reads=[Bq], writes=[Bq])
            for h in range(8):
                P.op("dve", lambda e, h=h: e.scalar_tensor_tensor(
                    out=qh[0:nt, h * 128:(h + 1) * 128], in0=pj[0:nt, h * 128:(h + 1) * 128], scalar=ssq[0:nt, h:h + 1],
                    in1=gq_bc[0:nt, :], op0=ALU.mult, op1=ALU.mult), reads=[Bpj, Bq, B_gq], writes=[Bqh])
            for h in range(8):
                P.op("dve", lambda e, h=h: e.scalar_tensor_tensor(
                    out=kf[0:nt, h * 128:(h + 1) * 128], in0=pj[0:nt, 1024 + h * 128:1024 + (h + 1) * 128],
                    scalar=ssq[0:nt, 8 + h:9 + h], in1=gk_bc[0:nt, :], op0=ALU.mult, op1=ALU.mult),
                    reads=[Bpj, Bq, B_gk], writes=[Bkf])
            P.op("dve", lambda e: e.tensor_copy(out=kh[0:nt, :], in_=kf[0:nt, :]), reads=[Bkf], writes=[Bkh])

        def phase_B1(s):
            AR.reset(PERSIST_TOP)
            KT = BIG[:, 0:8, 0:2048]
            pj = [AR.f32("pjqk%d" % i, 3072) for i in range(2)]
            junk, B_junk = AR.f32("junk", 2048)
            qh, Bqh = AR.b16("qh", 1024)
            kh, Bkh = AR.b16("kh", 1024)
            kf, Bkf = AR.f32("kf", 1024)
            QT, B_QT = AR.b16("QT", 8 * 512, ("p (h n) -> p h n", dict(h=8)))
            Eb = [AR.b16("E%d" % i, 512) for i in range(2)]
            Pb = [AR.b16("P%d" % i, 512) for i in range(2)]
            lnz, B_lnz = AR.f32("lnz", 512)
            mixA = [AR.b16("mixA%d" % i, 512) for i in range(2)]
            B_KT = [bigview("KT%d" % t) for t in range(16)]
            B_V = [bigview("V%d" % t) for t in range(16)]

            def Vtile(t):
                return BIG[:, 8 + t // 2, (t % 2) * 1024:(t % 2) * 1024 + 1024]

            hcount = [0]
            QT2, B_QT2 = AR.b16("QT2", 8 * 512, ("p (h n) -> p h n", dict(h=8)))
            QTs = [(QT, B_QT), (QT2, B_QT2)]

            def pjload(t):
                pj_, Bpj = pj[t % 2]
                P.dma(lambda e: e.dma_start(out=pj_[:, :], in_=projscr[s][t * 128:t * 128 + 128, 0:3072]), Bpj,
                      reads=[B_proj[s][t]], writes=[Bpj])

            def prep(t, part=3):
                ts = t % 4
                row0 = t * 128
                QT_, BQT_ = QTs[(t // 4) % 2]
                pj_, Bpj = pj[t % 2]
                if part & 1:
                    prep1(t, row0, pj_, Bpj)
                if part & 2:
                    prep2(t, ts, row0, QT_, BQT_)

            def prep1(t, row0, pj_, Bpj):
                if t + 1 < 16:
                    pjload(t + 1)
                qk_prep(pj_, Bpj, 128, junk, B_junk, qh, Bqh, kf, Bkf, kh, Bkh)
                P.dma(lambda e: e.dma_start(out=pk[s, row0:row0 + 128, :], in_=kf[:, :]), Bkf, reads=[Bkf], writes=[B_out])
                P.dma(lambda e: e.dma_start(out=pv[s, row0:row0 + 128, :], in_=pj_[:, 2048:3072]), Bpj, reads=[Bpj], writes=[B_out])
                P.op("pool", lambda e: e.tensor_copy(out=Vtile(t), in_=pj_[:, 2048:3072]), reads=[Bpj], writes=[B_V[t]])

            def prep2(t, ts, row0, QT_, BQT_):
                pt, Bp = rotT()
                for h in range(8):
                    P.op("pe", lambda e, pt=pt, h=h: e.transpose(out=pt[:, h, :], in_=qh[:, h * 128:(h + 1) * 128], identity=identb),
                         reads=[Bqh, B_cb], writes=[Bp])
                P.op("act", lambda e, pt=pt: e.activation(out=QT_[:, :, ts * 128:(ts + 1) * 128], in_=pt[:, :, :], func=AF.Copy),
                     reads=[Bp], writes=[BQT_])
                pt2, Bp2 = rotT()
                for h in range(8):
                    P.op("pe", lambda e, pt2=pt2, h=h: e.transpose(out=pt2[:, h, :], in_=kh[:, h * 128:(h + 1) * 128], identity=identb),
                         reads=[Bkh, B_cb], writes=[Bp2])
                P.op("dve", lambda e, pt2=pt2: e.tensor_copy(out=KT[:, :, row0:row0 + 128], in_=pt2[:, :, :]),
                     reads=[Bp2], writes=[B_KT[t]])

            Eb4 = Eb + [AR.b16("E%d" % i, 512) for i in range(2, 4)]
            Pb4 = Pb + [AR.b16("P%d" % i, 512) for i in range(2, 4)]
            lnz2 = [(lnz, B_lnz), AR.f32("lnzB", 512)]
            sTbanks = [[psA[4], psA[5]],
                       [(psT[0][0].rearrange("p a b -> p (a b)").bitcast(F32), psT[0][1]),
                        (psT[1][0].rearrange("p a b -> p (a b)").bitcast(F32), psT[1][1])]]

            def attention_pair(b, hA, hB):
                QT_, BQT_ = QTs[b % 2]
                nkt = 4 * b + 4
                heads = (hA, hB)
                oTs = [psA[0], psA[2]]
                zbs = [psA[1], psA[3]]

                def geom(kt):
                    a = kt - 4 * b
                    q0 = max(0, a) * 128
                    return q0, 512 - q0

                def emit_sT(w, kt):
                    h = heads[w]
                    q0, nq = geom(kt)
                    sT, BsT = sTbanks[w][kt % 2]
                    P.op("pe", lambda e: e.matmul(sT[:, 0:nq], lhsT=KT[:, h, kt * 128:(kt + 1) * 128], rhs=QT_[:, h, q0:512],
                                                  start=True, stop=True), reads=[B_KT[kt], BQT_], writes=[BsT])

                def emit_soft(w, kt):
                    q0, nq = geom(kt)
                    sT, BsT = sTbanks[w][kt % 2]
                    E_, BE = Eb4[2 * w + kt % 2]
                    P_, BP = Pb4[2 * w + kt % 2]
                    soff = b * 512 - kt * 128 + STRIP0 + q0
                    P.op("act", lambda e: e.activation(out=E_[:, 0:nq], in_=sT[:, 0:nq], func=AF.Exp), reads=[BsT], writes=[BE])
                    P.op("dve", lambda e: e.tensor_tensor(out=P_[:, 0:nq], in0=E_[:, 0:nq], in1=strip[:, soff:soff + nq], op=ALU.mult),
                         reads=[BE, B_cb], writes=[BP])

                def emit_pv(w, kt):
                    h = heads[w]
                    q0, nq = geom(kt)
                    P_, BP = Pb4[2 * w + kt % 2]
                    oT, BoT = oTs[w]
                    zb, Bzb = zbs[w]
                    P.op("pe", lambda e: e.matmul(oT[:, q0:512], lhsT=Vtile(kt)[:, h * 128:(h + 1) * 128], rhs=P_[:, 0:nq],
                                                  start=(kt == 0), stop=(kt == nkt - 1)), reads=[B_V[kt], BP], writes=[BoT])
                    P.op("pe", lambda e: e.matmul(zb[:, q0:512], lhsT=onesb, rhs=P_[:, 0:nq], start=(kt == 0), stop=(kt == nkt - 1)),
                         reads=[B_cb, BP], writes=[Bzb])

                emit_sT(0, 0)
                emit_sT(1, 0)
                for kt in range(nkt):
                    if kt + 1 < nkt:
                        emit_sT(0, kt + 1)
                        emit_sT(1, kt + 1)
                    emit_soft(0, kt)
                    emit_soft(1, kt)
                    emit_pv(0, kt)
                    emit_pv(1, kt)
                for w in range(2):
                    h = heads[w]
                    oT, BoT = oTs[w]
                    zb, Bzb = zbs[w]
                    lz, Blz = lnz2[w]
                    m_, Bm = mixA[w]

                    def tail(h=h, oT=oT, BoT=BoT, zb=zb, Bzb=Bzb, lz=lz, Blz=Blz, m_=m_, Bm=Bm):
                        P.op("act", lambda e: e.activation(out=lz[:, :], in_=zb[:, :], func=AF.Ln), reads=[Bzb], writes=[Blz])
                        P.op("act", lambda e: e.activation(out=lz[:, :], in_=lz[:, :], func=AF.Exp, scale=-1.0), reads=[Blz], writes=[Blz])
                        P.op("dve", lambda e: e.tensor_tensor(out=m_[:, :], in0=oT[:, :], in1=lz[:, :], op=ALU.mult),
                             reads=[BoT, Blz], writes=[Bm])
                        P.dma(lambda e: e.dma_start(out=mixscr[s][h, :, b * 512:(b + 1) * 512], in_=m_[:, :]), Bm,
                              reads=[Bm], writes=[B_mix[s]])
                    tail()

            pjload(0)
            for t in range(4):
                prep(t)
            for b in range(4):
                for hp in range(4):
                    if b < 3:
                        prep(4 * (b + 1) + hp, 1)
                    attention_pair(b, 2 * hp, 2 * hp + 1)
                    if b < 3:
                        prep(4 * (b + 1) + hp, 2)
            P.barrier()
            if s != 1:
                return
            def _sample_part():
                AR.reset(PERSIST_TOP)
                KTs = BIG[:, 0:8, 0:128 * 9].rearrange("p h (t n) -> p h t n", t=9)
                pjs, Bpjs = AR.f32("pjs", 3072)
                junk, B_junk = AR.f32("junk", 2048)
                qh, Bqh = AR.b16("qh", 1024)
                kh, Bkh = AR.b16("kh", 1024)
                kf, Bkf = AR.f32("kf", 1024)
                qTs, BqTs = AR.b16("qTs", 32)
                ckt = [AR.f32("ckt%d" % i, 1024) for i in range(2)]
                ckb = [AR.b16("ckb%d" % i, 1024) for i in range(2)]
                Vs = [AR.b16("Vs%d" % i, 1024) for i in range(9)]
                cvt = [AR.f32("cvt%d" % i, 1024) for i in range(2)]
                Es, BEs = AR.b16("Es", 32)
                Ps = [AR.b16("Ps%d" % i, 32) for i in range(9)]
                rzs, Brzs = AR.f32("rzs", 32)
                mAs, BmAs = AR.b16("mAs", 32)
                B_KTs = [bigview("KTs%d" % i, B_KT) for i in range(9)]
                for bi in range(4):
                    r0 = 2048 + 4 * bi
                    P.dma(lambda e, r0=r0: e.dma_start(out=pjs[0:4, :], in_=projscr[1][r0:r0 + 4, 0:3072]), Bpjs,
                          reads=[B_proj[1][16]], writes=[Bpjs])
                    qk_prep(pjs, Bpjs, 4, junk, B_junk, qh, Bqh, kf, Bkf, kh, Bkh)
                    P.dma(lambda e, bi=bi: e.dma_start(out=sk[4 * bi:4 * bi + 4, :], in_=kf[0:4, :]), Bkf, reads=[Bkf], writes=[B_out])
                    P.dma(lambda e, bi=bi: e.dma_start(out=sv[4 * bi:4 * bi + 4, :], in_=pjs[0:4, 2048:3072]), Bpjs, reads=[Bpjs], writes=[B_out])
                    pt, Bp = rotT()
                    for h in range(8):
                        P.op("pe", lambda e, pt=pt, h=h: e.transpose(out=pt[:, h, 0:4], in_=qh[0:4, h * 128:(h + 1) * 128], identity=identb[0:4, 0:4]),
                             reads=[Bqh, B_cb], writes=[Bp])
                    P.op("act", lambda e, pt=pt: e.activation(out=qTs.rearrange("p (h t) -> p h t", h=8), in_=pt[:, :, 0:4], func=AF.Copy),
                         reads=[Bp], writes=[BqTs])
                    for i in range(9):
                        nk = 128 if i < 8 else 4
                        V_, BV = Vs[i]
                        if i < 8:
                            c_, Bc = ckt[i % 2]
                            cb_, Bcb_ = ckb[i % 2]
                            v_, Bv = cvt[i % 2]
                            if i < 4:
                                srck = ck[bi, 1536 + 128 * i:1536 + 128 * (i + 1), :]
                                srcv = cv[bi, 1536 + 128 * i:1536 + 128 * (i + 1), :]
                            else:
                                r = i - 4
                                srck = ck[bi].rearrange("(m r) c -> r m c", r=16)[r]
                                srcv = cv[bi].rearrange("(m r) c -> r m c", r=16)[r]
                            P.dma(lambda e, c_=c_, srck=srck: e.dma_start(out=c_[:, :], in_=srck), Bc, reads=[B_in], writes=[Bc])
                            P.dma(lambda e, v_=v_, srcv=srcv: e.dma_start(out=v_[:, :], in_=srcv), Bv, reads=[B_in], writes=[Bv])
                            P.op("dve", lambda e, c_=c_, cb_=cb_: e.tensor_copy(out=cb_[:, :], in_=c_[:, :]), reads=[Bc], writes=[Bcb_])
                            P.op("act", lambda e, v_=v_, V_=V_: e.activation(out=V_[:, :], in_=v_[:, :], func=AF.Copy), reads=[Bv], writes=[BV])
                            ksrc, Bks = cb_, Bcb_
                        else:
                            P.op("pool", lambda e, V_=V_: e.tensor_copy(out=V_[0:4, :], in_=pjs[0:4, 2048:3072]), reads=[Bpjs], writes=[BV])
                            ksrc, Bks = kh, Bkh
                        pt, Bp = rotT()
                        for h in range(8):
                            P.op("pe", lambda e, pt=pt, h=h, ksrc=ksrc, nk=nk: e.transpose(
                                out=pt[:, h, 0:nk], in_=ksrc[0:nk, h * 128:(h + 1) * 128], identity=identb[0:nk, 0:nk]),
                                reads=[Bks, B_cb], writes=[Bp])
                        P.op("dve", lambda e, pt=pt, i=i, nk=nk: e.tensor_copy(out=KTs[:, :, i, 0:nk], in_=pt[:, :, 0:nk]),
                             reads=[Bp], writes=[B_KTs[i]])
                        sT, BsT = psA[4 + i % 2]
                        for h in range(8):
                            P.op("pe", lambda e, sT=sT, h=h, i=i, nk=nk: e.matmul(
                                sT[0:nk, h * 4:h * 4 + 4], lhsT=KTs[:, h, i, 0:nk], rhs=qTs[:, h * 4:h * 4 + 4], start=True, stop=True),
                                reads=[B_KTs[i], BqTs], writes=[BsT])
                        P_, BP = Ps[i]
                        P.op("act", lambda e, sT=sT, nk=nk: e.activation(out=Es[0:nk, :], in_=sT[0:nk, 0:32], func=AF.Exp), reads=[BsT], writes=[BEs])
                        P.op("dve", lambda e, P_=P_, i=i, nk=nk: e.tensor_tensor(out=P_[0:nk, :], in0=Es[0:nk, :], in1=smask[0:nk, i * 32:(i + 1) * 32],
                                                                              op=ALU.mult), reads=[BEs, B_cb], writes=[BP])
                    oT, BoT = psA[bi % 2]
                    zb, Bzb = psA[2 + bi % 2]
                    for h in range(8):
                        for i in range(9):
                            nk = 128 if i < 8 else 4
                            P.op("pe", lambda e, oT=oT, h=h, i=i, nk=nk: e.matmul(
                                oT[:, h * 4:h * 4 + 4], lhsT=Vs[i][0][0:nk, h * 128:(h + 1) * 128], rhs=Ps[i][0][0:nk, h * 4:h * 4 + 4],
                                start=(i == 0), stop=(i == 8)), reads=[Vs[i][1], Ps[i][1]], writes=[BoT])
                    for i in range(9):
                        nk = 128 if i < 8 else 4
                        P.op("pe", lambda e, zb=zb, i=i, nk=nk: e.matmul(zb[:, 0:32], lhsT=onesb[0:nk, :], rhs=Ps[i][0][0:nk, :],
                                                                        start=(i == 0), stop=(i == 8)), reads=[B_cb, Ps[i][1]], writes=[Bzb])
                    P.op("dve", lambda e, zb=zb: e.reciprocal(out=rzs[:, :], in_=zb[:, 0:32]), reads=[Bzb], writes=[Brzs])
                    P.op("dve", lambda e, oT=oT: e.tensor_tensor(out=mAs[:, :], in0=oT[:, 0:32], in1=rzs[:, :], op=ALU.mult),
                         reads=[BoT, Brzs], writes=[BmAs])
                    P.dma(lambda e, r0=r0: e.dma_start(out=mixscr[1][0:8, :, r0:r0 + 4].rearrange("h p t -> p h t"),
                                                      in_=mAs.rearrange("p (h t) -> p h t", h=8)), BmAs, reads=[BmAs], writes=[B_mix[1]])
                P.barrier()

            _sample_part()

        def phase_B2(s):
            AR.reset(PERSIST_TOP)
            pjg = [AR.f32("pjg%d" % i, 3088) for i in range(2)]
            lbuf, Bl = AR.f32("lbuf", 512)
            Ebuf = [AR.f32("Eb%d" % i, 512) for i in range(2)]
            qd, Bqd = AR.b16("qd", 512)
            ki, Bki = AR.b16("ki", 512)
            ke, Bke = AR.b16("ke", 512)
            vb16, Bvb = AR.b16("vb16", 1024)
            sr, Bsr = AR.f32("sr", 1024)
            ob, Bob = AR.f32("ob", 1024)
            junk, B_junk = AR.f32("junkg", 1024)
            mixb, Bmixb = AR.b16("mixb", 1024)
            qdT, BqdT = AR.b16("qdT", 512, ("p (h n) -> p h n", dict(h=4)))
            kiT, BkiT = AR.b16("kiT", 512, ("p (h n) -> p h n", dict(h=4)))
            attm = [AR.b16("attm%d" % i, 128) for i in range(4)]
            dec, Bdec = AR.f32("dec", 8)
            mixBT, BmixBT = AR.b16("mixBT", 8 * 512, ("p (c n) -> p c n", dict(c=8)))

            import os
            KSTEP = int(os.environ.get("KGLA_STEP", "99"))
            KTILES = int(os.environ.get("KGLA_TILES", "16"))

            def gla_tile(pj_, Bpj, nt, colslot):
                ps, Bps = psA[0]
                P.op("pe", lambda e: e.transpose(out=ps[0:16, 0:nt], in_=pj_[0:nt, 3072:3088], identity=identf[0:nt, 0:nt]),
                     reads=[Bpj, B_cf], writes=[Bps])
                P.op("dve", lambda e: e.tensor_copy(out=gaT[0:16, 0:nt], in_=ps[0:16, 0:nt]), reads=[Bps], writes=[B_gaT])
                pz, Bpz = psA[1]
                P.op("pe", lambda e: e.matmul(pz[0:nt, :], lhsT=gaT[0:17, 0:nt], rhs=wga[0:17, :], start=True, stop=True),
                     reads=[B_gaT, B_wga], writes=[Bpz])
                P.op("act", lambda e: e.activation(out=lbuf[0:nt, :], in_=pz[0:nt, :], func=AF.Exp, scale=-1.0), reads=[Bpz], writes=[Bl])
                P.op("act", lambda e: e.activation(out=lbuf[0:nt, :], in_=lbuf[0:nt, :], func=AF.Ln, bias=1.0), reads=[Bl], writes=[Bl])
                if KSTEP <= 1:
                    return
                pc, Bpc = psA[2]
                P.op("pe", lambda e: e.matmul(pc[0:nt, :], lhsT=Umat[0:nt, 0:nt], rhs=lbuf[0:nt, :], start=True, stop=True),
                     reads=[B_cf, Bl], writes=[Bpc])
                pr, Bpr = psA[3]
                P.op("pe", lambda e: e.matmul(pr[0:nt, :], lhsT=Lmat[0:nt, 0:nt], rhs=lbuf[0:nt, :], start=True, stop=True),
                     reads=[B_cf, Bl], writes=[Bpr])
                ptt, Bptt = psA[0]
                for h in range(4):
                    P.op("pe", lambda e, h=h: e.matmul(ptt[:, h:h + 1], lhsT=lbuf[0:nt, h * 128:(h + 1) * 128], rhs=onescol[0:nt, :],
                                                       start=True, stop=True), reads=[Bl, B_cf], writes=[Bptt])
                if KSTEP <= 2:
                    return
                E0, BE0 = Ebuf[0]
                E1, BE1 = Ebuf[1]
                P.op("act", lambda e: e.activation(out=E0[0:nt, :], in_=pc[0:nt, :], func=AF.Exp, scale=-1.0 / 16), reads=[Bpc], writes=[BE0])
                P.op("dve", lambda e: e.scalar_tensor_tensor(out=qd[0:nt, :], in0=pj_[0:nt, 0:512], scalar=128.0 ** -0.5, in1=E0[0:nt, :],
                                                             op0=ALU.mult, op1=ALU.mult), reads=[Bpj, BE0], writes=[Bqd])
                P.op("act", lambda e: e.activation(out=E1[0:nt, :], in_=pc[0:nt, :], func=AF.Exp, scale=1.0 / 16), reads=[Bpc], writes=[BE1])
                P.op("dve", lambda e: e.tensor_tensor(out=ki[0:nt, :], in0=pj_[0:nt, 512:1024], in1=E1[0:nt, :], op=ALU.mult),
                     reads=[Bpj, BE1], writes=[Bki])
                P.op("act", lambda e: e.activation(out=E0[0:nt, :], in_=pr[0:nt, :], func=AF.Exp, scale=-1.0 / 16), reads=[Bpr], writes=[BE0])
                P.op("dve", lambda e: e.tensor_tensor(out=ke[0:nt, :], in0=pj_[0:nt, 512:1024], in1=E0[0:nt, :], op=ALU.mult),
                     reads=[Bpj, BE0], writes=[Bke])
                P.op("act", lambda e: e.activation(out=dec[:, 0:4], in_=ptt[:, 0:4],
                                                   func=AF.Exp, scale=-1.0 / 16), reads=[Bptt], writes=[Bdec])
                if KSTEP <= 3:
                    return
                P.op("dve", lambda e: e.tensor_copy(out=vb16[0:nt, :], in_=pj_[0:nt, 1024:2048]), reads=[Bpj], writes=[Bvb])
                P.op("act", lambda e: e.activation(out=sr[0:nt, :], in_=pj_[0:nt, 2048:3072], func=AF.Silu), reads=[Bpj], writes=[Bsr])
                if KSTEP <= 4:
                    return
                pt1, Bp1 = rotT()
                for h in range(4):
                    P.op("pe", lambda e, h=h: e.transpose(out=pt1[:, h, 0:nt], in_=qd[0:nt, h * 128:(h + 1) * 128], identity=identb[0:nt, 0:nt]),
                         reads=[Bqd, B_cb], writes=[Bp1])
                for h in range(4):
                    P.op("pe", lambda e, h=h: e.transpose(out=pt1[:, 4 + h, 0:nt], in_=ki[0:nt, h * 128:(h + 1) * 128], identity=identb[0:nt, 0:nt]),
                         reads=[Bki, B_cb], writes=[Bp1])
                P.op("act", lambda e: e.activation(out=qdT[:, :, 0:nt], in_=pt1[:, 0:4, 0:nt], func=AF.Copy), reads=[Bp1], writes=[BqdT])
                P.op("dve", lambda e: e.tensor_copy(out=kiT[:, :, 0:nt], in_=pt1[:, 4:8, 0:nt]), reads=[Bp1], writes=[BkiT])
                if KSTEP <= 5:
                    return
                po = [psA[4], psA[5]]
                for h in range(4):
                    pa, Bpa = psA[h % 2]
                    am, Bam = attm[h]
                    P.op("pe", lambda e, h=h, pa=pa: e.matmul(pa[0:nt, 0:nt], lhsT=kiT[:, h, 0:nt], rhs=qdT[:, h, 0:nt], start=True, stop=True),
                         reads=[BkiT, BqdT], writes=[Bpa])
                    P.op("dve", lambda e, pa=pa, am=am: e.tensor_tensor(out=am[0:nt, 0:nt], in0=pa[0:nt, 0:nt], in1=Umat[0:nt, 0:nt], op=ALU.mult),
                         reads=[Bpa, B_cf], writes=[Bam])
                    o_, Bo = po[h // 2]
                    c0 = (h % 2) * 256
                    P.op("pe", lambda e, h=h, o_=o_, c0=c0, am=am: e.matmul(o_[0:nt, c0:c0 + 256], lhsT=am[0:nt, 0:nt],
                                                                           rhs=vb16[0:nt, h * 256:(h + 1) * 256], start=True, stop=False),
                         reads=[Bam, Bvb], writes=[Bo])
                    P.op("pe", lambda e, h=h, o_=o_, c0=c0: e.matmul(o_[0:nt, c0:c0 + 256], lhsT=qdT[:, h, 0:nt], rhs=Sb[:, h, :],
                                                                    start=False, stop=True), reads=[BqdT, B_Sb], writes=[Bo])
                    pd, Bpd = psA[2 + h % 2]
                    P.op("pe", lambda e, h=h, pd=pd: e.matmul(pd[:, 0:256], lhsT=ke[0:nt, h * 128:(h + 1) * 128],
                                                              rhs=vb16[0:nt, h * 256:(h + 1) * 256], start=True, stop=True),
                         reads=[Bke, Bvb], writes=[Bpd])
                    P.op("dve", lambda e, h=h, pd=pd: e.scalar_tensor_tensor(out=Sst[:, h, :], in0=Sst[:, h, :], scalar=dec[:, h:h + 1],
                                                                            in1=pd[:, 0:256], op0=ALU.mult, op1=ALU.add),
                         reads=[B_S, Bdec, Bpd], writes=[B_S])
                    P.op("act", lambda e, h=h: e.activation(out=Sb[:, h, :], in_=Sst[:, h, :], func=AF.Copy), reads=[B_S], writes=[B_Sb])
                if KSTEP <= 6:
                    return
                for j in range(2):
                    o_, Bo = po[j]
                    P.op("act", lambda e, j=j, o_=o_: e.activation(out=ob[0:nt, j * 512:(j + 1) * 512], in_=o_[0:nt, :], func=AF.Copy),
                         reads=[Bo], writes=[Bob])
                P.op("dve", lambda e: e.tensor_tensor(out=junk[0:nt, :], in0=ob[0:nt, :], in1=ob[0:nt, :], op=ALU.mult), reads=[Bob], writes=[B_junk])
                ssq, Bq = newstat(4)
                P.op("dve", lambda e: e.tensor_reduce(out=ssq[0:nt, :], in_=junk[0:nt, :].rearrange("p (h d) -> p h d", h=4), axis=AX.X, op=ALU.add),
                     reads=[B_junk], writes=[Bq])
                P.op("act", lambda e: e.activation(out=ssq[0:nt, :], in_=ssq[0:nt, :], func=AF.Ln, scale=1.0 / 256, bias=EPS), reads=[Bq], writes=[Bq])
                P.op("act", lambda e: e.activation(out=ssq[0:nt, :], in_=ssq[0:nt, :], func=AF.Exp, scale=-0.5), reads=[Bq], writes=[Bq])
                for h in range(4):
                    P.op("dve", lambda e, h=h: e.scalar_tensor_tensor(out=junk[0:nt, h * 256:(h + 1) * 256], in0=ob[0:nt, h * 256:(h + 1) * 256],
                                                                      scalar=ssq[0:nt, h:h + 1], in1=gg_bc[0:nt, h * 256:(h + 1) * 256],
                                                                      op0=ALU.mult, op1=ALU.mult), reads=[Bob, Bq, B_gg, B_junk], writes=[B_junk])
                P.op("dve", lambda e: e.tensor_tensor(out=mixb[0:nt, :], in0=junk[0:nt, :], in1=sr[0:nt, :], op=ALU.mult),
                     reads=[B_junk, Bsr], writes=[Bmixb])
                if KSTEP <= 7:
                    return
                pt, Bp = rotT()
                for j in range(8):
                    P.op("pe", lambda e, j=j, pt=pt: e.transpose(out=pt[:, j, 0:nt], in_=mixb[0:nt, j * 128:(j + 1) * 128], identity=identb[0:nt, 0:nt]),
                         reads=[Bmixb, B_cb], writes=[Bp])
                P.op("act", lambda e, pt=pt: e.activation(out=mixBT[:, :, colslot:colslot + nt], in_=pt[:, :, 0:nt], func=AF.Copy),
                     reads=[Bp], writes=[BmixBT])

            P.op("pool", lambda e: e.memset(Sst[:, :, :], 0.0), writes=[B_S])
            P.op("pool", lambda e: e.memset(Sb[:, :, :], 0.0), writes=[B_Sb])
            def pjgload(t):
                pj_, Bpj = pjg[t % 2]
                P.dma(lambda e: e.dma_start(out=pj_[:, :], in_=projscr[s][t * 128:t * 128 + 128, 3072:NIN]), Bpj,
                      reads=[B_proj[s][t]], writes=[Bpj])

            pjgload(0)
            for t in range(KTILES):
                pj_, Bpj = pjg[t % 2]
                row0 = t * 128
                if t + 1 < KTILES:
                    pjgload(t + 1)
                gla_tile(pj_, Bpj, 128, (t % 4) * 128)
                if t % 4 == 3:
                    b = t // 4
                    P.dma(lambda e, b=b: e.dma_start(out=mixscr[s][8:16, :, b * 512:(b + 1) * 512].rearrange("c p n -> p c n"),
                                                    in_=mixBT[:, :, :]), BmixBT, reads=[BmixBT], writes=[B_mix[s]])
            P.dma(lambda e: e.dma_start(out=pst[s].rearrange("h p v -> p h v"), in_=Sst[:, :, :]), B_S, reads=[B_S], writes=[B_out])
            if s == 1:
                for bi in range(4):
                    pj_, Bpj = pjg[bi % 2]
                    r0 = 2048 + 4 * bi
                    P.dma(lambda e, bi=bi: e.dma_start(out=Sst[:, :, :], in_=sg[bi].rearrange("h p v -> p h v")), B_S, reads=[B_in], writes=[B_S])
                    P.op("pool", lambda e: e.tensor_copy(out=Sb[:, :, :], in_=Sst[:, :, :]), reads=[B_S], writes=[B_Sb])
                    P.dma(lambda e, pj_=pj_, r0=r0: e.dma_start(out=pj_[0:4, :], in_=projscr[1][r0:r0 + 4, 3072:NIN]), Bpj,
                          reads=[B_proj[1][16]], writes=[Bpj])
                    gla_tile(pj_, Bpj, 4, 4 * bi)
                    P.dma(lambda e, bi=bi: e.dma_start(out=sst[bi].rearrange("h p v -> p h v"), in_=Sst[:, :, :]), B_S, reads=[B_S], writes=[B_out])
                P.dma(lambda e: e.dma_start(out=mixscr[1][8:16, :, 2048:2064].rearrange("c p n -> p c n"), in_=mixBT[:, :, 0:16]), BmixBT,
                      reads=[BmixBT], writes=[B_mix[1]])
            P.barrier()

        def phase_D(s):
            AR.reset(PERSIST_TOP)
            wst = [AR.f32("wst%d" % i, 16 * 256, ("p (k n) -> p k n", dict(k=16)), top=True) for i in range(2)]
            wbf = [AR.b16("wbf%d" % i, 16 * 256, ("p (k n) -> p k n", dict(k=16)), top=True) for i in range(2)]
            ust = [AR.b16("ust%d" % i, 2 * NTOK, ("p (c n) -> p c n", dict(c=2)), top=True) for i in range(2)]
            rt = [AR.f32("rt%d" % i, 512, top=True) for i in range(2)]
            Wpk = w_up.rearrange("(k p) n -> p k n", p=128)
            ntok = 2048 + (16 if s == 1 else 0)
            blocks = [(i * 512, 512) for i in range(4)] if s == 0 else [(0, 416), (416, 416), (832, 416), (1248, 416), (1664, 400)]
            ngr = DFF // 256

            def wload(gi):
                ws, Bws = wst[gi % 2]
                wb, Bwb = wbf[gi % 2]
                P.dma(lambda e: e.dma_start(out=ws[:, :, :], in_=Wpk[:, :, gi * 256:(gi + 1) * 256]), Bws, reads=[B_in], writes=[Bws])
                P.op("dve", lambda e: e.tensor_copy(out=wb[:, 0:8, :], in_=ws[:, 0:8, :]), reads=[Bws], writes=[Bwb])
                P.op("act", lambda e: e.activation(out=wb[:, 8:16, :], in_=ws[:, 8:16, :], func=AF.Copy), reads=[Bws], writes=[Bwb])

            wload(0)
            cnt = 0
            for gi in range(ngr):
                if gi + 1 < ngr:
                    wload(gi + 1)
                wb, Bwb = wbf[gi % 2]
                us, Bus = ust[gi % 2]
                for cc in range(2):
                    for (tb0, nb) in blocks:
                        ps, Bps = rotA()
                        for k in range(16):
                            P.op("pe", lambda e, ps=ps, k=k, cc=cc, tb0=tb0, nb=nb, wb=wb: e.matmul(
                                ps[:, 0:nb], lhsT=wb[:, k, cc * 128:(cc + 1) * 128], rhs=BIG[:, k, tb0:tb0 + nb],
                                start=(k == 0), stop=(k == 15)), reads=[B_big, Bwb], writes=[Bps])
                        r_, Br = rt[cnt % 2]
                        cnt += 1
                        P.op("act", lambda e, ps=ps, r_=r_, nb=nb: e.activation(out=r_[:, 0:nb], in_=ps[:, 0:nb], func=AF.Relu),
                             reads=[Bps], writes=[Br])
                        P.op("dve", lambda e, r_=r_, us=us, cc=cc, tb0=tb0, nb=nb: e.tensor_tensor(
                            out=us[:, cc, tb0:tb0 + nb], in0=r_[:, 0:nb], in1=r_[:, 0:nb], op=ALU.mult), reads=[Br], writes=[Bus])
                for cc in range(2):
                    P.dma(lambda e, us=us, gi=gi, cc=cc: e.dma_start(
                        out=utscr[s][0:16, :, 2 * gi + cc, :].rearrange("t p n -> p t n"),
                        in_=us[:, cc, 0:2048].rearrange("p (t n) -> p t n", t=16)), Bus, reads=[Bus], writes=B_ut[s][0:16])
                if s == 1:
                    P.dma(lambda e, us=us, gi=gi: e.dma_start(out=utscr[s][16, :, 2 * gi:2 * gi + 2, 0:16], in_=us[:, :, 2048:2064]), Bus,
                          reads=[Bus], writes=[B_ut[s][16]])
            P.barrier()

        def phase_E(s):
            AR.reset(PERSIST_TOP)
            def wres(c):
                return BIG[:, c // 4, (c % 4) * 512:(c % 4) * 512 + 512]
            wst = [AR.f32("wst%d" % i, 8 * 512, ("p (k n) -> p k n", dict(k=8)), top=True) for i in range(2)]
            ut = [AR.b16("ut%d" % i, 64 * 128, ("p (c n) -> p c n", dict(c=64))) for i in range(2)]
            x1s = [AR.f32("x1s%d" % i, 512) for i in range(2)]
            yst = [AR.f32("yst%d" % i, 512) for i in range(2)]
            Wpc = w_down.rearrange("(c p) n -> p c n", p=128)
            tl = tiles_of(s)
            steps = [(g, t, row0, nt) for g in range(4) for (t, row0, nt) in tl]
            wcnt = [0]

            def wgroup(g):
                c0 = g * 512
                for i in range(8):
                    ws, Bws = wst[wcnt[0] % 2]
                    wcnt[0] += 1
                    P.dma(lambda e, ws=ws, i=i, c0=c0: e.dma_start(out=ws[:, :, :], in_=Wpc[:, 8 * i:8 * i + 8, c0:c0 + 512]), Bws,
                          reads=[B_in], writes=[Bws])
                    if i % 2 == 0:
                        P.op("dve", lambda e, ws=ws, i=i: e.tensor_copy(out=BIG[:, 2 * i:2 * i + 2, 0:2048].rearrange("p k (a n) -> p k a n", n=512),
                                                                        in_=ws[:, :, :].rearrange("p (k a) n -> p k a n", k=2)),
                             reads=[Bws], writes=[B_big])
                    else:
                        P.op("act", lambda e, ws=ws, i=i: e.activation(out=BIG[:, 2 * i:2 * i + 2, 0:2048].rearrange("p k (a n) -> p k a n", n=512),
                                                                       in_=ws[:, :, :].rearrange("p (k a) n -> p k a n", k=2), func=AF.Copy),
                             reads=[Bws], writes=[B_big])

            def pre(j):
                g, t, row0, nt = steps[j]
                c0 = g * 512
                u_, Bu = ut[j % 2]
                x_, Bx = x1s[j % 2]
                P.dma(lambda e: e.dma_start(out=u_[:, :, 0:nt], in_=utscr[s][t, :, :, 0:nt]), Bu, reads=[B_ut[s][t]], writes=[Bu])
                P.dma(lambda e: e.dma_start(out=x_[0:nt, :], in_=x1scr[s][row0:row0 + nt, c0:c0 + 512]), Bx, reads=[B_x1[s][t]], writes=[Bx])

            pre(0)
            for j, (g, t, row0, nt) in enumerate(steps):
                c0 = g * 512
                if t == 0:
                    wgroup(g)
                if j + 1 < len(steps):
                    pre(j + 1)
                u_, Bu = ut[j % 2]
                x_, Bx = x1s[j % 2]
                y_, By = yst[j % 2]
                ps, Bps = rotA()
                for c in range(64):
                    P.op("pe", lambda e, ps=ps, c=c, u_=u_, nt=nt: e.matmul(ps[0:nt, :], lhsT=u_[:, c, 0:nt], rhs=wres(c),
                                                                            start=(c == 0), stop=(c == 63)),
                         reads=[Bu, B_big], writes=[Bps])
                P.op("dve", lambda e, ps=ps, x_=x_, y_=y_, nt=nt: e.tensor_tensor(out=y_[0:nt, :], in0=ps[0:nt, :], in1=x_[0:nt, :], op=ALU.add),
                     reads=[Bps, Bx], writes=[By])
                dst = yp[s, row0:row0 + nt, c0:c0 + 512] if t < 16 else ys[0:nt, c0:c0 + 512]
                P.dma(lambda e, y_=y_, dst=dst, nt=nt: e.dma_start(out=dst, in_=y_[0:nt, :]), By, reads=[By], writes=[B_out])
            P.barrier()

        P.barrier(force=True)
        for s in range(2):
            phase_norm(s, lambda t, nt, s=s: xrows(s, t, nt), [B_in] * 17, ag)
            if stop == "A0":
                break
            if not os.environ.get("KSKIP"):
                phase_gemm_tok(s, w_in, NIN, evac_A1(s))
            if stop == "A1":
                break
            if not os.environ.get("KSKIP"):
                phase_B1(s)
            if stop == "B1":
                break
            phase_B2(s)
            if stop == "B2":
                break
            AR.reset(PERSIST_TOP)
            ntok = 2048 + (16 if s == 1 else 0)
            for half in range(2):
                P.dma(lambda e, half=half, s=s, ntok=ntok: e.dma_start(
                    out=BIG[:, half * 8:half * 8 + 8, 0:ntok],
                    in_=mixscr[s][half * 8:half * 8 + 8, :, 0:ntok].rearrange("c p n -> p c n")), B_big,
                    reads=[B_mix[s]], writes=[B_big])
            phase_gemm_tok(s, w_out, D, evac_C(s))
            if stop == "C":
                break
            phase_norm(s, lambda t, nt, s=s: x1scr[s][t * 128:t * 128 + nt, :], B_x1[s], mg)
            phase_D(s)
            if stop == "D":
                break
            phase_E(s)
            if stop == "E":
                break
        P.emit(nc, es)
    return nc


def _consts():
    cf = np.zeros((128, 392), np.float32)
    j = np.arange(128)[:, None]
    i = np.arange(128)[None, :]
    cf[:, 0:128] = (j == i)
    cf[:, 128:256] = (j <= i)
    cf[:, 256:384] = (j > i)
    cf[:, 384] = 1.0
    cb = np.zeros((128, 256 + STRIPW + 288), np.float32)
    cb[:, 0:128] = (j == i)
    cb[:, 128:256] = 1.0
    c = np.arange(STRIPW)[None, :] - STRIP0
    dl = c - j
    m = ((dl <= 128).astype(np.float32) + ((dl % 4 == 0) & (dl <= 512)) + ((dl % 16 == 0) & (dl <= 2048))) * (dl >= 0)
    cb[:, 256:256 + STRIPW] = m
    sm = np.zeros((128, 9, 8, 4), np.float32)
    p = np.arange(128)
    for t in range(4):
        for ti in range(4):
            dl = 2048 + t - (1536 + 128 * ti + p)
            sm[:, ti, :, t] = ((dl <= 128).astype(np.float32) + ((dl % 4 == 0) & (dl <= 512)))[:, None]
        for r in range(4):
            sm[:, 4 + r, :, t] = 1.0 if t == r else 0.0
        for tp in range(4):
            sm[tp, 8, :, t] = 0.0 if tp > t else (3.0 if tp == t else 1.0)
    cb[:, 256 + STRIPW:] = sm.reshape(128, 288)
    return cf, cb.astype(ml_dtypes.bfloat16)


_NC_CACHE = {}


def kernel(x_prompt, x_sample, cache_win_k, cache_win_v, state_gla, attn_norm_g, w_in, q_norm_g, k_norm_g,
           w_gate2, b_gate, gla_norm_g, w_out, mlp_norm_g, w_up, w_down, _cores=None, _stop=None):
    f = lambda a: np.ascontiguousarray(np.asarray(a, dtype=np.float32))
    x_prompt, x_sample = f(x_prompt), f(x_sample)
    cache_win_k, cache_win_v, state_gla = f(cache_win_k), f(cache_win_v), f(state_gla)
    cf, cb = _consts()
    shared = {
        "w_in": f(w_in)[0], "w_out": f(w_out)[0], "w_up": f(w_up)[0], "w_down": f(w_down)[0],
        "ag": f(attn_norm_g).reshape(1, D), "mg": f(mlp_norm_g).reshape(1, D),
        "qg": f(q_norm_g).reshape(1, 128), "kg": f(k_norm_g).reshape(1, 128),
        "gg": f(gla_norm_g).reshape(1, 1024), "wg2": f(w_gate2)[0], "bg": f(b_gate).reshape(1, 512),
        "constf": cf, "constb": cb,
    }
    cores = list(range(NCORES)) if _cores is None else list(_cores)
    in_maps = []
    for c in cores:
        m = dict(shared)
        m["xp"] = x_prompt[2 * c:2 * c + 2]
        m["xs"] = x_sample[4 * c:4 * c + 4].reshape(16, D)
        m["ck"] = cache_win_k[0, 4 * c:4 * c + 4].reshape(4, SEQ, 1024)
        m["cv"] = cache_win_v[0, 4 * c:4 * c + 4].reshape(4, SEQ, 1024)
        m["sg"] = state_gla[0, 4 * c:4 * c + 4]
        in_maps.append(m)
    if _stop not in _NC_CACHE:
        _NC_CACHE[_stop] = build(_stop)
    res = run_bass_kernel_spmd(_NC_CACHE[_stop], in_maps, core_ids=list(range(len(cores))))
    R = res.results
    if _stop:
        return R
    n = len(cores)
    y_p = np.concatenate([R[i]["yp"] for i in range(n)], 0)
    y_s = np.concatenate([R[i]["ys"].reshape(4, 4, D) for i in range(n)], 0)
    p_k = np.concatenate([R[i]["pk"].reshape(2, SEQ, 8, 128) for i in range(n)], 0)[None]
    p_v = np.concatenate([R[i]["pv"].reshape(2, SEQ, 8, 128) for i in range(n)], 0)[None]
    p_s = np.concatenate([R[i]["pst"] for i in range(n)], 0)[None]
    s_k = np.concatenate([R[i]["sk"].reshape(4, 4, 8, 128) for i in range(n)], 0)[None]
    s_v = np.concatenate([R[i]["sv"].reshape(4, 4, 8, 128) for i in range(n)], 0)[None]
    s_s = np.concatenate([R[i]["sst"] for i in range(n)], 0)[None]
    return (y_p, y_s, p_k, p_v, p_s, s_k, s_v, s_s)
```

```python
import os
import numpy as np
from contextlib import ExitStack
import ml_dtypes
import concourse.bass as bass
import concourse.mybir as mybir
from concourse.bass_utils import run_bass_kernel_spmd

F32 = mybir.dt.float32
BF16 = mybir.dt.bfloat16
AF = mybir.ActivationFunctionType
ALU = mybir.AluOpType
AX = mybir.AxisListType

NCORES = 8
SEQ = 2048
D = 2048
NIN = 6160
DFF = 8192
EPS = 1e-6
NTOK = 2176
STRIP0 = 384
STRIPW = 2432


class Buf:
    __slots__ = ("name", "lw", "rd", "dsem", "alias", "keep")

    def __init__(self, name):
        self.name = name
        self.lw = {}
        self.rd = {}
        self.dsem = None
        self.alias = ()
        self.keep = False


class Prog:
    ENG = ("pe", "act", "dve", "pool", "sp")

    def __init__(self):
        self.q = {e: [] for e in self.ENG}
        self.cnt = {}
        self.known = {e: {} for e in self.ENG}
        self.nsem = 0

    def _issue(self, eng, fn, reads, writes, semkey, inc):
        d = {}
        for b in reads:
            for k, v in b.lw.items():
                if d.get(k, 0) < v:
                    d[k] = v
        ali = False
        for b in writes:
            for k, v in b.lw.items():
                if d.get(k, 0) < v:
                    d[k] = v
            for k, v in b.rd.items():
                if d.get(k, 0) < v:
                    d[k] = v
            if b.alias:
                al = b.alias() if callable(b.alias) else b.alias
                for ab in al:
                    if ab is b:
                        continue
                    ali = True
                    for k, v in ab.lw.items():
                        if d.get(k, 0) < v:
                            d[k] = v
                    for k, v in ab.rd.items():
                        if d.get(k, 0) < v:
                            d[k] = v
                if not b.keep:
                    b.alias = ()
        waits = []
        kn = self.known[eng]
        for k, v in d.items():
            if k == "pe" and eng == "pe" and not ali:
                continue
            if isinstance(k, tuple):
                v = self.cnt[k]
            if kn.get(k, 0) >= v:
                continue
            kn[k] = v
            waits.append((k, v))
        self.cnt[semkey] = self.cnt.get(semkey, 0) + inc
        val = self.cnt[semkey]
        self.q[eng].append((waits, fn, semkey, inc))
        for b in reads:
            if b.rd.get(semkey, 0) < val:
                b.rd[semkey] = val
        for b in writes:
            if b.lw.get(semkey, 0) < val:
                b.lw[semkey] = val

    def op(self, eng, fn, reads=(), writes=()):
        self._issue(eng, fn, reads, writes, eng, 1)

    def dma(self, fn, sbuf, reads=(), writes=()):
        if sbuf.dsem is None:
            sbuf.dsem = ("d", self.nsem)
            self.nsem += 1
        self._issue("sp", fn, reads, writes, sbuf.dsem, 16)

    def barrier(self, force=False):
        if not force:
            return
        for e in self.ENG:
            kn = self.known[e]
            waits = []
            for k, v in self.cnt.items():
                if kn.get(k, 0) < v:
                    kn[k] = v
                    waits.append((k, v))
            if waits:
                self.q[e].append((waits, None, None, 0))

    def emit(self, nc, es):
        sems = {}
        for k in sorted(self.cnt.keys(), key=str):
            nm = k if isinstance(k, str) else "d%d" % k[1]
            sems[k] = es.enter_context(nc.semaphore("s_" + nm))
        block = es.enter_context(nc.Block())
        final = dict(self.cnt)

        def run(ename, e):
            for waits, fn, semkey, inc in self.q[ename]:
                for k, v in waits:
                    e.wait_ge(sems[k], v)
                if fn is not None:
                    fn(e).then_inc(sems[semkey], inc)
            if ename == "sp":
                for k, v in final.items():
                    e.wait_ge(sems[k], v)

        @block.tensor
        def _(e):
            run("pe", e)

        @block.scalar
        def _(e):
            run("act", e)

        @block.vector
        def _(e):
            run("dve", e)

        @block.gpsimd
        def _(e):
            run("pool", e)

        @block.sync
        def _(e):
            run("sp", e)


class Arena:
    def __init__(self, tensor, nwords):
        self.t = tensor
        self.n = nwords
        self.top = 0
        self.hi = nwords
        self.hist = []
        self.track = False

    def reset(self, base):
        self.top = base
        self.hi = self.n

    def _take(self, name, nw, top):
        nw = (nw + 7) // 8 * 8
        if top:
            self.hi -= nw
            off = self.hi
        else:
            off = self.top
            self.top += nw
        assert self.top <= self.hi <= self.n, (name, self.top, self.hi, self.n)
        b = Buf(name)
        if self.track:
            b.alias = [ob for (lo, hi, ob) in self.hist if lo < off + nw and off < hi]
            self.hist.append((off, off + nw, b))
        return off, b

    def f32(self, name, ncols, shape=None, top=False):
        off, b = self._take(name, ncols, top)
        ap = self.t[:, off:off + ncols]
        if shape:
            ap = ap.rearrange(shape[0], **shape[1])
        return ap, b

    def b16(self, name, ncols, shape=None, top=False):
        nw = (ncols + 1) // 2
        off, b = self._take(name, nw, top)
        ap = self.t[:, off:off + nw].bitcast(BF16)[:, 0:ncols]
        if shape:
            ap = ap.rearrange(shape[0], **shape[1])
        return ap, b


def tiles_of(s):
    tl = [(t, t * 128, 128) for t in range(16)]
    if s == 1:
        tl.append((16, 2048, 16))
    return tl


def build(stop=None):
    nc = bass.Bass("TRN2", target_bir_lowering=False)

    def din(name, shape, dt=F32):
        return nc.dram_tensor(name, list(shape), dt, kind="ExternalInput").ap()

    def dout(name, shape, dt=F32):
        return nc.dram_tensor(name, list(shape), dt, kind="ExternalOutput").ap()

    def dscr(name, shape, dt=F32):
        return nc.dram_tensor(name, list(shape), dt, kind="ExternalOutput" if (stop or os.environ.get("KSCR")) else "Internal").ap()

    xp = din("xp", [2, SEQ, D])
    xs = din("xs", [16, D])
    ck = din("ck", [4, SEQ, 1024])
    cv = din("cv", [4, SEQ, 1024])
    sg = din("sg", [4, 4, 128, 256])
    w_in = din("w_in", [D, NIN])
    w_out = din("w_out", [D, D])
    w_up = din("w_up", [D, DFF])
    w_down = din("w_down", [DFF, D])
    ag = din("ag", [1, D])
    mg = din("mg", [1, D])
    qg = din("qg", [1, 128])
    kg = din("kg", [1, 128])
    gg = din("gg", [1, 1024])
    wg2 = din("wg2", [16, 512])
    bg = din("bg", [1, 512])
    constf = din("constf", [128, 392])
    constb = din("constb", [128, 256 + STRIPW + 288], BF16)

    yp = dout("yp", [2, SEQ, D])
    ys = dout("ys", [16, D])
    pk = dout("pk", [2, SEQ, 1024])
    pv = dout("pv", [2, SEQ, 1024])
    pst = dout("pst", [2, 4, 128, 256])
    sk = dout("sk", [16, 1024])
    sv = dout("sv", [16, 1024])
    sst = dout("sst", [4, 4, 128, 256])

    projscr = [dscr("projscr%d" % s, [2064, NIN]) for s in range(2)]
    mixscr = [dscr("mixscr%d" % s, [16, 128, NTOK], BF16) for s in range(2)]
    x1scr = [dscr("x1scr%d" % s, [NTOK, D]) for s in range(2)]
    utscr = [dscr("utscr%d" % s, [17, 128, 64, 128], BF16) for s in range(2)]
    B_proj = [[Buf("proj%d_%d" % (s, t)) for t in range(17)] for s in range(2)]
    B_mix = [Buf("mix%d" % s) for s in range(2)]
    B_x1 = [[Buf("x1_%d_%d" % (s, t)) for t in range(17)] for s in range(2)]
    B_ut = [[Buf("ut_%d_%d" % (s, t)) for t in range(17)] for s in range(2)]
    B_in = Buf("inputs")
    B_out = Buf("outputs")

    P = Prog()
    es = ExitStack()
    with es:
        NW = 45056
        arena_t = es.enter_context(nc.sbuf_tensor("arena", [128, NW], F32))
        AR = Arena(arena_t, NW)
        psA = []
        for i in range(6):
            t_ = es.enter_context(nc.psum_tensor("psA%d" % i, [128, 512], F32))
            psA.append((t_, Buf("psA%d" % i)))
        psT = []
        for i in range(2):
            t_ = es.enter_context(nc.psum_tensor("psT%d" % i, [128, 8, 128], BF16))
            psT.append((t_, Buf("psT%d" % i)))
        rot = {"a": 0, "t": 0}

        def rotA():
            r = psA[rot["a"] % 6]
            rot["a"] += 1
            return r

        def rotT():
            r = psT[rot["t"] % 2]
            rot["t"] += 1
            return r

        cf, B_cf = AR.f32("constf", 392)
        cb, B_cb = AR.b16("constb", 256 + STRIPW + 288)
        identf = cf[:, 0:128]
        Umat = cf[:, 128:256]
        Lmat = cf[:, 256:384]
        onescol = cf[:, 384:385]
        identb = cb[:, 0:128]
        onesb = cb[:, 128:256]
        strip = cb[:, 256:256 + STRIPW]
        smask = cb[:, 256 + STRIPW:256 + STRIPW + 288]
        gq_bc, B_gq = AR.f32("gq", 128)
        gk_bc, B_gk = AR.f32("gk", 128)
        gg_bc, B_gg = AR.f32("gg", 1024)
        wga, B_wga = AR.f32("wga", 512)
        Sst, B_S = AR.f32("S", 1024, ("p (h v) -> p h v", dict(h=4)))
        Sb, B_Sb = AR.b16("Sb", 1024, ("p (h v) -> p h v", dict(h=4)))
        gaT, B_gaT = AR.f32("gaT", 128)
        gbc, B_gbc = AR.f32("gbc", 2048)
        stat, B_stat_unused = AR.f32("stat", 128)
        stat_bufs = [Buf("stat%d" % i) for i in range(8)]
        BIG, B_big = AR.b16("BIG", 16 * NTOK, ("p (k n) -> p k n", dict(k=16)))
        PERSIST_TOP = AR.top
        AR.track = True
        BIGAL = []
        B_big.alias = lambda: BIGAL
        B_big.keep = True

        def bigview(name, extra=()):
            b = Buf(name)
            b.alias = [B_big] + list(extra)
            b.keep = True
            BIGAL.append(b)
            return b

        P.dma(lambda e: e.dma_start(out=cf, in_=constf), B_cf, reads=[B_in], writes=[B_cf])
        P.dma(lambda e: e.dma_start(out=cb, in_=constb), B_cb, reads=[B_in], writes=[B_cb])
        P.dma(lambda e: e.dma_start(out=gq_bc, in_=qg[0, :].partition_broadcast(128)), B_gq, reads=[B_in], writes=[B_gq])
        P.dma(lambda e: e.dma_start(out=gk_bc, in_=kg[0, :].partition_broadcast(128)), B_gk, reads=[B_in], writes=[B_gk])
        P.dma(lambda e: e.dma_start(out=gg_bc, in_=gg[0, :].partition_broadcast(128)), B_gg, reads=[B_in], writes=[B_gg])
        P.dma(lambda e: e.dma_start(out=wga[0:16, :], in_=wg2), B_wga, reads=[B_in], writes=[B_wga])
        P.dma(lambda e: e.dma_start(out=wga[16:17, :], in_=bg), B_wga, reads=[B_in], writes=[B_wga])
        P.op("act", lambda e: e.mul(gq_bc, gq_bc, 128.0 ** -0.5), reads=[B_gq], writes=[B_gq])
        P.op("pool", lambda e: e.memset(gaT[:, :], 1.0), writes=[B_gaT])

        stat_i = [0]

        def newstat(n):
            i = stat_i[0] % 8
            stat_i[0] += 1
            return stat[:, i * 16:i * 16 + n], stat_bufs[i]

        def phase_norm(s, src_fn, src_bufs, gvec):
            AR.reset(PERSIST_TOP)
            xt = [AR.f32("xt%d" % i, 2048) for i in range(2)]
            junk, B_junk = AR.f32("junk", 2048)
            hb = [AR.b16("hb%d" % i, 2048) for i in range(2)]
            P.dma(lambda e: e.dma_start(out=gbc, in_=gvec[0, :].partition_broadcast(128)), B_gbc, reads=[B_in], writes=[B_gbc])
            tl = tiles_of(s)

            def S1(i):
                t, row0, nt = tl[i]
                x_, Bx = xt[i % 2]
                h_, Bh = hb[i % 2]
                P.dma(lambda e: e.dma_start(out=x_[0:nt, :], in_=src_fn(t, nt)), Bx, reads=[src_bufs[t]], writes=[Bx])
                ssq, Bq = newstat(1)
                rs, Br = newstat(1)
                P.op("act", lambda e: e.activation(out=junk[0:nt, :], in_=x_[0:nt, :], func=AF.Square, accum_out=ssq[0:nt, :]),
                     reads=[Bx], writes=[B_junk, Bq])
                P.op("act", lambda e: e.activation(out=rs[0:nt, :], in_=ssq[0:nt, :], func=AF.Sqrt, scale=1.0 / D, bias=EPS),
                     reads=[Bq], writes=[Br])
                P.op("dve", lambda e: e.reciprocal(out=rs[0:nt, :], in_=rs[0:nt, :]), reads=[Br], writes=[Br])
                P.op("dve", lambda e: e.scalar_tensor_tensor(out=h_[0:nt, :], in0=x_[0:nt, :], scalar=rs[0:nt, 0:1], in1=gbc[0:nt, :],
                                                             op0=ALU.mult, op1=ALU.mult), reads=[Bx, Br, B_gbc], writes=[Bh])

            def S2(i):
                t, row0, nt = tl[i]
                h_, Bh = hb[i % 2]
                for half in range(2):
                    pt, Bp = rotT()
                    for k in range(8):
                        kk = half * 8 + k
                        P.op("pe", lambda e, pt=pt, k=k, kk=kk: e.transpose(
                            out=pt[:, k, 0:nt], in_=h_[0:nt, kk * 128:(kk + 1) * 128], identity=identb[0:nt, 0:nt]),
                            reads=[Bh, B_cb], writes=[Bp])
                    if half == 0:
                        P.op("act", lambda e, pt=pt, half=half: e.activation(
                            out=BIG[:, half * 8:half * 8 + 8, row0:row0 + nt], in_=pt[:, :, 0:nt], func=AF.Copy),
                            reads=[Bp], writes=[B_big])
                    else:
                        P.op("dve", lambda e, pt=pt, half=half: e.tensor_copy(
                            out=BIG[:, half * 8:half * 8 + 8, row0:row0 + nt], in_=pt[:, :, 0:nt]),
                            reads=[Bp], writes=[B_big])

            S1(0)
            for i in range(len(tl)):
                if i + 1 < len(tl):
                    S1(i + 1)
                S2(i)
            P.barrier()

        def phase_gemm_tok(s, W, ncols_total, evac):
            AR.reset(PERSIST_TOP)
            wst = [AR.f32("wst%d" % i, 16 * 256, ("p (k n) -> p k n", dict(k=16)), top=True) for i in range(2)]
            wbf = [AR.b16("wbf%d" % i, 16 * 256, ("p (k n) -> p k n", dict(k=16)), top=True) for i in range(2)]
            ctx = {"AR": AR}
            Wpk = W.rearrange("(k p) n -> p k n", p=128)
            groups = []
            c0 = 0
            while c0 < ncols_total:
                nc_ = min(256, ncols_total - c0)
                groups.append((c0, nc_))
                c0 += nc_

            def wload(gi):
                c0, ncol = groups[gi]
                ws, Bws = wst[gi % 2]
                wb, Bwb = wbf[gi % 2]
                P.dma(lambda e: e.dma_start(out=ws[:, :, 0:ncol], in_=Wpk[:, :, c0:c0 + ncol]), Bws, reads=[B_in], writes=[Bws])
                P.op("dve", lambda e: e.tensor_copy(out=wb[:, 0:8, 0:ncol], in_=ws[:, 0:8, 0:ncol]), reads=[Bws], writes=[Bwb])
                P.op("act", lambda e: e.activation(out=wb[:, 8:16, 0:ncol], in_=ws[:, 8:16, 0:ncol], func=AF.Copy), reads=[Bws], writes=[Bwb])

            import os
            if os.environ.get("KGROUPS"):
                a_, b_ = [int(v) for v in os.environ["KGROUPS"].split(",")]
                groups = groups[a_:b_]
            st = evac("init", ctx)
            wload(0)
            steps = [(gi, c0, ncol, t, row0, nt) for gi, (c0, ncol) in enumerate(groups) for (t, row0, nt) in tiles_of(s)]
            PF = 2
            for j in range(min(PF, len(steps))):
                evac("pre", ctx, st, *steps[j][3:6], *steps[j][1:3])
            for j, (gi, c0, ncol, t, row0, nt) in enumerate(steps):
                if t == 0 and gi + 1 < len(groups):
                    wload(gi + 1)
                if j + PF < len(steps):
                    evac("pre", ctx, st, *steps[j + PF][3:6], *steps[j + PF][1:3])
                wb, Bwb = wbf[gi % 2]
                ps, Bps = rotA()
                for k in range(16):
                    P.op("pe", lambda e, ps=ps, k=k, row0=row0, nt=nt, wb=wb, ncol=ncol: e.matmul(
                        ps[0:nt, 0:ncol], lhsT=BIG[:, k, row0:row0 + nt], rhs=wb[:, k, 0:ncol],
                        start=(k == 0), stop=(k == 15)), reads=[B_big, Bwb], writes=[Bps])
                evac("tile", ctx, st, t, row0, nt, c0, ncol, ps, Bps)
            P.barrier()

        def evac_A1(s):
            cnt = [0]

            def f(kind, ctx, st=None, t=0, row0=0, nt=0, c0=0, ncol=0, ps=None, Bps=None):
                if kind == "init":
                    return [ctx["AR"].f32("ost%d" % i, 256) for i in range(4)]
                if kind == "pre":
                    return
                o_, Bo = st[cnt[0] % 4]
                eng = "act" if cnt[0] % 2 == 0 else "dve"
                cnt[0] += 1
                if eng == "act":
                    P.op("act", lambda e: e.activation(out=o_[0:nt, 0:ncol], in_=ps[0:nt, 0:ncol], func=AF.Copy),
                         reads=[Bps], writes=[Bo])
                else:
                    P.op("dve", lambda e: e.tensor_copy(out=o_[0:nt, 0:ncol], in_=ps[0:nt, 0:ncol]), reads=[Bps], writes=[Bo])
                P.dma(lambda e: e.dma_start(out=projscr[s][row0:row0 + nt, c0:c0 + ncol], in_=o_[0:nt, 0:ncol]), Bo,
                      reads=[Bo], writes=[B_proj[s][t]])
            return f

        def xrows(s, t, nt):
            if t < 16:
                return xp[s, t * 128:t * 128 + nt, :]
            return xs[0:nt, :]

        def evac_C(s):
            cnt = [0]
            pcnt = [0]

            def f(kind, ctx, st=None, t=0, row0=0, nt=0, c0=0, ncol=0, ps=None, Bps=None):
                if kind == "init":
                    a = [ctx["AR"].f32("ost%d" % i, 256) for i in range(4)]
                    b = [ctx["AR"].f32("xsl%d" % i, 256) for i in range(4)]
                    return (a, b)
                if kind == "pre":
                    x_, Bx = st[1][pcnt[0] % 4]
                    pcnt[0] += 1
                    P.dma(lambda e: e.dma_start(out=x_[0:nt, 0:ncol], in_=xrows(s, t, nt)[:, c0:c0 + ncol]), Bx,
                          reads=[B_in], writes=[Bx])
                    return
                o_, Bo = st[0][cnt[0] % 4]
                x_, Bx = st[1][cnt[0] % 4]
                cnt[0] += 1
                P.op("dve", lambda e: e.tensor_tensor(out=o_[0:nt, 0:ncol], in0=ps[0:nt, 0:ncol], in1=x_[0:nt, 0:ncol], op=ALU.add),
                     reads=[Bps, Bx], writes=[Bo])
                P.dma(lambda e: e.dma_start(out=x1scr[s][row0:row0 + nt, c0:c0 + ncol], in_=o_[0:nt, 0:ncol]), Bo,
                      reads=[Bo], writes=[B_x1[s][t]])
            return f

        def qk_prep(pj, Bpj, nt, junk, B_junk, qh, Bqh, kf, Bkf, kh, Bkh):
            P.op("act", lambda e: e.activation(out=junk[0:nt, 0:2048], in_=pj[0:nt, 0:2048], func=AF.Square),
                 reads=[Bpj], writes=[B_junk])
            ssq, Bq = newstat(16)
            P.op("dve", lambda e: e.tensor_reduce(out=ssq[0:nt, :], in_=junk[0:nt, 0:2048].rearrange("p (h d) -> p h d", h=16),
                                                  axis=AX.X, op=ALU.add), reads=[B_junk], writes=[Bq])
            P.op("act", lambda e: e.activation(out=ssq[0:nt, :], in_=ssq[0:nt, :], func=AF.Sqrt, scale=1.0 / 128, bias=EPS),
                 reads=[Bq], writes=[Bq])
            P.op("dve", lambda e: e.reciprocal(out=ssq[0:nt, :], in_=ssq[0:nt, :]), reads=[Bq], writes=[Bq])
            for h in range(8):
                P.op("dve", lambda e, h=h: e.scalar_tensor_tensor(
                    out=qh[0:nt, h * 128:(h + 1) * 128], in0=pj[0:nt, h * 128:(h + 1) * 128], scalar=ssq[0:nt, h:h + 1],
                    in1=gq_bc[0:nt, :], op0=ALU.mult, op1=ALU.mult), reads=[Bpj, Bq, B_gq], writes=[Bqh])
            for h in range(8):
                P.op("dve", lambda e, h=h: e.scalar_tensor_tensor(
                    out=kf[0:nt, h * 128:(h + 1) * 128], in0=pj[0:nt, 1024 + h * 128:1024 + (h + 1) * 128],
                    scalar=ssq[0:nt, 8 + h:9 + h], in1=gk_bc[0:nt, :], op0=ALU.mult, op1=ALU.mult),
                    reads=[Bpj, Bq, B_gk], writes=[Bkf])
            P.op("dve", lambda e: e.tensor_copy(out=kh[0:nt, :], in_=kf[0:nt, :]), reads=[Bkf], writes=[Bkh])

        def phase_B1(s):
            AR.reset(PERSIST_TOP)
            KT = BIG[:, 0:8, 0:2048]
            pj = [AR.f32("pjqk%d" % i, 3072) for i in range(2)]
            junk, B_junk = AR.f32("junk", 2048)
            qh, Bqh = AR.b16("qh", 1024)
            kh, Bkh = AR.b16("kh", 1024)
            kf, Bkf = AR.f32("kf", 1024)
            QT, B_QT = AR.b16("QT", 8 * 512, ("p (h n) -> p h n", dict(h=8)))
            Eb = [AR.b16("E%d" % i, 512) for i in range(2)]
            Pb = [AR.b16("P%d" % i, 512) for i in range(2)]
            lnz, B_lnz = AR.f32("lnz", 512)
            mixA = [AR.b16("mixA%d" % i, 512) for i in range(2)]
            B_KT = [bigview("KT%d" % t) for t in range(16)]
            B_V = [bigview("V%d" % t) for t in range(16)]

            def Vtile(t):
                return BIG[:, 8 + t // 2, (t % 2) * 1024:(t % 2) * 1024 + 1024]

            hcount = [0]
            QT2, B_QT2 = AR.b16("QT2", 8 * 512, ("p (h n) -> p h n", dict(h=8)))
            QTs = [(QT, B_QT), (QT2, B_QT2)]

            def pjload(t):
                pj_, Bpj = pj[t % 2]
                P.dma(lambda e: e.dma_start(out=pj_[:, :], in_=projscr[s][t * 128:t * 128 + 128, 0:3072]), Bpj,
                      reads=[B_proj[s][t]], writes=[Bpj])

            def prep(t, part=3):
                ts = t % 4
                row0 = t * 128
                QT_, BQT_ = QTs[(t // 4) % 2]
                pj_, Bpj = pj[t % 2]
                if part & 1:
                    prep1(t, row0, pj_, Bpj)
                if part & 2:
                    prep2(t, ts, row0, QT_, BQT_)

            def prep1(t, row0, pj_, Bpj):
                if t + 1 < 16:
                    pjload(t + 1)
                qk_prep(pj_, Bpj, 128, junk, B_junk, qh, Bqh, kf, Bkf, kh, Bkh)
                P.dma(lambda e: e.dma_start(out=pk[s, row0:row0 + 128, :], in_=kf[:, :]), Bkf, reads=[Bkf], writes=[B_out])
                P.dma(lambda e: e.dma_start(out=pv[s, row0:row0 + 128, :], in_=pj_[:, 2048:3072]), Bpj, reads=[Bpj], writes=[B_out])
                P.op("pool", lambda e: e.tensor_copy(out=Vtile(t), in_=pj_[:, 2048:3072]), reads=[Bpj], writes=[B_V[t]])

            def prep2(t, ts, row0, QT_, BQT_):
                pt, Bp = rotT()
                for h in range(8):
                    P.op("pe", lambda e, pt=pt, h=h: e.transpose(out=pt[:, h, :], in_=qh[:, h * 128:(h + 1) * 128], identity=identb),
                         reads=[Bqh, B_cb], writes=[Bp])
                P.op("act", lambda e, pt=pt: e.activation(out=QT_[:, :, ts * 128:(ts + 1) * 128], in_=pt[:, :, :], func=AF.Copy),
                     reads=[Bp], writes=[BQT_])
                pt2, Bp2 = rotT()
                for h in range(8):
                    P.op("pe", lambda e, pt2=pt2, h=h: e.transpose(out=pt2[:, h, :], in_=kh[:, h * 128:(h + 1) * 128], identity=identb),
                         reads=[Bkh, B_cb], writes=[Bp2])
                P.op("dve", lambda e, pt2=pt2: e.tensor_copy(out=KT[:, :, row0:row0 + 128], in_=pt2[:, :, :]),
                     reads=[Bp2], writes=[B_KT[t]])

            Eb4 = Eb + [AR.b16("E%d" % i, 512) for i in range(2, 4)]
            Pb4 = Pb + [AR.b16("P%d" % i, 512) for i in range(2, 4)]
            lnz2 = [(lnz, B_lnz), AR.f32("lnzB", 512)]
            sTbanks = [[psA[4], psA[5]],
                       [(psT[0][0].rearrange("p a b -> p (a b)").bitcast(F32), psT[0][1]),
                        (psT[1][0].rearrange("p a b -> p (a b)").bitcast(F32), psT[1][1])]]

            def attention_pair(b, hA, hB):
                QT_, BQT_ = QTs[b % 2]
                nkt = 4 * b + 4
                heads = (hA, hB)
                oTs = [psA[0], psA[2]]
                zbs = [psA[1], psA[3]]

                def geom(kt):
                    a = kt - 4 * b
                    q0 = max(0, a) * 128
                    return q0, 512 - q0

                def emit_sT(w, kt):
                    h = heads[w]
                    q0, nq = geom(kt)
                    sT, BsT = sTbanks[w][kt % 2]
                    P.op("pe", lambda e: e.matmul(sT[:, 0:nq], lhsT=KT[:, h, kt * 128:(kt + 1) * 128], rhs=QT_[:, h, q0:512],
                                                  start=True, stop=True), reads=[B_KT[kt], BQT_], writes=[BsT])

                def emit_soft(w, kt):
                    q0, nq = geom(kt)
                    sT, BsT = sTbanks[w][kt % 2]
                    E_, BE = Eb4[2 * w + kt % 2]
                    P_, BP = Pb4[2 * w + kt % 2]
                    soff = b * 512 - kt * 128 + STRIP0 + q0
                    P.op("act", lambda e: e.activation(out=E_[:, 0:nq], in_=sT[:, 0:nq], func=AF.Exp), reads=[BsT], writes=[BE])
                    P.op("dve", lambda e: e.tensor_tensor(out=P_[:, 0:nq], in0=E_[:, 0:nq], in1=strip[:, soff:soff + nq], op=ALU.mult),
                         reads=[BE, B_cb], writes=[BP])

                def emit_pv(w, kt):
                    h = heads[w]
                    q0, nq = geom(kt)
                    P_, BP = Pb4[2 * w + kt % 2]
                    oT, BoT = oTs[w]
                    zb, Bzb = zbs[w]
                    P.op("pe", lambda e: e.matmul(oT[:, q0:512], lhsT=Vtile(kt)[:, h * 128:(h + 1) * 128], rhs=P_[:, 0:nq],
                                                  start=(kt == 0), stop=(kt == nkt - 1)), reads=[B_V[kt], BP], writes=[BoT])
                    P.op("pe", lambda e: e.matmul(zb[:, q0:512], lhsT=onesb, rhs=P_[:, 0:nq], start=(kt == 0), stop=(kt == nkt - 1)),
                         reads=[B_cb, BP], writes=[Bzb])

                emit_sT(0, 0)
                emit_sT(1, 0)
                for kt in range(nkt):
                    if kt + 1 < nkt:
                        emit_sT(0, kt + 1)
                        emit_sT(1, kt + 1)
                    emit_soft(0, kt)
                    emit_soft(1, kt)
                    emit_pv(0, kt)
                    emit_pv(1, kt)
                for w in range(2):
                    h = heads[w]
                    oT, BoT = oTs[w]
                    zb, Bzb = zbs[w]
                    lz, Blz = lnz2[w]
                    m_, Bm = mixA[w]

                    def tail(h=h, oT=oT, BoT=BoT, zb=zb, Bzb=Bzb, lz=lz, Blz=Blz, m_=m_, Bm=Bm):
                        P.op("act", lambda e: e.activation(out=lz[:, :], in_=zb[:, :], func=AF.Ln), reads=[Bzb], writes=[Blz])
                        P.op("act", lambda e: e.activation(out=lz[:, :], in_=lz[:, :], func=AF.Exp, scale=-1.0), reads=[Blz], writes=[Blz])
                        P.op("dve", lambda e: e.tensor_tensor(out=m_[:, :], in0=oT[:, :], in1=lz[:, :], op=ALU.mult),
                             reads=[BoT, Blz], writes=[Bm])
                        P.dma(lambda e: e.dma_start(out=mixscr[s][h, :, b * 512:(b + 1) * 512], in_=m_[:, :]), Bm,
                              reads=[Bm], writes=[B_mix[s]])
                    tail()

            pjload(0)
            for t in range(4):
                prep(t)
            for b in range(4):
                for hp in range(4):
                    if b < 3:
                        prep(4 * (b + 1) + hp, 1)
                    attention_pair(b, 2 * hp, 2 * hp + 1)
                    if b < 3:
                        prep(4 * (b + 1) + hp, 2)
            P.barrier()
            if s != 1:
                return
            def _sample_part():
                AR.reset(PERSIST_TOP)
                KTs = BIG[:, 0:8, 0:128 * 9].rearrange("p h (t n) -> p h t n", t=9)
                pjs, Bpjs = AR.f32("pjs", 3072)
                junk, B_junk = AR.f32("junk", 2048)
                qh, Bqh = AR.b16("qh", 1024)
                kh, Bkh = AR.b16("kh", 1024)
                kf, Bkf = AR.f32("kf", 1024)
                qTs, BqTs = AR.b16("qTs", 32)
                ckt = [AR.f32("ckt%d" % i, 1024) for i in range(2)]
                ckb = [AR.b16("ckb%d" % i, 1024) for i in range(2)]
                Vs = [AR.b16("Vs%d" % i, 1024) for i in range(9)]
                cvt = [AR.f32("cvt%d" % i, 1024) for i in range(2)]
                Es, BEs = AR.b16("Es", 32)
                Ps = [AR.b16("Ps%d" % i, 32) for i in range(9)]
                rzs, Brzs = AR.f32("rzs", 32)
                mAs, BmAs = AR.b16("mAs", 32)
                B_KTs = [bigview("KTs%d" % i, B_KT) for i in range(9)]
                for bi in range(4):
                    r0 = 2048 + 4 * bi
                    P.dma(lambda e, r0=r0: e.dma_start(out=pjs[0:4, :], in_=projscr[1][r0:r0 + 4, 0:3072]), Bpjs,
                          reads=[B_proj[1][16]], writes=[Bpjs])
                    qk_prep(pjs, Bpjs, 4, junk, B_junk, qh, Bqh, kf, Bkf, kh, Bkh)
                    P.dma(lambda e, bi=bi: e.dma_start(out=sk[4 * bi:4 * bi + 4, :], in_=kf[0:4, :]), Bkf, reads=[Bkf], writes=[B_out])
                    P.dma(lambda e, bi=bi: e.dma_start(out=sv[4 * bi:4 * bi + 4, :], in_=pjs[0:4, 2048:3072]), Bpjs, reads=[Bpjs], writes=[B_out])
                    pt, Bp = rotT()
                    for h in range(8):
                        P.op("pe", lambda e, pt=pt, h=h: e.transpose(out=pt[:, h, 0:4], in_=qh[0:4, h * 128:(h + 1) * 128], identity=identb[0:4, 0:4]),
                             reads=[Bqh, B_cb], writes=[Bp])
                    P.op("act", lambda e, pt=pt: e.activation(out=qTs.rearrange("p (h t) -> p h t", h=8), in_=pt[:, :, 0:4], func=AF.Copy),
                         reads=[Bp], writes=[BqTs])
                    for i in range(9):
                        nk = 128 if i < 8 else 4
                        V_, BV = Vs[i]
                        if i < 8:
                            c_, Bc = ckt[i % 2]
                            cb_, Bcb_ = ckb[i % 2]
                            v_, Bv = cvt[i % 2]
                            if i < 4:
                                srck = ck[bi, 1536 + 128 * i:1536 + 128 * (i + 1), :]
                                srcv = cv[bi, 1536 + 128 * i:1536 + 128 * (i + 1), :]
                            else:
                                r = i - 4
                                srck = ck[bi].rearrange("(m r) c -> r m c", r=16)[r]
                                srcv = cv[bi].rearrange("(m r) c -> r m c", r=16)[r]
                            P.dma(lambda e, c_=c_, srck=srck: e.dma_start(out=c_[:, :], in_=srck), Bc, reads=[B_in], writes=[Bc])
                            P.dma(lambda e, v_=v_, srcv=srcv: e.dma_start(out=v_[:, :], in_=srcv), Bv, reads=[B_in], writes=[Bv])
                            P.op("dve", lambda e, c_=c_, cb_=cb_: e.tensor_copy(out=cb_[:, :], in_=c_[:, :]), reads=[Bc], writes=[Bcb_])
                            P.op("act", lambda e, v_=v_, V_=V_: e.activation(out=V_[:, :], in_=v_[:, :], func=AF.Copy), reads=[Bv], writes=[BV])
                            ksrc, Bks = cb_, Bcb_
                        else:
                            P.op("pool", lambda e, V_=V_: e.tensor_copy(out=V_[0:4, :], in_=pjs[0:4, 2048:3072]), reads=[Bpjs], writes=[BV])
                            ksrc, Bks = kh, Bkh
                        pt, Bp = rotT()
                        for h in range(8):
                            P.op("pe", lambda e, pt=pt, h=h, ksrc=ksrc, nk=nk: e.transpose(
                                out=pt[:, h, 0:nk], in_=ksrc[0:nk, h * 128:(h + 1) * 128], identity=identb[0:nk, 0:nk]),
                                reads=[Bks, B_cb], writes=[Bp])
                        P.op("dve", lambda e, pt=pt, i=i, nk=nk: e.tensor_copy(out=KTs[:, :, i, 0:nk], in_=pt[:, :, 0:nk]),
                             reads=[Bp], writes=[B_KTs[i]])
                        sT, BsT = psA[4 + i % 2]
                        for h in range(8):
                            P.op("pe", lambda e, sT=sT, h=h, i=i, nk=nk: e.matmul(
                                sT[0:nk, h * 4:h * 4 + 4], lhsT=KTs[:, h, i, 0:nk], rhs=qTs[:, h * 4:h * 4 + 4], start=True, stop=True),
                                reads=[B_KTs[i], BqTs], writes=[BsT])
                        P_, BP = Ps[i]
                        P.op("act", lambda e, sT=sT, nk=nk: e.activation(out=Es[0:nk, :], in_=sT[0:nk, 0:32], func=AF.Exp), reads=[BsT], writes=[BEs])
                        P.op("dve", lambda e, P_=P_, i=i, nk=nk: e.tensor_tensor(out=P_[0:nk, :], in0=Es[0:nk, :], in1=smask[0:nk, i * 32:(i + 1) * 32],
                                                                              op=ALU.mult), reads=[BEs, B_cb], writes=[BP])
                    oT, BoT = psA[bi % 2]
                    zb, Bzb = psA[2 + bi % 2]
                    for h in range(8):
                        for i in range(9):
                            nk = 128 if i < 8 else 4
                            P.op("pe", lambda e, oT=oT, h=h, i=i, nk=nk: e.matmul(
                                oT[:, h * 4:h * 4 + 4], lhsT=Vs[i][0][0:nk, h * 128:(h + 1) * 128], rhs=Ps[i][0][0:nk, h * 4:h * 4 + 4],
                                start=(i == 0), stop=(i == 8)), reads=[Vs[i][1], Ps[i][1]], writes=[BoT])
                    for i in range(9):
                        nk = 128 if i < 8 else 4
                        P.op("pe", lambda e, zb=zb, i=i, nk=nk: e.matmul(zb[:, 0:32], lhsT=onesb[0:nk, :], rhs=Ps[i][0][0:nk, :],
                                                                        start=(i == 0), stop=(i == 8)), reads=[B_cb, Ps[i][1]], writes=[Bzb])
                    P.op("dve", lambda e, zb=zb: e.reciprocal(out=rzs[:, :], in_=zb[:, 0:32]), reads=[Bzb], writes=[Brzs])
                    P.op("dve", lambda e, oT=oT: e.tensor_tensor(out=mAs[:, :], in0=oT[:, 0:32], in1=rzs[:, :], op=ALU.mult),
                         reads=[BoT, Brzs], writes=[BmAs])
                    P.dma(lambda e, r0=r0: e.dma_start(out=mixscr[1][0:8, :, r0:r0 + 4].rearrange("h p t -> p h t"),
                                                      in_=mAs.rearrange("p (h t) -> p h t", h=8)), BmAs, reads=[BmAs], writes=[B_mix[1]])
                P.barrier()

            _sample_part()

        def phase_B2(s):
            AR.reset(PERSIST_TOP)
            pjg = [AR.f32("pjg%d" % i, 3088) for i in range(2)]
            lbuf, Bl = AR.f32("lbuf", 512)
            Ebuf = [AR.f32("Eb%d" % i, 512) for i in range(2)]
            qd, Bqd = AR.b16("qd", 512)
            ki, Bki = AR.b16("ki", 512)
            ke, Bke = AR.b16("ke", 512)
            vb16, Bvb = AR.b16("vb16", 1024)
            sr, Bsr = AR.f32("sr", 1024)
            ob, Bob = AR.f32("ob", 1024)
            junk, B_junk = AR.f32("junkg", 1024)
            mixb, Bmixb = AR.b16("mixb", 1024)
            qdT, BqdT = AR.b16("qdT", 512, ("p (h n) -> p h n", dict(h=4)))
            kiT, BkiT = AR.b16("kiT", 512, ("p (h n) -> p h n", dict(h=4)))
            attm = [AR.b16("attm%d" % i, 128) for i in range(4)]
            dec, Bdec = AR.f32("dec", 8)
            mixBT, BmixBT = AR.b16("mixBT", 8 * 512, ("p (c n) -> p c n", dict(c=8)))

            import os
            KSTEP = int(os.environ.get("KGLA_STEP", "99"))
            KTILES = int(os.environ.get("KGLA_TILES", "16"))

            def gla_tile(pj_, Bpj, nt, colslot):
                ps, Bps = psA[0]
                P.op("pe", lambda e: e.transpose(out=ps[0:16, 0:nt], in_=pj_[0:nt, 3072:3088], identity=identf[0:nt, 0:nt]),
                     reads=[Bpj, B_cf], writes=[Bps])
                P.op("dve", lambda e: e.tensor_copy(out=gaT[0:16, 0:nt], in_=ps[0:16, 0:nt]), reads=[Bps], writes=[B_gaT])
                pz, Bpz = psA[1]
                P.op("pe", lambda e: e.matmul(pz[0:nt, :], lhsT=gaT[0:17, 0:nt], rhs=wga[0:17, :], start=True, stop=True),
                     reads=[B_gaT, B_wga], writes=[Bpz])
                P.op("act", lambda e: e.activation(out=lbuf[0:nt, :], in_=pz[0:nt, :], func=AF.Exp, scale=-1.0), reads=[Bpz], writes=[Bl])
                P.op("act", lambda e: e.activation(out=lbuf[0:nt, :], in_=lbuf[0:nt, :], func=AF.Ln, bias=1.0), reads=[Bl], writes=[Bl])
                if KSTEP <= 1:
                    return
                pc, Bpc = psA[2]
                P.op("pe", lambda e: e.matmul(pc[0:nt, :], lhsT=Umat[0:nt, 0:nt], rhs=lbuf[0:nt, :], start=True, stop=True),
                     reads=[B_cf, Bl], writes=[Bpc])
                pr, Bpr = psA[3]
                P.op("pe", lambda e: e.matmul(pr[0:nt, :], lhsT=Lmat[0:nt, 0:nt], rhs=lbuf[0:nt, :], start=True, stop=True),
                     reads=[B_cf, Bl], writes=[Bpr])
                ptt, Bptt = psA[0]
                for h in range(4):
                    P.op("pe", lambda e, h=h: e.matmul(ptt[:, h:h + 1], lhsT=lbuf[0:nt, h * 128:(h + 1) * 128], rhs=onescol[0:nt, :],
                                                       start=True, stop=True), reads=[Bl, B_cf], writes=[Bptt])
                if KSTEP <= 2:
                    return
                E0, BE0 = Ebuf[0]
                E1, BE1 = Ebuf[1]
                P.op("act", lambda e: e.activation(out=E0[0:nt, :], in_=pc[0:nt, :], func=AF.Exp, scale=-1.0 / 16), reads=[Bpc], writes=[BE0])
                P.op("dve", lambda e: e.scalar_tensor_tensor(out=qd[0:nt, :], in0=pj_[0:nt, 0:512], scalar=128.0 ** -0.5, in1=E0[0:nt, :],
                                                             op0=ALU.mult, op1=ALU.mult), reads=[Bpj, BE0], writes=[Bqd])
                P.op("act", lambda e: e.activation(out=E1[0:nt, :], in_=pc[0:nt, :], func=AF.Exp, scale=1.0 / 16), reads=[Bpc], writes=[BE1])
                P.op("dve", lambda e: e.tensor_tensor(out=ki[0:nt, :], in0=pj_[0:nt, 512:1024], in1=E1[0:nt, :], op=ALU.mult),
                     reads=[Bpj, BE1], writes=[Bki])
                P.op("act", lambda e: e.activation(out=E0[0:nt, :], in_=pr[0:nt, :], func=AF.Exp, scale=-1.0 / 16), reads=[Bpr], writes=[BE0])
                P.op("dve", lambda e: e.tensor_tensor(out=ke[0:nt, :], in0=pj_[0:nt, 512:1024], in1=E0[0:nt, :], op=ALU.mult),
                     reads=[Bpj, BE0], writes=[Bke])
                P.op("act", lambda e: e.activation(out=dec[:, 0:4], in_=ptt[:, 0:4],
                                                   func=AF.Exp, scale=-1.0 / 16), reads=[Bptt], writes=[Bdec])
                if KSTEP <= 3:
                    return
                P.op("dve", lambda e: e.tensor_copy(out=vb16[0:nt, :], in_=pj_[0:nt, 1024:2048]), reads=[Bpj], writes=[Bvb])
                P.op("act", lambda e: e.activation(out=sr[0:nt, :], in_=pj_[0:nt, 2048:3072], func=AF.Silu), reads=[Bpj], writes=[Bsr])
                if KSTEP <= 4:
                    return
                pt1, Bp1 = rotT()
                for h in range(4):
                    P.op("pe", lambda e, h=h: e.transpose(out=pt1[:, h, 0:nt], in_=qd[0:nt, h * 128:(h + 1) * 128], identity=identb[0:nt, 0:nt]),
                         reads=[Bqd, B_cb], writes=[Bp1])
                for h in range(4):
                    P.op("pe", lambda e, h=h: e.transpose(out=pt1[:, 4 + h, 0:nt], in_=ki[0:nt, h * 128:(h + 1) * 128], identity=identb[0:nt, 0:nt]),
                         reads=[Bki, B_cb], writes=[Bp1])
                P.op("act", lambda e: e.activation(out=qdT[:, :, 0:nt], in_=pt1[:, 0:4, 0:nt], func=AF.Copy), reads=[Bp1], writes=[BqdT])
                P.op("dve", lambda e: e.tensor_copy(out=kiT[:, :, 0:nt], in_=pt1[:, 4:8, 0:nt]), reads=[Bp1], writes=[BkiT])
                if KSTEP <= 5:
                    return
                po = [psA[4], psA[5]]
                for h in range(4):
                    pa, Bpa = psA[h % 2]
                    am, Bam = attm[h]
                    P.op("pe", lambda e, h=h, pa=pa: e.matmul(pa[0:nt, 0:nt], lhsT=kiT[:, h, 0:nt], rhs=qdT[:, h, 0:nt], start=True, stop=True),
                         reads=[BkiT, BqdT], writes=[Bpa])
                    P.op("dve", lambda e, pa=pa, am=am: e.tensor_tensor(out=am[0:nt, 0:nt], in0=pa[0:nt, 0:nt], in1=Umat[0:nt, 0:nt], op=ALU.mult),
                         reads=[Bpa, B_cf], writes=[Bam])
                    o_, Bo = po[h // 2]
                    c0 = (h % 2) * 256
                    P.op("pe", lambda e, h=h, o_=o_, c0=c0, am=am: e.matmul(o_[0:nt, c0:c0 + 256], lhsT=am[0:nt, 0:nt],
                                                                           rhs=vb16[0:nt, h * 256:(h + 1) * 256], start=True, stop=False),
                         reads=[Bam, Bvb], writes=[Bo])
                    P.op("pe", lambda e, h=h, o_=o_, c0=c0: e.matmul(o_[0:nt, c0:c0 + 256], lhsT=qdT[:, h, 0:nt], rhs=Sb[:, h, :],
                                                                    start=False, stop=True), reads=[BqdT, B_Sb], writes=[Bo])
                    pd, Bpd = psA[2 + h % 2]
                    P.op("pe", lambda e, h=h, pd=pd: e.matmul(pd[:, 0:256], lhsT=ke[0:nt, h * 128:(h + 1) * 128],
                                                              rhs=vb16[0:nt, h * 256:(h + 1) * 256], start=True, stop=True),
                         reads=[Bke, Bvb], writes=[Bpd])
                    P.op("dve", lambda e, h=h, pd=pd: e.scalar_tensor_tensor(out=Sst[:, h, :], in0=Sst[:, h, :], scalar=dec[:, h:h + 1],
                                                                            in1=pd[:, 0:256], op0=ALU.mult, op1=ALU.add),
                         reads=[B_S, Bdec, Bpd], writes=[B_S])
                    P.op("act", lambda e, h=h: e.activation(out=Sb[:, h, :], in_=Sst[:, h, :], func=AF.Copy), reads=[B_S], writes=[B_Sb])
                if KSTEP <= 6:
                    return
                for j in range(2):
                    o_, Bo = po[j]
                    P.op("act", lambda e, j=j, o_=o_: e.activation(out=ob[0:nt, j * 512:(j + 1) * 512], in_=o_[0:nt, :], func=AF.Copy),
                         reads=[Bo], writes=[Bob])
                P.op("dve", lambda e: e.tensor_tensor(out=junk[0:nt, :], in0=ob[0:nt, :], in1=ob[0:nt, :], op=ALU.mult), reads=[Bob], writes=[B_junk])
                ssq, Bq = newstat(4)
                P.op("dve", lambda e: e.tensor_reduce(out=ssq[0:nt, :], in_=junk[0:nt, :].rearrange("p (h d) -> p h d", h=4), axis=AX.X, op=ALU.add),
                     reads=[B_junk], writes=[Bq])
                P.op("act", lambda e: e.activation(out=ssq[0:nt, :], in_=ssq[0:nt, :], func=AF.Sqrt, scale=1.0 / 256, bias=EPS), reads=[Bq], writes=[Bq])
                P.op("dve", lambda e: e.reciprocal(out=ssq[0:nt, :], in_=ssq[0:nt, :]), reads=[Bq], writes=[Bq])
                for h in range(4):
                    P.op("dve", lambda e, h=h: e.scalar_tensor_tensor(out=junk[0:nt, h * 256:(h + 1) * 256], in0=ob[0:nt, h * 256:(h + 1) * 256],
                                                                      scalar=ssq[0:nt, h:h + 1], in1=gg_bc[0:nt, h * 256:(h + 1) * 256],
                                                                      op0=ALU.mult, op1=ALU.mult), reads=[Bob, Bq, B_gg, B_junk], writes=[B_junk])
                P.op("dve", lambda e: e.tensor_tensor(out=mixb[0:nt, :], in0=junk[0:nt, :], in1=sr[0:nt, :], op=ALU.mult),
                     reads=[B_junk, Bsr], writes=[Bmixb])
                if KSTEP <= 7:
                    return
                pt, Bp = rotT()
                for j in range(8):
                    P.op("pe", lambda e, j=j, pt=pt: e.transpose(out=pt[:, j, 0:nt], in_=mixb[0:nt, j * 128:(j + 1) * 128], identity=identb[0:nt, 0:nt]),
                         reads=[Bmixb, B_cb], writes=[Bp])
                P.op("act", lambda e, pt=pt: e.activation(out=mixBT[:, :, colslot:colslot + nt], in_=pt[:, :, 0:nt], func=AF.Copy),
                     reads=[Bp], writes=[BmixBT])

            P.op("pool", lambda e: e.memset(Sst[:, :, :], 0.0), writes=[B_S])
            P.op("pool", lambda e: e.memset(Sb[:, :, :], 0.0), writes=[B_Sb])
            def pjgload(t):
                pj_, Bpj = pjg[t % 2]
                P.dma(lambda e: e.dma_start(out=pj_[:, :], in_=projscr[s][t * 128:t * 128 + 128, 3072:NIN]), Bpj,
                      reads=[B_proj[s][t]], writes=[Bpj])

            pjgload(0)
            for t in range(KTILES):
                pj_, Bpj = pjg[t % 2]
                row0 = t * 128
                if t + 1 < KTILES:
                    pjgload(t + 1)
                gla_tile(pj_, Bpj, 128, (t % 4) * 128)
                if t % 4 == 3:
                    b = t // 4
                    P.dma(lambda e, b=b: e.dma_start(out=mixscr[s][8:16, :, b * 512:(b + 1) * 512].rearrange("c p n -> p c n"),
                                                    in_=mixBT[:, :, :]), BmixBT, reads=[BmixBT], writes=[B_mix[s]])
            P.dma(lambda e: e.dma_start(out=pst[s].rearrange("h p v -> p h v"), in_=Sst[:, :, :]), B_S, reads=[B_S], writes=[B_out])
            if s == 1:
                for bi in range(4):
                    pj_, Bpj = pjg[bi % 2]
                    r0 = 2048 + 4 * bi
                    P.dma(lambda e, bi=bi: e.dma_start(out=Sst[:, :, :], in_=sg[bi].rearrange("h p v -> p h v")), B_S, reads=[B_in], writes=[B_S])
                    P.op("pool", lambda e: e.tensor_copy(out=Sb[:, :, :], in_=Sst[:, :, :]), reads=[B_S], writes=[B_Sb])
                    P.dma(lambda e, pj_=pj_, r0=r0: e.dma_start(out=pj_[0:4, :], in_=projscr[1][r0:r0 + 4, 3072:NIN]), Bpj,
                          reads=[B_proj[1][16]], writes=[Bpj])
                    gla_tile(pj_, Bpj, 4, 4 * bi)
                    P.dma(lambda e, bi=bi: e.dma_start(out=sst[bi].rearrange("h p v -> p h v"), in_=Sst[:, :, :]), B_S, reads=[B_S], writes=[B_out])
                P.dma(lambda e: e.dma_start(out=mixscr[1][8:16, :, 2048:2064].rearrange("c p n -> p c n"), in_=mixBT[:, :, 0:16]), BmixBT,
                      reads=[BmixBT], writes=[B_mix[1]])
            P.barrier()

        def phase_D(s):
            AR.reset(PERSIST_TOP)
            wst = [AR.f32("wst%d" % i, 16 * 256, ("p (k n) -> p k n", dict(k=16)), top=True) for i in range(2)]
            wbf = [AR.b16("wbf%d" % i, 16 * 256, ("p (k n) -> p k n", dict(k=16)), top=True) for i in range(2)]
            ust = [AR.b16("ust%d" % i, 2 * NTOK, ("p (c n) -> p c n", dict(c=2)), top=True) for i in range(2)]
            rt = [AR.f32("rt%d" % i, 512, top=True) for i in range(2)]
            Wpk = w_up.rearrange("(k p) n -> p k n", p=128)
            ntok = 2048 + (16 if s == 1 else 0)
            blocks = [(i * 512, 512) for i in range(4)] if s == 0 else [(0, 416), (416, 416), (832, 416), (1248, 416), (1664, 400)]
            ngr = DFF // 256

            def wload(gi):
                ws, Bws = wst[gi % 2]
                wb, Bwb = wbf[gi % 2]
                P.dma(lambda e: e.dma_start(out=ws[:, :, :], in_=Wpk[:, :, gi * 256:(gi + 1) * 256]), Bws, reads=[B_in], writes=[Bws])
                P.op("dve", lambda e: e.tensor_copy(out=wb[:, 0:8, :], in_=ws[:, 0:8, :]), reads=[Bws], writes=[Bwb])
                P.op("act", lambda e: e.activation(out=wb[:, 8:16, :], in_=ws[:, 8:16, :], func=AF.Copy), reads=[Bws], writes=[Bwb])

            wload(0)
            cnt = 0
            for gi in range(ngr):
                if gi + 1 < ngr:
                    wload(gi + 1)
                wb, Bwb = wbf[gi % 2]
                us, Bus = ust[gi % 2]
                for cc in range(2):
                    for (tb0, nb) in blocks:
                        ps, Bps = rotA()
                        for k in range(16):
                            P.op("pe", lambda e, ps=ps, k=k, cc=cc, tb0=tb0, nb=nb, wb=wb: e.matmul(
                                ps[:, 0:nb], lhsT=wb[:, k, cc * 128:(cc + 1) * 128], rhs=BIG[:, k, tb0:tb0 + nb],
                                start=(k == 0), stop=(k == 15)), reads=[B_big, Bwb], writes=[Bps])
                        r_, Br = rt[cnt % 2]
                        cnt += 1
                        P.op("act", lambda e, ps=ps, r_=r_, nb=nb: e.activation(out=r_[:, 0:nb], in_=ps[:, 0:nb], func=AF.Relu),
                             reads=[Bps], writes=[Br])
                        P.op("dve", lambda e, r_=r_, us=us, cc=cc, tb0=tb0, nb=nb: e.tensor_tensor(
                            out=us[:, cc, tb0:tb0 + nb], in0=r_[:, 0:nb], in1=r_[:, 0:nb], op=ALU.mult), reads=[Br], writes=[Bus])
                for cc in range(2):
                    P.dma(lambda e, us=us, gi=gi, cc=cc: e.dma_start(
                        out=utscr[s][0:16, :, 2 * gi + cc, :].rearrange("t p n -> p t n"),
                        in_=us[:, cc, 0:2048].rearrange("p (t n) -> p t n", t=16)), Bus, reads=[Bus], writes=B_ut[s][0:16])
                if s == 1:
                    P.dma(lambda e, us=us, gi=gi: e.dma_start(out=utscr[s][16, :, 2 * gi:2 * gi + 2, 0:16], in_=us[:, :, 2048:2064]), Bus,
                          reads=[Bus], writes=[B_ut[s][16]])
            P.barrier()

        def phase_E(s):
            AR.reset(PERSIST_TOP)
            def wres(c):
                return BIG[:, c // 4, (c % 4) * 512:(c % 4) * 512 + 512]
            wst = [AR.f32("wst%d" % i, 8 * 512, ("p (k n) -> p k n", dict(k=8)), top=True) for i in range(2)]
            ut = [AR.b16("ut%d" % i, 64 * 128, ("p (c n) -> p c n", dict(c=64))) for i in range(2)]
            x1s = [AR.f32("x1s%d" % i, 512) for i in range(2)]
            yst = [AR.f32("yst%d" % i, 512) for i in range(2)]
            Wpc = w_down.rearrange("(c p) n -> p c n", p=128)
            tl = tiles_of(s)
            steps = [(g, t, row0, nt) for g in range(4) for (t, row0, nt) in tl]
            wcnt = [0]
            B_w = [bigview("wres%d" % i) for i in range(8)]

            def wgroup(g):
                c0 = g * 512
                for i in range(8):
                    ws, Bws = wst[wcnt[0] % 2]
                    wcnt[0] += 1
                    P.dma(lambda e, ws=ws, i=i, c0=c0: e.dma_start(out=ws[:, :, :], in_=Wpc[:, 8 * i:8 * i + 8, c0:c0 + 512]), Bws,
                          reads=[B_in], writes=[Bws])
                    if i % 2 == 0:
                        P.op("dve", lambda e, ws=ws, i=i: e.tensor_copy(out=BIG[:, 2 * i:2 * i + 2, 0:2048].rearrange("p k (a n) -> p k a n", n=512),
                                                                        in_=ws[:, :, :].rearrange("p (k a) n -> p k a n", k=2)),
                             reads=[Bws], writes=[B_w[i]])
                    else:
                        P.op("act", lambda e, ws=ws, i=i: e.activation(out=BIG[:, 2 * i:2 * i + 2, 0:2048].rearrange("p k (a n) -> p k a n", n=512),
                                                                       in_=ws[:, :, :].rearrange("p (k a) n -> p k a n", k=2), func=AF.Copy),
                             reads=[Bws], writes=[B_w[i]])

            def pre(j):
                g, t, row0, nt = steps[j]
                c0 = g * 512
                u_, Bu = ut[j % 2]
                x_, Bx = x1s[j % 2]
                P.dma(lambda e: e.dma_start(out=u_[:, :, 0:nt], in_=utscr[s][t, :, :, 0:nt]), Bu, reads=[B_ut[s][t]], writes=[Bu])
                P.dma(lambda e: e.dma_start(out=x_[0:nt, :], in_=x1scr[s][row0:row0 + nt, c0:c0 + 512]), Bx, reads=[B_x1[s][t]], writes=[Bx])

            pre(0)
            for j, (g, t, row0, nt) in enumerate(steps):
                c0 = g * 512
                if t == 0:
                    wgroup(g)
                if j + 1 < len(steps):
                    pre(j + 1)
                u_, Bu = ut[j % 2]
                x_, Bx = x1s[j % 2]
                y_, By = yst[j % 2]
                ps, Bps = rotA()
                for c in range(64):
                    P.op("pe", lambda e, ps=ps, c=c, u_=u_, nt=nt: e.matmul(ps[0:nt, :], lhsT=u_[:, c, 0:nt], rhs=wres(c),
                                                                            start=(c == 0), stop=(c == 63)),
                         reads=[Bu, B_w[c // 8]], writes=[Bps])
                P.op("dve", lambda e, ps=ps, x_=x_, y_=y_, nt=nt: e.tensor_tensor(out=y_[0:nt, :], in0=ps[0:nt, :], in1=x_[0:nt, :], op=ALU.add),
                     reads=[Bps, Bx], writes=[By])
                dst = yp[s, row0:row0 + nt, c0:c0 + 512] if t < 16 else ys[0:nt, c0:c0 + 512]
                P.dma(lambda e, y_=y_, dst=dst, nt=nt: e.dma_start(out=dst, in_=y_[0:nt, :]), By, reads=[By], writes=[B_out])
            P.barrier()

        P.barrier(force=True)
        for s in range(2):
            phase_norm(s, lambda t, nt, s=s: xrows(s, t, nt), [B_in] * 17, ag)
            if stop == "A0":
                break
            if not os.environ.get("KSKIP"):
                phase_gemm_tok(s, w_in, NIN, evac_A1(s))
            if stop == "A1":
                break
            if not os.environ.get("KSKIP"):
                phase_B1(s)
            if stop == "B1":
                break
            phase_B2(s)
            if stop == "B2":
                break
            AR.reset(PERSIST_TOP)
            ntok = 2048 + (16 if s == 1 else 0)
            for half in range(2):
                P.dma(lambda e, half=half, s=s, ntok=ntok: e.dma_start(
                    out=BIG[:, half * 8:half * 8 + 8, 0:ntok],
                    in_=mixscr[s][half * 8:half * 8 + 8, :, 0:ntok].rearrange("c p n -> p c n")), B_big,
                    reads=[B_mix[s]], writes=[B_big])
            phase_gemm_tok(s, w_out, D, evac_C(s))
            if stop == "C":
                break
            phase_norm(s, lambda t, nt, s=s: x1scr[s][t * 128:t * 128 + nt, :], B_x1[s], mg)
            phase_D(s)
            if stop == "D":
                break
            phase_E(s)
            if stop == "E":
                break
        P.emit(nc, es)
    return nc


def _consts():
    cf = np.zeros((128, 392), np.float32)
    j = np.arange(128)[:, None]
    i = np.arange(128)[None, :]
    cf[:, 0:128] = (j == i)
    cf[:, 128:256] = (j <= i)
    cf[:, 256:384] = (j > i)
    cf[:, 384] = 1.0
    cb = np.zeros((128, 256 + STRIPW + 288), np.float32)
    cb[:, 0:128] = (j == i)
    cb[:, 128:256] = 1.0
    c = np.arange(STRIPW)[None, :] - STRIP0
    dl = c - j
    m = ((dl <= 128).astype(np.float32) + ((dl % 4 == 0) & (dl <= 512)) + ((dl % 16 == 0) & (dl <= 2048))) * (dl >= 0)
    cb[:, 256:256 + STRIPW] = m
    sm = np.zeros((128, 9, 8, 4), np.float32)
    p = np.arange(128)
    for t in range(4):
        for ti in range(4):
            dl = 2048 + t - (1536 + 128 * ti + p)
            sm[:, ti, :, t] = ((dl <= 128).astype(np.float32) + ((dl % 4 == 0) & (dl <= 512)))[:, None]
        for r in range(4):
            sm[:, 4 + r, :, t] = 1.0 if t == r else 0.0
        for tp in range(4):
            sm[tp, 8, :, t] = 0.0 if tp > t else (3.0 if tp == t else 1.0)
    cb[:, 256 + STRIPW:] = sm.reshape(128, 288)
    return cf, cb.astype(ml_dtypes.bfloat16)


_NC_CACHE = {}


def kernel(x_prompt, x_sample, cache_win_k, cache_win_v, state_gla, attn_norm_g, w_in, q_norm_g, k_norm_g,
           w_gate2, b_gate, gla_norm_g, w_out, mlp_norm_g, w_up, w_down, _cores=None, _stop=None):
    f = lambda a: np.ascontiguousarray(np.asarray(a, dtype=np.float32))
    x_prompt, x_sample = f(x_prompt), f(x_sample)
    cache_win_k, cache_win_v, state_gla = f(cache_win_k), f(cache_win_v), f(state_gla)
    cf, cb = _consts()
    shared = {
        "w_in": f(w_in)[0], "w_out": f(w_out)[0], "w_up": f(w_up)[0], "w_down": f(w_down)[0],
        "ag": f(attn_norm_g).reshape(1, D), "mg": f(mlp_norm_g).reshape(1, D),
        "qg": f(q_norm_g).reshape(1, 128), "kg": f(k_norm_g).reshape(1, 128),
        "gg": f(gla_norm_g).reshape(1, 1024), "wg2": f(w_gate2)[0], "bg": f(b_gate).reshape(1, 512),
        "constf": cf, "constb": cb,
    }
    cores = list(range(NCORES)) if _cores is None else list(_cores)
    in_maps = []
    for c in cores:
        m = dict(shared)
        m["xp"] = x_prompt[2 * c:2 * c + 2]
        m["xs"] = x_sample[4 * c:4 * c + 4].reshape(16, D)
        m["ck"] = cache_win_k[0, 4 * c:4 * c + 4].reshape(4, SEQ, 1024)
        m["cv"] = cache_win_v[0, 4 * c:4 * c + 4].reshape(4, SEQ, 1024)
        m["sg"] = state_gla[0, 4 * c:4 * c + 4]
        in_maps.append(m)
    if _stop not in _NC_CACHE:
        _NC_CACHE[_stop] = build(_stop)
    res = run_bass_kernel_spmd(_NC_CACHE[_stop], in_maps, core_ids=list(range(len(cores))))
    R = res.results
    if _stop:
        return R
    n = len(cores)
    y_p = np.concatenate([R[i]["yp"] for i in range(n)], 0)
    y_s = np.concatenate([R[i]["ys"].reshape(4, 4, D) for i in range(n)], 0)
    p_k = np.concatenate([R[i]["pk"].reshape(2, SEQ, 8, 128) for i in range(n)], 0)[None]
    p_v = np.concatenate([R[i]["pv"].reshape(2, SEQ, 8, 128) for i in range(n)], 0)[None]
    p_s = np.concatenate([R[i]["pst"] for i in range(n)], 0)[None]
    s_k = np.concatenate([R[i]["sk"].reshape(4, 4, 8, 128) for i in range(n)], 0)[None]
    s_v = np.concatenate([R[i]["sv"].reshape(4, 4, 8, 128) for i in range(n)], 0)[None]
    s_s = np.concatenate([R[i]["sst"] for i in range(n)], 0)[None]
    return (y_p, y_s, p_k, p_v, p_s, s_k, s_v, s_s)
```
